# Optimizing a Trainium2 kernel written in Bass

```python
import math
import jax, jax.numpy as jnp
from jax import lax
import numpy as np

D_MODEL = 1024
BATCH = 32
SEQ = 2048
DEPTH = 2

GRID_W = 64
CTX_LEN = 256
D_MIX = D_MODEL
D_RWKV = D_MIX // 4
RWKV_HEAD_DIM = 64
RWKV_HEADS = D_RWKV // RWKV_HEAD_DIM
DECAY_LORA = 64
ICL_LORA = 64
D_CONV = D_MIX // 4
CONV_WIDTH = 3
D_ATTN = D_MIX // 2
DIFF_HEAD_DIM = 64
DIFF_V_DIM = 2 * DIFF_HEAD_DIM
DIFF_HEADS = D_ATTN // DIFF_V_DIM
Q_BLOCK = 128
ROPE_THETA = 10000.0
ROPE_AXIS_DIM = DIFF_HEAD_DIM // 2
NORM_EPS = 1e-6
RWKV_GN_EPS = 64e-5
IN_SPLITS = (D_RWKV, D_RWKV, D_RWKV, DECAY_LORA, DECAY_LORA, ICL_LORA, ICL_LORA, D_RWKV,
             D_CONV, D_CONV, D_CONV, D_CONV,
             D_ATTN, D_ATTN, D_ATTN, D_ATTN)
D_IN = 4 * D_RWKV + 2 * DECAY_LORA + 2 * ICL_LORA + 4 * D_CONV + 4 * D_ATTN

kernel_name = 'hybrid_rwkv7_shortconv_diffattn_prefix_dit'


def rms_norm(x, g):
    xf = x.astype(jnp.float32)
    y = xf * lax.rsqrt(jnp.mean(xf * xf, axis=-1, keepdims=True) + NORM_EPS)
    return (y * g.astype(jnp.float32)).astype(x.dtype)


def modulation(cond, mod_w, mod_b):
    m = jax.nn.silu(cond) @ mod_w + mod_b
    return jnp.split(m, 3, axis=-1)


def split_projection(u):
    idx = np.cumsum(IN_SPLITS)[:-1].tolist()
    return jnp.split(u, idx, axis=-1)


def axial_rope_tables(seq_len):
    rows = seq_len // GRID_W
    row = jnp.repeat(jnp.arange(rows, dtype=jnp.float32), GRID_W)
    col = jnp.tile(jnp.arange(GRID_W, dtype=jnp.float32), rows)
    inv_freq = ROPE_THETA ** (-jnp.arange(0, ROPE_AXIS_DIM, 2, dtype=jnp.float32) / ROPE_AXIS_DIM)
    ang_r = row[:, None] * inv_freq
    ang_c = col[:, None] * inv_freq
    return (jnp.cos(ang_r), jnp.sin(ang_r), jnp.cos(ang_c), jnp.sin(ang_c))


def rotate(x, cos, sin):
    half = x.shape[-1] // 2
    x1, x2 = x[..., :half], x[..., half:]
    return jnp.concatenate([x1 * cos - x2 * sin, x2 * cos + x1 * sin], axis=-1)


def apply_axial_rope(x, tables):
    cos_r, sin_r, cos_c, sin_c = (t[None, :, None, None, :] for t in tables)
    xf = x.astype(jnp.float32)
    out = jnp.concatenate([rotate(xf[..., :ROPE_AXIS_DIM], cos_r, sin_r),
                           rotate(xf[..., ROPE_AXIS_DIM:], cos_c, sin_c)], axis=-1)
    return out.astype(x.dtype)


def to_heads(t):
    return t.reshape(t.shape[0], t.shape[1], RWKV_HEADS, RWKV_HEAD_DIM)


def rwkv_direction_inputs(parts, d, w0, w_up, a0, a_up, k_k, k_a):
    r, k, v, lw_f, lw_b, la_f, la_b, _ = parts
    lw = lw_f if d == 0 else lw_b
    la = la_f if d == 0 else la_b
    w = -jax.nn.softplus(-(w0[d] + jnp.tanh(lw) @ w_up[d])) - 0.5
    decay = jnp.exp(-jnp.exp(w))
    a = jax.nn.sigmoid(a0[d] + la @ a_up[d])
    kk = to_heads(k * k_k)
    kk = kk / jnp.maximum(jnp.linalg.norm(kk, axis=-1, keepdims=True), 1e-12)
    k_mod = k * (1.0 + (a - 1.0) * k_a)
    return (to_heads(r), to_heads(decay), to_heads(k_mod), to_heads(v), -kk, kk * to_heads(a))


def rwkv_scan(state0, terms, reverse, with_output):
    seq_terms = terms if with_output else terms[1:]
    xs = tuple(jnp.moveaxis(t, 1, 0) for t in seq_terms)

    def step(S, inp):
        w_t, k_t, v_t, a_t, b_t = inp[-5:]
        sa = jnp.einsum('bhvk,bhk->bhv', S, a_t)
        S = S * w_t[:, :, None, :] + sa[..., None] * b_t[:, :, None, :] + v_t[..., None] * k_t[:, :, None, :]
        y_t = jnp.einsum('bhvk,bhk->bhv', S, inp[0]) if with_output else None
        return S, y_t

    S, ys = lax.scan(step, state0, xs, reverse=reverse)
    return S, (jnp.moveaxis(ys, 0, 1) if with_output else None)


def rwkv_readout(ys, ks, parts, r_k, ln_g, ln_b):
    r, v, z = to_heads(parts[0]), to_heads(parts[2]), parts[7]
    y = ys[0] + ys[1]
    mu = jnp.mean(y, axis=-1, keepdims=True)
    var = jnp.mean(jnp.square(y - mu), axis=-1, keepdims=True)
    y = (y - mu) * lax.rsqrt(var + RWKV_GN_EPS)
    B, T = y.shape[0], y.shape[1]
    y = y.reshape(B, T, D_RWKV) * ln_g + ln_b
    bonus = jnp.sum(r * (ks[0] + ks[1]) * r_k, axis=-1, keepdims=True) * v
    y = y + bonus.reshape(B, T, D_RWKV)
    return y * jax.nn.silu(z)


def rwkv_mixer(lat, ctx, w0, w_up, a0, a_up, k_k, k_a, r_k, ln_g, ln_b, need_ctx_out):
    f32 = jnp.float32
    lat = [t.astype(f32) for t in lat]
    ctx = [t.astype(f32) for t in ctx]
    w0, w_up, a0, a_up, k_k, k_a, r_k, ln_g, ln_b = (
        p.astype(f32) for p in (w0, w_up, a0, a_up, k_k, k_a, r_k, ln_g, ln_b))
    B = lat[0].shape[0]
    state0 = jnp.zeros((B, RWKV_HEADS, RWKV_HEAD_DIM, RWKV_HEAD_DIM), f32)
    ys_l, ks_l, ys_c, ks_c = [], [], [], []
    for d, reverse in ((0, False), (1, True)):
        tc = rwkv_direction_inputs(ctx, d, w0, w_up, a0, a_up, k_k, k_a)
        tl = rwkv_direction_inputs(lat, d, w0, w_up, a0, a_up, k_k, k_a)
        s_ctx, y_c = rwkv_scan(state0, tc, reverse, need_ctx_out)
        _, y_l = rwkv_scan(s_ctx, tl, reverse, True)
        ys_l.append(y_l)
        ks_l.append(tl[2])
        ys_c.append(y_c)
        ks_c.append(tc[2])
    out_l = rwkv_readout(ys_l, ks_l, lat, r_k, ln_g, ln_b)
    out_c = rwkv_readout(ys_c, ks_c, ctx, r_k, ln_g, ln_b) if need_ctx_out else None
    return out_l, out_c


def short_conv_mixer(b_gate, c_gate, h, z, conv_w):
    u = c_gate * h
    T = u.shape[1]
    up = jnp.pad(u, ((0, 0), (CONV_WIDTH // 2, CONV_WIDTH // 2), (0, 0)))
    y = up[:, 0:T] * conv_w[0]
    for j in range(1, CONV_WIDTH):
        y = y + up[:, j:j + T] * conv_w[j]
    return b_gate * y * jax.nn.silu(z)


def diff_attend(q, k, v, lam, scale):
    s = jnp.einsum('bqhjd,bkhjd->bhjqk', q, k).astype(jnp.float32) * scale
    p = jax.nn.softmax(s, axis=-1)
    attn = p[:, :, 0] - lam * p[:, :, 1]
    return jnp.einsum('bhqk,bkhe->bqhe', attn.astype(v.dtype), v)


def diff_attention_mixer(lat, ctx, diff_lambda, subln_g, rope, layer_idx, need_ctx_out):
    q_l, k_l, v_l, z_l = lat
    q_c, k_c, v_c, z_c = ctx
    B, T = q_l.shape[0], q_l.shape[1]

    def qk_heads(t):
        return t.reshape(t.shape[0], t.shape[1], DIFF_HEADS, 2, DIFF_HEAD_DIM)

    def v_heads(t):
        return t.reshape(t.shape[0], t.shape[1], DIFF_HEADS, DIFF_V_DIM)

    q_l = apply_axial_rope(qk_heads(q_l), rope)
    k_l = apply_axial_rope(qk_heads(k_l), rope)
    v_l = v_heads(v_l)
    q_c, k_c, v_c = qk_heads(q_c), qk_heads(k_c), v_heads(v_c)

    lam_init = 0.8 - 0.6 * math.exp(-0.3 * layer_idx)
    lf = diff_lambda.astype(jnp.float32)
    lam = jnp.exp(jnp.sum(lf[0] * lf[1])) - jnp.exp(jnp.sum(lf[2] * lf[3])) + lam_init
    scale = DIFF_HEAD_DIM ** -0.5

    def finish(o, z):
        o = rms_norm(o, subln_g) * (1.0 - lam_init)
        return o.reshape(o.shape[0], o.shape[1], D_ATTN) * jax.nn.silu(z)

    k_all = jnp.concatenate([k_l, k_c], axis=1)
    v_all = jnp.concatenate([v_l, v_c], axis=1)
    nb = T // Q_BLOCK
    qb = jnp.moveaxis(q_l.reshape(B, nb, Q_BLOCK, DIFF_HEADS, 2, DIFF_HEAD_DIM), 1, 0)
    o = lax.map(lambda qblk: diff_attend(qblk, k_all, v_all, lam, scale), qb)
    o = jnp.moveaxis(o, 0, 1).reshape(B, T, DIFF_HEADS, DIFF_V_DIM)
    out_l = finish(o, z_l)
    out_c = finish(diff_attend(q_c, k_c, v_c, lam, scale), z_c) if need_ctx_out else None
    return out_l, out_c


def hybrid_layer(x, xc, c, c_ctx, mod_w, mod_b, pre_g, post_g, w_in, w_out,
                 rwkv_w0, rwkv_w_up, rwkv_a0, rwkv_a_up, rwkv_k_k, rwkv_k_a, rwkv_r_k,
                 rwkv_ln_g, rwkv_ln_b, conv_w, diff_lambda, diff_subln_g,
                 rope, layer_idx, need_ctx_out):
    shift, scale, gate = modulation(c, mod_w, mod_b)
    shift_c, scale_c, gate_c = modulation(c_ctx, mod_w, mod_b)
    h = rms_norm(x, pre_g) * (1.0 + scale[:, None, :]) + shift[:, None, :]
    hc = rms_norm(xc, pre_g) * (1.0 + scale_c) + shift_c
    p_l = split_projection(h @ w_in)
    p_c = split_projection(hc @ w_in)

    y_rwkv_l, y_rwkv_c = rwkv_mixer(p_l[0:8], p_c[0:8], rwkv_w0, rwkv_w_up, rwkv_a0, rwkv_a_up,
                                    rwkv_k_k, rwkv_k_a, rwkv_r_k, rwkv_ln_g, rwkv_ln_b, need_ctx_out)
    y_conv_l = short_conv_mixer(*p_l[8:12], conv_w)
    y_attn_l, y_attn_c = diff_attention_mixer(p_l[12:16], p_c[12:16], diff_lambda, diff_subln_g,
                                              rope, layer_idx, need_ctx_out)

    y_l = jnp.concatenate([y_rwkv_l.astype(x.dtype), y_conv_l, y_attn_l], axis=-1) @ w_out
    x = x + gate[:, None, :] * rms_norm(y_l, post_g)
    if need_ctx_out:
        y_conv_c = short_conv_mixer(*p_c[8:12], conv_w)
        y_c = jnp.concatenate([y_rwkv_c.astype(xc.dtype), y_conv_c, y_attn_c], axis=-1) @ w_out
        xc = xc + gate_c * rms_norm(y_c, post_g)
    return x, xc


def setup_inputs(seed: int = 0) -> dict:
    key = jax.random.key(seed)
    ks = jax.random.split(key, 24)
    f32 = jnp.float32
    L = DEPTH

    def nrm(k, shape, s):
        return jax.random.normal(k, shape, f32) * s

    return {
        'x': nrm(ks[0], (BATCH, SEQ, D_MODEL), 1.0),
        'c': nrm(ks[1], (BATCH, D_MODEL), 1.0),
        'ctx': nrm(ks[2], (BATCH, CTX_LEN, D_MODEL), 1.0),
        'c_ctx': nrm(ks[3], (D_MODEL,), 1.0),
        'mod_w': nrm(ks[4], (L, D_MODEL, 3 * D_MODEL), 0.5 * D_MODEL ** -0.5),
        'mod_b': nrm(ks[5], (L, 3 * D_MODEL), 0.01),
        'norm_pre_g': 1.0 + nrm(ks[6], (L, D_MODEL), 0.05),
        'norm_post_g': 1.0 + nrm(ks[7], (L, D_MODEL), 0.05),
        'w_in': nrm(ks[8], (L, D_MODEL, D_IN), D_MODEL ** -0.5),
        'w_out': nrm(ks[9], (L, D_MIX, D_MODEL), D_MIX ** -0.5),
        'rwkv_w0': jax.random.uniform(ks[10], (L, 2, D_RWKV), f32, -6.0, -1.0),
        'rwkv_w_up': nrm(ks[11], (L, 2, DECAY_LORA, D_RWKV), 0.05),
        'rwkv_a0': nrm(ks[12], (L, 2, D_RWKV), 0.5),
        'rwkv_a_up': nrm(ks[13], (L, 2, ICL_LORA, D_RWKV), 0.3 * ICL_LORA ** -0.5),
        'rwkv_k_k': 0.85 + nrm(ks[14], (L, D_RWKV), 0.05),
        'rwkv_k_a': 1.0 + nrm(ks[15], (L, D_RWKV), 0.05),
        'rwkv_r_k': nrm(ks[16], (L, RWKV_HEADS, RWKV_HEAD_DIM), 0.1),
        'rwkv_ln_g': 1.0 + nrm(ks[17], (L, D_RWKV), 0.05),
        'rwkv_ln_b': nrm(ks[18], (L, D_RWKV), 0.01),
        'conv_w': nrm(ks[19], (L, CONV_WIDTH, D_CONV), CONV_WIDTH ** -0.5),
        'diff_lambda': nrm(ks[20], (L, 4, DIFF_HEAD_DIM), 0.1),
        'diff_subln_g': 1.0 + nrm(ks[21], (L, DIFF_V_DIM), 0.05),
    }


def reference(x, c, ctx, c_ctx, mod_w, mod_b, norm_pre_g, norm_post_g, w_in, w_out,
              rwkv_w0, rwkv_w_up, rwkv_a0, rwkv_a_up, rwkv_k_k, rwkv_k_a, rwkv_r_k,
              rwkv_ln_g, rwkv_ln_b, conv_w, diff_lambda, diff_subln_g):
    rope = axial_rope_tables(x.shape[1])
    xc = ctx
    for l in range(DEPTH):
        x, xc = hybrid_layer(x, xc, c, c_ctx, mod_w[l], mod_b[l], norm_pre_g[l], norm_post_g[l],
                             w_in[l], w_out[l], rwkv_w0[l], rwkv_w_up[l], rwkv_a0[l], rwkv_a_up[l],
                             rwkv_k_k[l], rwkv_k_a[l], rwkv_r_k[l], rwkv_ln_g[l], rwkv_ln_b[l],
                             conv_w[l], diff_lambda[l], diff_subln_g[l],
                             rope, l, l < DEPTH - 1)
    return x
```

```python
import contextlib
import math
import numpy as np
import concourse.bass as bass
import concourse.mybir as mybir
from concourse.bass_utils import run_bass_kernel_spmd

F32 = mybir.dt.float32
BF16 = mybir.dt.bfloat16
AF = mybir.ActivationFunctionType
ALU = mybir.AluOpType
AX = mybir.AxisListType

EPOCH = 20000
RW_CUT = 99
RW_SUB = 99
RW_NCH = 99
RW_ND = 2
RW_VAR = 0
AT_CUT = 99
AT_NG = 99
N_DMA_SEMS = 8


class Buf:
    __slots__ = ("w", "r", "name")

    def __init__(self, name=""):
        self.w = None
        self.r = []
        self.name = name


class _Cap:
    def __init__(self):
        self.call = None

    def __getattr__(self, name):
        def f(*a, **k):
            self.call = (name, a, k)
            return self
        return f


class Rec:
    __slots__ = ("eng", "fn", "deps", "raw", "sig", "needed", "is_dma", "pos")

    def __init__(self, eng, fn, is_dma):
        self.eng = eng
        self.fn = fn
        self.deps = set()
        self.raw = set()
        self.sig = None
        self.needed = False
        self.is_dma = is_dma
        self.pos = 0


class Prog:
    ENGS = ("sp", "act", "pool", "dve", "pe")

    def __init__(self, nc):
        self.nc = nc
        self.streams = {e: [] for e in self.ENGS}
        self.stack = contextlib.ExitStack()
        self.dma_sems = {}
        self.dma_rr = {}
        self.dma_last = {}
        self.dma_cnt = {}
        self.nsem = 0
        self.all_dma = []
        self.fence = []

    def barrier(self):
        fence = []
        for e in self.ENGS:
            for rec in reversed(self.streams[e]):
                if not rec.is_dma:
                    fence.append(rec)
                    break
        fence += list(self.dma_last.values())
        self.fence = fence

    def sem(self, name):
        self.nsem += 1
        return self.stack.enter_context(self.nc.semaphore(name))

    def sbuf(self, name, shape, dt):
        return self.stack.enter_context(self.nc.sbuf_tensor("s_" + name, list(shape), dt))

    def psum(self, name, shape, dt):
        return self.stack.enter_context(self.nc.psum_tensor("p_" + name, list(shape), dt))

    def _track(self, rec, reads, writes):
        for b in reads:
            if b.w is not None:
                rec.deps.add(b.w)
                rec.raw.add(b.w)
        for b in writes:
            if b.w is not None:
                rec.deps.add(b.w)
            for r in b.r:
                rec.deps.add(r)
        for b in reads:
            if not rec.is_dma:
                b.r = [r for r in b.r if r.is_dma or r.eng != rec.eng]
            b.r.append(rec)
        for b in writes:
            b.w = rec
            b.r = []
        for f in self.fence:
            rec.deps.add(f)
            rec.raw.add(f)
        rec.deps.discard(rec)
        rec.raw.discard(rec)

    def op(self, eng, fn, reads=(), writes=()):
        cap = _Cap()
        fn(cap)
        rec = Rec(eng, cap.call, False)
        self._track(rec, reads, writes)
        self.streams[eng].append(rec)
        return rec

    def dma(self, q, fn, reads=(), writes=()):
        cap = _Cap()
        fn(cap)
        rec = Rec(q, cap.call, True)
        self._track(rec, reads, writes)
        if q not in self.dma_sems:
            self.dma_sems[q] = [self.sem(f"dma_{q}_{i}") for i in range(N_DMA_SEMS)]
            self.dma_rr[q] = 0
        i = self.dma_rr[q]
        self.dma_rr[q] = (i + 1) % N_DMA_SEMS
        s = self.dma_sems[q][i]
        key = (q, i)
        prev = self.dma_last.get(key)
        if prev is not None:
            rec.deps.add(prev)
        self.dma_last[key] = rec
        self.dma_cnt[key] = self.dma_cnt.get(key, 0) + 1
        rec.sig = (s, 16 * self.dma_cnt[key])
        self.streams[q].append(rec)
        self.all_dma.append(rec)
        return rec

    @staticmethod
    def _skip(rec, d):
        if d.is_dma or rec.is_dma or d.eng != rec.eng:
            return False
        if rec.eng == "pe":
            return True
        return d not in rec.raw

    def finish(self):
        nc = self.nc
        for e in self.ENGS:
            for rec in self.streams[e]:
                for d in rec.deps:
                    if d.is_dma or self._skip(rec, d):
                        continue
                    d.needed = True
        for e in self.ENGS:
            cnt = 0
            sem = None
            for rec in self.streams[e]:
                if rec.is_dma or not rec.needed:
                    continue
                if sem is None or cnt >= EPOCH:
                    sem = self.sem(f"c_{e}_{self.nsem}")
                    cnt = 0
                cnt += 1
                rec.sig = (sem, cnt)
            self.sigcnt = getattr(self, "sigcnt", {})
            self.sigcnt[e] = cnt
        final_waits = {}
        for rec in self.all_dma:
            s, v = rec.sig
            final_waits[id(s)] = (s, max(v, final_waits.get(id(s), (s, 0))[1]))
        streams = self.streams
        skip = self._skip

        def emit(ename, eng):
            waited = {}
            for rec in streams[ename]:
                for d in rec.deps:
                    if skip(rec, d):
                        continue
                    s, v = d.sig
                    if waited.get(id(s), 0) < v:
                        eng.wait_ge(s, v)
                        waited[id(s)] = v
                name, a_, k_ = rec.fn
                ins = getattr(eng, name)(*a_, **k_)
                if rec.is_dma:
                    ins.then_inc(rec.sig[0], 16)
                elif rec.sig is not None:
                    ins.then_inc(rec.sig[0], 1)
            if ename == "sp":
                for s, v in final_waits.values():
                    eng.wait_ge(s, v)

        with nc.Block() as block:
            @block.sync
            def _(e):
                emit("sp", e)

            @block.scalar
            def _(e):
                emit("act", e)

            @block.gpsimd
            def _(e):
                emit("pool", e)

            @block.vector
            def _(e):
                emit("dve", e)

            @block.tensor
            def _(e):
                emit("pe", e)
        self.stack.close()
        return {e: (len(self.streams[e]), self.sigcnt.get(e)) for e in self.ENGS}


class Rot:
    def __init__(self, tiles, bufs=None):
        self.tiles = tiles
        self.bufs = bufs if bufs is not None else [Buf() for _ in tiles]
        self.i = 0

    def next(self):
        i = self.i
        self.i = (i + 1) % len(self.tiles)
        return self.tiles[i], self.bufs[i]


D = 1024
L_FULL = 2
NBF = 4
CTX = 256
SEQ = 2048
T = CTX + SEQ
NT = T // 128
WC = 5376
NFM = 26
DECAY_C = -math.exp(-0.5)
NORM_EPS = 1e-6
GN_EPS = 64e-5


def _colperm():
    cols = []
    cols += list(range(768, 896)) + list(range(896, 1024))
    cols += list(range(1536, 1792)) + list(range(1792, 2048)) + list(range(2048, 2304)) + list(range(1280, 1536))

    def rot_src(base):
        out = []
        for jd in range(128):
            j, dd = divmod(jd, 64)
            g, i = divmod(dd, 32)
            out.append(base + j * 64 + g * 32 + (i + 16 if i < 16 else i - 16))
        return out

    for s0 in (2304, 2816):
        for h in range(4):
            base = s0 + h * 128
            cols += list(range(base, base + 128)) + rot_src(base)
    cols += list(range(0, 768)) + list(range(1024, 1280))
    cols += list(range(3328, 3840)) + list(range(3840, 4352))
    assert len(cols) == WC
    return np.array(cols)


def _consts():
    p = np.arange(128)[:, None]
    f = np.arange(128)[None, :]
    LE, GE, LT, GT = (p <= f), (p >= f), (p < f), (p > f)
    c = {}
    c["ident"] = np.eye(128, dtype=np.float32)
    c["cm"] = (np.stack([LE, GE, LT, GT], 1).astype(np.float32) * DECAY_C).astype(np.float32)
    m4 = np.zeros((128, 2, 512), np.float32)
    m4[:, 0] = np.concatenate([LT, LE, LT, LE], 1)
    m4[:, 1] = np.concatenate([GT, GE, GT, GE], 1)
    c["mask4"] = m4
    mL = np.zeros((128, 2, 512), np.float32)
    mL[:, 0] = np.concatenate([GT] * 4, 1)
    mL[:, 1] = np.concatenate([LT] * 4, 1)
    c["maskL"] = mL
    pos = np.arange(SEQ)
    row = (pos // 64).astype(np.float32)
    col = (pos % 64).astype(np.float32)
    inv = (10000.0 ** (-np.arange(0, 32, 2, dtype=np.float32) / 32)).astype(np.float32)
    cosT = np.ones((128, T), np.float32)
    sinT = np.zeros((128, T), np.float32)
    for pp in range(128):
        dd = pp % 64
        g, i = divmod(dd, 32)
        ang = (row if g == 0 else col) * inv[i % 16]
        cosT[pp, CTX:] = np.cos(ang)
        sinT[pp, CTX:] = np.sin(ang) * (-1.0 if i < 16 else 1.0)
    c["cosT"] = cosT
    c["sinT"] = sinT
    sel = np.zeros((5, 5, 128), np.float32)
    for b in range(5):
        sel[b, b, :] = 1.0
    c["sel"] = sel
    return c


def build(NB=NBF, NL=L_FULL, dbg=False, upto=9):
    nc = bass.Bass("TRN2", target_bir_lowering=False)
    P = Prog(nc)

    def din(name, shape, dt=F32):
        return nc.dram_tensor(name, list(shape), dt, kind="ExternalInput").ap()

    def dscr(name, shape, dt):
        return nc.dram_tensor(name, list(shape), dt, kind="Internal").ap()

    xall = din("xall", [NB, T, D])
    cc = din("cc", [5, D])
    wext = din("wext", [L_FULL, D, WC])
    woutd = din("wout", [L_FULL, D, D])
    modw = din("modw", [L_FULL, D, 3 * D])
    modb = din("modb", [L_FULL, 1, 3 * D])
    pregd = din("preg", [L_FULL, 1, D])
    postgd = din("postg", [L_FULL, 1, D])
    w0d = din("w0", [L_FULL, 1, 512])
    a0d = din("a0", [L_FULL, 1, 512])
    wupd = din("wup", [L_FULL, 128, 256])
    aupd = din("aup", [L_FULL, 128, 256])
    vec256 = din("vec256", [L_FULL, 6, 256])
    convwd = din("convw", [L_FULL, 3, 256])
    subgd = din("subg", [L_FULL, 1, 128])
    identd = din("ident", [128, 128])
    cmd = din("cm", [128, 4, 128])
    mask4d = din("mask4", [128, 2, 512])
    maskLd = din("maskL", [128, 2, 512])
    cosd = din("cosT", [128, T])
    sind = din("sinT", [128, T])
    seld = din("sel", [5, 5, 128])
    outd = nc.dram_tensor("out", [NB, SEQ, D], F32, kind="ExternalOutput").ap()
    dbgd = {}
    if dbg:
        for nm, shp, dt_ in (("d_rkvz", [T, 1024], BF16), ("d_mix", [8, 128, T], BF16), ("d_x1", [T, D], F32), ("d_q", [2, 4, 128, T], BF16)):
            dbgd[nm] = nc.dram_tensor(nm, shp, dt_, kind="ExternalOutput").ap()

    x1 = dscr("x1", [NB, T, D], F32)
    wbf = dscr("wbf", [L_FULL, 11, 128, 8, 512], BF16)
    rkvz = dscr("rkvz", [NB, T, 1024], BF16)
    avd = dscr("av", [NB, T, 512], BF16)
    aszd = dscr("asz", [NB, T, 512], BF16)
    qkT = dscr("qkT", [NB, 2, 4, 128, T], BF16)
    mixT = dscr("mixT", [NB, 8, 128, T], BF16)
    woutbf = dscr("woutbf", [L_FULL, 128, 8, D], BF16)
    B_woutbf = [Buf() for _ in range(L_FULL)]

    def bufs(n):
        return [Buf() for _ in range(n)]

    B_x = {0: [bufs(NT) for _ in range(NB)], 1: [bufs(NT) for _ in range(NB)], 2: [bufs(NT) for _ in range(NB)]}
    B_wbf = [bufs(11) for _ in range(L_FULL)]
    B_rkvz = [bufs(NT) for _ in range(NB)]
    B_av = [bufs(NT) for _ in range(NB)]
    B_asz = [bufs(NT) for _ in range(NB)]
    B_q = [bufs(NT) for _ in range(NB)]
    B_k = [bufs(NT) for _ in range(NB)]
    B_mixR = [bufs(NT) for _ in range(NB)]
    B_mixC = [bufs(NT) for _ in range(NB)]
    B_mixA = [bufs(NT) for _ in range(NB)]

    psF = Rot([P.psum(f"psF{i}", [128, 512], F32) for i in range(6)])
    psB = Rot([P.psum(f"psB{i}", [128, 1024], BF16) for i in range(2)])

    def ctile(name, shape, dt=F32):
        return P.sbuf(name, shape, dt), Buf(name)

    identf, b_identf = ctile("identf", [128, 128])
    identb, b_identb = ctile("identb", [128, 128], BF16)
    cm, b_cm = ctile("cm", [128, 4, 128])
    mask4, b_mask4 = ctile("mask4", [128, 2, 512], BF16)
    maskL, b_maskL = ctile("maskL", [128, 2, 512], BF16)
    cosT, b_cos = ctile("cosT", [128, T], BF16)
    sinT, b_sin = ctile("sinT", [128, T], BF16)
    arF = P.sbuf("arF", [128, 8192], F32)
    arB = P.sbuf("arB", [128, 29696], BF16)

    class Arena:
        def __init__(self):
            self.o = {F32: 0, BF16: 0}

        def reset(self):
            self.o = {F32: 0, BF16: 0}
            P.barrier()

        def alloc(self, shape, dt):
            n = 1
            for x in shape[1:]:
                n *= x
            ar = arF if dt == F32 else arB
            o = self.o[dt]
            assert o + n <= (8192 if dt == F32 else 29696), (shape, dt, o)
            self.o[dt] = o + n
            v = ar[0:shape[0], o:o + n]
            if len(shape) == 3:
                v = v.rearrange("p (a b) -> p a b", a=shape[1])
            elif len(shape) == 4:
                v = v.rearrange("p (a b c) -> p a b c", a=shape[1], b=shape[2])
            return v

        def rot(self, n, shape, dt):
            return Rot([self.alloc(shape, dt) for _ in range(n)])

    AR = Arena()
    sel, b_sel = ctile("sel", [5, 5, 128])
    negcol, b_negcol = ctile("negcol", [128, 1])
    onesrow, b_ones = ctile("onesrow", [1, 128])
    epsc, b_eps = ctile("epsc", [128, 2])
    P.dma("sp", lambda e: e.dma_start(out=identf[:], in_=identd[:, :]), [], [b_identf])
    P.dma("sp", lambda e: e.dma_start(out=cm[:], in_=cmd[:, :, :]), [], [b_cm])
    for (dst, bdst, src, n) in ((mask4, b_mask4, mask4d.rearrange("p a b -> p (a b)"), 1024), (maskL, b_maskL, maskLd.rearrange("p a b -> p (a b)"), 1024),
                                (cosT, b_cos, cosd, T), (sinT, b_sin, sind, T)):
        AR.reset()
        tmpc = AR.alloc([128, n], F32)
        btmp = Buf()
        P.dma("sp", lambda e, tmpc=tmpc, src=src: e.dma_start(out=tmpc, in_=src), [], [btmp])
        dv = dst[:].rearrange("p a b -> p (a b)") if n == 1024 else dst[:]
        P.op("dve", lambda e, dv=dv, tmpc=tmpc: e.tensor_copy(out=dv, in_=tmpc), [btmp], [bdst])
    P.dma("sp", lambda e: e.dma_start(out=sel[:], in_=seld[:, :, :]), [], [b_sel])
    P.op("dve", lambda e: e.tensor_copy(out=identb[:], in_=identf[:]), [b_identf], [b_identb])
    P.op("pool", lambda e: e.memset(negcol[:], DECAY_C), [], [b_negcol])
    P.op("pool", lambda e: e.memset(onesrow[:], 1.0), [], [b_ones])
    P.op("pool", lambda e: e.memset(epsc[:, 0:1], NORM_EPS), [], [b_eps])
    P.op("pool", lambda e: e.memset(epsc[:, 1:2], GN_EPS), [], [b_eps])

    AR.reset()
    wstg = AR.rot(2, [128, 8, 256], F32)
    wcast = AR.rot(2, [128, 8, 256], BF16)
    cast_eng = ["dve", "pool", "act"]
    ci = 0
    for l in range(NL):
        for sb in range(11):
            wfull = 512 if sb != 6 else 256
            c0 = sb * 512 if sb < 6 else (3072 if sb == 6 else 3328 + (sb - 7) * 512)
            for hf in range(wfull // 256):
                st, bst = wstg.next()
                cb, bcb = wcast.next()
                P.dma("sp", lambda e, st=st, l=l, c0=c0, hf=hf: e.dma_start(
                    out=st, in_=wext[l, :, c0 + hf * 256:c0 + (hf + 1) * 256].rearrange("(c p) n -> p c n", p=128)), [], [bst])
                eng = cast_eng[ci % 3]
                ci += 1
                if eng == "act":
                    P.op("act", lambda e, st=st, cb=cb: e.activation(out=cb, in_=st, func=AF.Copy), [bst], [bcb])
                else:
                    P.op(eng, lambda e, st=st, cb=cb: e.tensor_copy(out=cb, in_=st), [bst], [bcb])
                P.dma("pool", lambda e, cb=cb, l=l, sb=sb, hf=hf: e.dma_start(out=wbf[l, sb, :, :, hf * 256:(hf + 1) * 256], in_=cb), [bcb], [B_wbf[l][sb]])

    srow = P.sbuf("srow", [5, D], F32); b_srow = Buf()
    arow = P.sbuf("arow", [5, D], F32); b_arow = Buf()
    grow = P.sbuf("grow", [5, D], F32); b_grow = Buf()
    bcC = [P.sbuf(f"bcC{i}", [128, D], F32) for i in range(3)]; b_bcC = bufs(3)
    bcB = [P.sbuf(f"bcB{i}", [128, D], F32) for i in range(3)]; b_bcB = bufs(3)
    v256 = P.sbuf("v256", [128, 6, 256], F32); b_v256 = Buf()
    wup = P.sbuf("wup", [128, 2, 256], F32); b_wup = Buf()
    aup = P.sbuf("aup", [128, 2, 256], F32); b_aup = Buf()
    w0r = P.sbuf("w0r", [1, 512], F32); b_w0r = Buf()
    a0r = P.sbuf("a0r", [1, 512], F32); b_a0r = Buf()
    cwc = P.sbuf("cwc", [128, 2, 3], F32); b_cwc = Buf()
    gsub = P.sbuf("gsub", [128, 128], F32); b_gsub = Buf()
    neglam = P.sbuf("neglam", [128, 1], F32); b_neglam = Buf()
    lamt = P.sbuf("lamt", [128, 132], F32); b_lamt = Buf()
    ures = P.sbuf("ures", [128, 2, T], BF16); b_ures = Buf()
    bzres = P.sbuf("bzres", [128, 2, T], BF16); b_bzres = Buf()
    lwla = P.sbuf("lwla", [128, 2, T], F32); b_lwla = Buf()
    sm_r = Rot([P.sbuf(f"sm{i}", [128, 8], F32) for i in range(12)])

    LAM_INIT = [0.8 - 0.6 * math.exp(-0.3 * l) for l in range(L_FULL)]

    def rstd_from_ms(ms, bms, eps_col, n=1):
        P.op("act", lambda e: e.activation(out=ms, in_=ms, func=AF.Ln, bias=epsc[:, eps_col:eps_col + 1], scale=1.0), [bms, b_eps], [bms])
        P.op("act", lambda e: e.activation(out=ms, in_=ms, func=AF.Exp, scale=-0.5), [bms], [bms])

    def layer_setup(l):
        AR.reset()
        scT = AR.alloc([128, 8, 5], F32); b_scT = Buf()
        modb5 = AR.rot(2, [5, 512], F32)
        preg5 = AR.alloc([5, D], F32); b_preg5 = Buf()
        postg5 = AR.alloc([5, D], F32); b_postg5 = Buf()
        wstg = AR.rot(2, [128, 8, 256], F32)
        for c in range(8):
            P.dma("sp", lambda e, c=c: e.dma_start(out=scT[:, c, :], in_=cc[:, c * 128:(c + 1) * 128].rearrange("b p -> p b"), allow_slow_non_contiguous=True), [], [b_scT])
        P.op("act", lambda e: e.activation(out=scT, in_=scT, func=AF.Silu), [b_scT], [b_scT])
        P.dma("sp", lambda e: e.dma_start(out=preg5, in_=pregd[l, 0:1, :].broadcast_to([5, D])), [], [b_preg5])
        P.dma("sp", lambda e: e.dma_start(out=postg5, in_=postgd[l, 0:1, :].broadcast_to([5, D])), [], [b_postg5])
        dsts = [(srow, b_srow), (arow, b_arow), (grow, b_grow)]
        for cb in range(6):
            mb, bmb = modb5.next()
            P.dma("sp", lambda e, mb=mb, cb=cb: e.dma_start(out=mb, in_=modb[l, 0:1, cb * 512:(cb + 1) * 512].broadcast_to([5, 512])), [], [bmb])
            ps, bps = psF.next()
            for hf in range(2):
                st, bst = wstg.next()
                P.dma("sp", lambda e, st=st, cb=cb, hf=hf: e.dma_start(
                    out=st, in_=modw[l, :, cb * 512 + hf * 256:cb * 512 + (hf + 1) * 256].rearrange("(c p) n -> p c n", p=128)), [], [bst])
                for c in range(8):
                    P.op("pe", lambda e, ps=ps, st=st, c=c, hf=hf: e.matmul(ps[0:5, hf * 256:(hf + 1) * 256], lhsT=scT[:, c, :], rhs=st[:, c, :], start=(c == 0), stop=(c == 7)), [b_scT, bst], [bps])
            dst, bd = dsts[cb // 2]
            P.op("dve", lambda e, ps=ps, cb=cb, dst=dst, mb=mb: e.tensor_tensor(out=dst[:, (cb % 2) * 512:(cb % 2 + 1) * 512], in0=ps[0:5, :], in1=mb, op=ALU.add), [bps, bmb], [bd])
        P.op("dve", lambda e: e.scalar_tensor_tensor(out=arow[:], in0=arow[:], scalar=1.0, in1=preg5, op0=ALU.add, op1=ALU.mult), [b_arow, b_preg5], [b_arow])
        P.op("dve", lambda e: e.tensor_tensor(out=grow[:], in0=grow[:], in1=postg5, op=ALU.mult), [b_grow, b_postg5], [b_grow])
        wcs_r = AR.rot(2, [128, 8, 256], BF16)
        for q4 in range(4):
            st, bst = wstg.next()
            cb, bcb = wcs_r.next()
            P.dma("sp", lambda e, st=st, q4=q4: e.dma_start(out=st, in_=woutd[l, :, q4 * 256:(q4 + 1) * 256].rearrange("(c p) n -> p c n", p=128)), [], [bst])
            P.op("pool", lambda e, st=st, cb=cb: e.tensor_copy(out=cb, in_=st), [bst], [bcb])
            P.dma("pool", lambda e, cb=cb, q4=q4: e.dma_start(out=woutbf[l, :, :, q4 * 256:(q4 + 1) * 256], in_=cb), [bcb], [B_woutbf[l]])
        P.dma("sp", lambda e: e.dma_start(out=v256[:], in_=vec256[l:l + 1, :, :].broadcast_to([128, 6, 256])), [], [b_v256])
        P.op("pool", lambda e: e.memset(wup[:], 0.0), [], [b_wup])
        P.op("pool", lambda e: e.memset(aup[:], 0.0), [], [b_aup])
        for dd in range(2):
            P.dma("sp", lambda e, dd=dd: e.dma_start(out=wup[dd * 64:(dd + 1) * 64, dd, :], in_=wupd[l, dd * 64:(dd + 1) * 64, :]), [], [b_wup])
            P.dma("sp", lambda e, dd=dd: e.dma_start(out=aup[dd * 64:(dd + 1) * 64, dd, :], in_=aupd[l, dd * 64:(dd + 1) * 64, :]), [], [b_aup])
        P.dma("sp", lambda e: e.dma_start(out=w0r[:], in_=w0d[l, 0:1, :]), [], [b_w0r])
        P.dma("sp", lambda e: e.dma_start(out=a0r[:], in_=a0d[l, 0:1, :]), [], [b_a0r])
        for fb in range(2):
            P.dma("sp", lambda e, fb=fb: e.dma_start(out=cwc[:, fb, :], in_=convwd[l, :, fb * 128:(fb + 1) * 128].rearrange("j p -> p j"), allow_slow_non_contiguous=True), [], [b_cwc])
        P.dma("sp", lambda e: e.dma_start(out=gsub[:], in_=subgd[l, 0:1, :].broadcast_to([128, 128])), [], [b_gsub])
        P.op("dve", lambda e: e.tensor_scalar_mul(out=gsub[:], in0=gsub[:], scalar1=1.0 - LAM_INIT[l]), [b_gsub], [b_gsub])
        P.op("dve", lambda e: e.tensor_tensor(out=lamt[:, 0:64], in0=v256[:, 5, 0:64], in1=v256[:, 5, 64:128], op=ALU.mult), [b_v256], [b_lamt])
        P.op("dve", lambda e: e.tensor_tensor(out=lamt[:, 64:128], in0=v256[:, 5, 128:192], in1=v256[:, 5, 192:256], op=ALU.mult), [b_v256], [b_lamt])
        P.op("dve", lambda e: e.reduce_sum(out=lamt[:, 128:130], in_=lamt[:, 0:128].rearrange("p (a b) -> p a b", a=2), axis=AX.X), [b_lamt], [b_lamt])
        P.op("act", lambda e: e.activation(out=lamt[:, 130:132], in_=lamt[:, 128:130], func=AF.Exp), [b_lamt], [b_lamt])
        P.op("dve", lambda e: e.tensor_tensor(out=neglam[:], in0=lamt[:, 131:132], in1=lamt[:, 130:131], op=ALU.subtract), [b_lamt], [b_neglam])
        P.op("dve", lambda e: e.tensor_scalar_add(out=neglam[:], in0=neglam[:], scalar1=-LAM_INIT[l]), [b_neglam], [b_neglam])
        bcast_rows(4, bcC, b_bcC)

    def bcast_rows(row, tiles, tb):
        srcs = [(arow, b_arow, None), (srow, b_srow, None), (grow, b_grow, None)]
        for qi, (src, bsrc, _) in enumerate(srcs):
            for hf in range(2):
                ps, bps = psF.next()
                P.op("pe", lambda e, ps=ps, src=src, hf=hf: e.matmul(ps[:, :], lhsT=sel[0:5, row, :], rhs=src[0:5, hf * 512:(hf + 1) * 512], start=True, stop=True), [b_sel, bsrc], [bps])
                P.op("act", lambda e, ps=ps, qi=qi, hf=hf: e.activation(out=tiles[qi][:, hf * 512:(hf + 1) * 512], in_=ps[:, :], func=AF.Copy), [bps], [tb[qi]])

    def phase_a(l, b):
        xsrc, Bx = (xall, B_x[0]) if l == 0 else (x1, B_x[1])
        AR.reset()
        hT = AR.alloc([128, 8, 1152], BF16); b_hT = Buf()
        xt_r = AR.rot(2, [128, D], F32)
        f32a = AR.rot(2, [128, D], F32)
        t512 = AR.rot(4, [128, 512], F32)
        hb_r = AR.rot(2, [128, D], BF16)
        junk_r = AR.rot(2, [128, D], BF16)
        wblk_r = AR.rot(2, [128, 8, 512], BF16)
        stg_r = AR.rot(3, [128, 1024], BF16)
        for part in range(2):
            t0 = part * 9
            for ti in range(9):
                t = t0 + ti
                A, S = (bcC, b_bcC) if t < 2 else (bcB, b_bcB)
                xt, bxt = xt_r.next()
                P.dma("sp", lambda e, xt=xt, t=t: e.dma_start(out=xt[:], in_=xsrc[b, t * 128:(t + 1) * 128, :]), [Bx[b][t]], [bxt])
                jk, bjk = junk_r.next()
                sm, bsm = sm_r.next()
                P.op("act", lambda e, xt=xt, jk=jk, sm=sm: e.activation(out=jk[:], in_=xt[:], func=AF.Square, scale=1.0 / 32.0, accum_out=sm[:, 0:1]), [bxt], [bjk, bsm])
                rstd_from_ms(sm[:, 0:1], bsm, 0)
                fa, bfa = f32a.next()
                P.op("dve", lambda e, fa=fa, xt=xt, sm=sm, A=A: e.scalar_tensor_tensor(out=fa[:], in0=xt[:], scalar=sm[:, 0:1], in1=A[0][:], op0=ALU.mult, op1=ALU.mult), [bxt, bsm, S[0]], [bfa])
                hb, bhb = hb_r.next()
                P.op("pool", lambda e, hb=hb, fa=fa, A=A: e.tensor_tensor(out=hb[:], in0=fa[:], in1=A[1][:], op=ALU.add), [bfa, S[1]], [bhb])
                pb, bpb = psB.next()
                for c in range(8):
                    P.op("pe", lambda e, pb=pb, hb=hb, c=c: e.transpose(out=pb[:, c * 128:(c + 1) * 128], in_=hb[:, c * 128:(c + 1) * 128], identity=identb[:]), [bhb, b_identb], [bpb])
                P.op("act", lambda e, pb=pb, ti=ti: e.activation(out=hT[:, :, ti * 128:(ti + 1) * 128], in_=pb[:, :].rearrange("p (c t) -> p c t", c=8), func=AF.Copy), [bpb], [b_hT])
            groups = [(0, 4), (4, 4), (8, 1)]
            for sb in range(11):
                wb, bwb = wblk_r.next()
                w = 512 if sb != 6 else 256
                P.dma("sp", lambda e, wb=wb, sb=sb, w=w: e.dma_start(out=wb[:, :, 0:w], in_=wbf[l, sb, :, :, 0:w]), [B_wbf[l][sb]], [bwb])
                if sb < 7:
                    nblk = 4 if sb < 6 else 2
                    k = 0
                    while k < nblk:
                        fm = sb * 4 + k
                        pair = fm >= 10
                        for (g0, gn) in groups:
                            ntok = gn * 128
                            tok0 = (t0 + g0) * 128
                            loc0 = g0 * 128
                            tiles_g = list(range(t0 + g0, t0 + g0 + gn))
                            ps, bps = psF.next()
                            for c in range(8):
                                P.op("pe", lambda e, ps=ps, wb=wb, c=c, k=k, loc0=loc0, ntok=ntok: e.matmul(ps[:, 0:ntok], lhsT=wb[:, c, k * 128:(k + 1) * 128], rhs=hT[:, c, loc0:loc0 + ntok], start=(c == 0), stop=(c == 7)), [bwb, b_hT], [bps])
                            if pair:
                                ps2, bps2 = psF.next()
                                for c in range(8):
                                    P.op("pe", lambda e, ps2=ps2, wb=wb, c=c, k=k, loc0=loc0, ntok=ntok: e.matmul(ps2[:, 0:ntok], lhsT=wb[:, c, (k + 1) * 128:(k + 2) * 128], rhs=hT[:, c, loc0:loc0 + ntok], start=(c == 0), stop=(c == 7)), [bwb, b_hT], [bps2])
                                ta, bta = t512.next()
                                tb_, btb = t512.next()
                                P.op("dve", lambda e, ta=ta, ps=ps, tok0=tok0, ntok=ntok: e.tensor_tensor(out=ta[:, 0:ntok], in0=ps[:, 0:ntok], in1=cosT[:, tok0:tok0 + ntok], op=ALU.mult), [bps, b_cos], [bta])
                                P.op("dve", lambda e, tb_=tb_, ps2=ps2, tok0=tok0, ntok=ntok: e.tensor_tensor(out=tb_[:, 0:ntok], in0=ps2[:, 0:ntok], in1=sinT[:, tok0:tok0 + ntok], op=ALU.mult), [bps2, b_sin], [btb])
                                sg, bsg = stg_r.next()
                                P.op("pool", lambda e, sg=sg, ta=ta, tb_=tb_, ntok=ntok: e.tensor_tensor(out=sg[:, 0:ntok], in0=ta[:, 0:ntok], in1=tb_[:, 0:ntok], op=ALU.add), [bta, btb], [bsg])
                                qk = 0 if fm < 18 else 1
                                hh = ((fm - 10) // 2) % 4
                                Bq = (B_q if qk == 0 else B_k)[b]
                                P.dma("pool", lambda e, sg=sg, qk=qk, hh=hh, tok0=tok0, ntok=ntok: e.dma_start(out=qkT[b, qk, hh, :, tok0:tok0 + ntok], in_=sg[:, 0:ntok]), [bsg], [Bq[t] for t in tiles_g])
                            elif fm == 0:
                                P.op("act", lambda e, ps=ps, tok0=tok0, ntok=ntok: e.activation(out=lwla[:, 0, tok0:tok0 + ntok], in_=ps[:, 0:ntok], func=AF.Tanh), [bps], [b_lwla])
                            elif fm == 1:
                                P.op("act", lambda e, ps=ps, tok0=tok0, ntok=ntok: e.activation(out=lwla[:, 1, tok0:tok0 + ntok], in_=ps[:, 0:ntok], func=AF.Copy), [bps], [b_lwla])
                            elif fm in (2, 3):
                                P.op("act", lambda e, ps=ps, fm=fm, tok0=tok0, ntok=ntok: e.activation(out=ures[:, fm - 2, tok0:tok0 + ntok], in_=ps[:, 0:ntok], func=AF.Copy), [bps], [b_ures])
                            elif fm in (4, 5):
                                P.op("dve", lambda e, ps=ps, fm=fm, tok0=tok0, ntok=ntok: e.tensor_tensor(out=ures[:, fm - 4, tok0:tok0 + ntok], in0=ps[:, 0:ntok], in1=ures[:, fm - 4, tok0:tok0 + ntok], op=ALU.mult), [bps, b_ures], [b_ures])
                            elif fm in (6, 7):
                                P.op("act", lambda e, ps=ps, fm=fm, tok0=tok0, ntok=ntok: e.activation(out=bzres[:, fm - 6, tok0:tok0 + ntok], in_=ps[:, 0:ntok], func=AF.Silu), [bps], [b_bzres])
                            elif fm in (8, 9):
                                P.op("dve", lambda e, ps=ps, fm=fm, tok0=tok0, ntok=ntok: e.tensor_tensor(out=bzres[:, fm - 8, tok0:tok0 + ntok], in0=ps[:, 0:ntok], in1=bzres[:, fm - 8, tok0:tok0 + ntok], op=ALU.mult), [bps, b_bzres], [b_bzres])
                        k += 2 if pair else 1
                else:
                    tmb = sb - 7
                    for ti in range(9):
                        t = t0 + ti
                        ps, bps = psF.next()
                        for c in range(8):
                            P.op("pe", lambda e, ps=ps, wb=wb, c=c, ti=ti: e.matmul(ps[:, :], lhsT=hT[:, c, ti * 128:(ti + 1) * 128], rhs=wb[:, c, :], start=(c == 0), stop=(c == 7)), [bwb, b_hT], [bps])
                        sg, bsg = stg_r.next()
                        if tmb == 3:
                            P.op("act", lambda e, sg=sg, ps=ps: e.activation(out=sg[:, 0:512], in_=ps[:, :], func=AF.Silu), [bps], [bsg])
                            P.dma("pool", lambda e, sg=sg, t=t: e.dma_start(out=aszd[b, t * 128:(t + 1) * 128, :], in_=sg[:, 0:512]), [bsg], [B_asz[b][t]])
                        elif tmb == 2:
                            P.op("dve", lambda e, sg=sg, ps=ps: e.tensor_copy(out=sg[:, 0:512], in_=ps[:, :]), [bps], [bsg])
                            P.dma("pool", lambda e, sg=sg, t=t: e.dma_start(out=avd[b, t * 128:(t + 1) * 128, :], in_=sg[:, 0:512]), [bsg], [B_av[b][t]])
                        else:
                            if tmb == 0:
                                P.op("act", lambda e, sg=sg, ps=ps: e.activation(out=sg[:, 0:512], in_=ps[:, :], func=AF.Copy), [bps], [bsg])
                            else:
                                P.op("dve", lambda e, sg=sg, ps=ps: e.tensor_copy(out=sg[:, 0:512], in_=ps[:, :]), [bps], [bsg])
                            P.dma("pool", lambda e, sg=sg, t=t, tmb=tmb: e.dma_start(out=rkvz[b, t * 128:(t + 1) * 128, tmb * 512:(tmb + 1) * 512], in_=sg[:, 0:512]), [bsg], [B_rkvz[b][t]])

    def phase_conv(l, b):
        AR.reset()
        cvt = AR.alloc([128, SEQ], F32); b_cvt = Buf()
        cvs = AR.alloc([128, SEQ], BF16); b_cvs = Buf()
        ranges = ([(0, CTX)] if l == 0 else []) + [(CTX, T)]
        for (r0, r1) in ranges:
            n = r1 - r0
            for fb in range(2):
                P.op("pool", lambda e, fb=fb, r0=r0, n=n: e.tensor_scalar(out=cvt[:, 0:n], in0=ures[:, fb, r0:r0 + n], scalar1=cwc[:, fb, 1:2], scalar2=None, op0=ALU.mult), [b_ures, b_cwc], [b_cvt])
                P.op("dve", lambda e, fb=fb, r0=r0, n=n: e.scalar_tensor_tensor(out=cvt[:, 1:n], in0=ures[:, fb, r0:r0 + n - 1], scalar=cwc[:, fb, 0:1], in1=cvt[:, 1:n], op0=ALU.mult, op1=ALU.add), [b_ures, b_cwc, b_cvt], [b_cvt])
                P.op("dve", lambda e, fb=fb, r0=r0, n=n: e.scalar_tensor_tensor(out=cvt[:, 0:n - 1], in0=ures[:, fb, r0 + 1:r0 + n], scalar=cwc[:, fb, 2:3], in1=cvt[:, 0:n - 1], op0=ALU.mult, op1=ALU.add), [b_ures, b_cwc, b_cvt], [b_cvt])
                P.op("pool", lambda e, fb=fb, r0=r0, n=n: e.tensor_tensor(out=cvs[:, 0:n], in0=cvt[:, 0:n], in1=bzres[:, fb, r0:r0 + n], op=ALU.mult), [b_cvt, b_bzres], [b_cvs])
                P.dma("pool", lambda e, fb=fb, r0=r0, n=n: e.dma_start(out=mixT[b, 2 + fb, :, r0:r0 + n], in_=cvs[:, 0:n]), [b_cvs], [B_mixC[b][t] for t in range(r0 // 128, r1 // 128)])

    def bc4(ap4):
        return ap4.unsqueeze(2).to_broadcast([128, 4, 64])

    def v3(ap):
        return ap.rearrange("p (h e) -> p h e", h=4)

    def phase_rwkv(l, b):
        AR.reset()
        Yf = AR.alloc([128, NT, 256], F32); b_Yf = bufs(NT)
        f256 = AR.rot(10, [128, 256], F32)
        e12_r = AR.rot(2, [128, 512], F32)
        Ksum = AR.alloc([128, NT, 256], BF16); b_Ksum = bufs(NT)
        rk_r = AR.rot(2, [128, 1024], BF16)
        b256 = AR.rot(14, [128, 256], BF16)
        FMz_r = [AR.rot(2, [128, 8, 128], BF16) for _ in range(2)]
        for par in range(2):
            for (tl, tb_) in zip(FMz_r[par].tiles, FMz_r[par].bufs):
                P.op("pool", lambda e, tl=tl: e.memset(tl, 0.0), [], [tb_])
        XT_r = AR.rot(2, [128, 4, 512], BF16)
        Lp_r = AR.rot(3, [128, 4, 128], BF16)
        LpT_r = AR.rot(3, [128, 4, 128], BF16)
        Z_r = AR.rot(2, [128, 4, 128], BF16)
        PhiT_r = AR.rot(2, [64, 4, 64], BF16)
        RhT_r = AR.rot(2, [64, 4, 128], BF16)
        H_r = AR.rot(2, [64, 4, 64], BF16)
        ro_r = AR.rot(2, [128, 2, 128], BF16)
        kkb, kab, rkb, lngb, lnbb = (v256[:, i, :] for i in range(5))
        for d in range(RW_ND):
            order = (list(range(NT)) if d == 0 else [1, 0] + list(range(NT - 1, 1, -1)))[:RW_NCH]
            H, bH = H_r.next()
            P.op("pool", lambda e, H=H: e.memset(H[:], 0.0), [], [bH])
            i_incl, i_strict, i_rem = (0, 2, 3) if d == 0 else (1, 3, 2)
            for ch in order:
                tk = slice(ch * 128, (ch + 1) * 128)
                rk, brk = rk_r.next()
                P.dma("sp", lambda e, rk=rk, ch=ch: e.dma_start(out=rk[:], in_=rkvz[b, ch * 128:(ch + 1) * 128, :]), [B_rkvz[b][ch]], [brk])
                r_, k_, v_, z_ = (rk[:, i * 256:(i + 1) * 256] for i in range(4))
                t1, bt1 = f256.next()
                P.op("dve", lambda e, t1=t1, k_=k_: e.tensor_tensor(out=t1[:], in0=k_, in1=kkb, op=ALU.mult), [brk, b_v256], [bt1])
                t2, bt2 = f256.next()
                P.op("pool", lambda e, t1=t1, t2=t2: e.tensor_tensor(out=t2[:], in0=t1[:], in1=t1[:], op=ALU.mult), [bt1], [bt2])
                sm, bsm = sm_r.next()
                P.op("dve", lambda e, sm=sm, t2=t2: e.reduce_sum(out=sm[:, 0:4], in_=v3(t2[:]), axis=AX.X), [bt2], [bsm])
                P.op("dve", lambda e, sm=sm: e.tensor_scalar_max(out=sm[:, 0:4], in0=sm[:, 0:4], scalar1=1e-24), [bsm], [bsm])
                P.op("act", lambda e, sm=sm: e.activation(out=sm[:, 0:4], in_=sm[:, 0:4], func=AF.Ln), [bsm], [bsm])
                P.op("act", lambda e, sm=sm: e.activation(out=sm[:, 0:4], in_=sm[:, 0:4], func=AF.Exp, scale=-0.5), [bsm], [bsm])
                kk, bkk = f256.next()
                P.op("dve", lambda e, kk=kk, t1=t1, sm=sm: e.tensor_tensor(out=v3(kk[:]), in0=v3(t1[:]), in1=bc4(sm[:, 0:4]), op=ALU.mult), [bt1, bsm], [bkk])
                if RW_CUT < 1:
                    continue
                psA, bpsA = psF.next()
                pp = slice(d * 64, (d + 1) * 64)
                P.op("pe", lambda e, psA=psA: e.matmul(psA[:, 0:256], lhsT=lwla[:, 1, tk], rhs=aup[:, d, :], start=True, stop=False), [b_lwla, b_aup], [bpsA])
                P.op("pe", lambda e, psA=psA: e.matmul(psA[:, 0:256], lhsT=onesrow[0:1, :], rhs=a0r[0:1, d * 256:(d + 1) * 256], start=False, stop=True), [b_ones, b_a0r], [bpsA])
                P.op("pe", lambda e, psA=psA: e.matmul(psA[:, 256:512], lhsT=lwla[:, 0, tk], rhs=wup[:, d, :], start=True, stop=False), [b_lwla, b_wup], [bpsA])
                P.op("pe", lambda e, psA=psA: e.matmul(psA[:, 256:512], lhsT=onesrow[0:1, :], rhs=w0r[0:1, d * 256:(d + 1) * 256], start=False, stop=True), [b_ones, b_w0r], [bpsA])
                asg, basg = e12_r.next()
                P.op("act", lambda e, asg=asg, psA=psA: e.activation(out=asg[:], in_=psA[:, :], func=AF.Sigmoid), [bpsA], [basg])
                a_ = asg[:, 0:256]
                sg_ = asg[:, 256:512]
                psX, bpsX = psF.next()
                psY, bpsY = psF.next()
                P.op("pe", lambda e, psX=psX: e.matmul(psX[:, 0:256], lhsT=cm[:, i_incl, :], rhs=sg_, start=True, stop=True), [b_cm, basg], [bpsX])
                P.op("pe", lambda e, psX=psX: e.matmul(psX[:, 256:512], lhsT=cm[:, i_strict, :], rhs=sg_, start=True, stop=True), [b_cm, basg], [bpsX])
                P.op("pe", lambda e, psY=psY: e.matmul(psY[:, 0:256], lhsT=cm[:, i_rem, :], rhs=sg_, start=True, stop=True), [b_cm, basg], [bpsY])
                for h in range(4):
                    P.op("pe", lambda e, psY=psY, h=h: e.matmul(psY[0:64, 256 + h:257 + h], lhsT=asg[:, 256 + h * 64:256 + (h + 1) * 64], rhs=negcol[:, 0:1], start=True, stop=True), [basg, b_negcol], [bpsY])
                e12, be12 = e12_r.next()
                P.op("act", lambda e, e12=e12, psX=psX: e.activation(out=e12[:], in_=psX[:, :], func=AF.Exp), [bpsX], [be12])
                encw, bencw = f256.next()
                P.op("act", lambda e, encw=encw, psX=psX: e.activation(out=encw[:], in_=psX[:, 0:256], func=AF.Exp, scale=-1.0), [bpsX], [bencw])
                erem, berem = f256.next()
                P.op("act", lambda e, erem=erem, psY=psY: e.activation(out=erem[:], in_=psY[:, 0:256], func=AF.Exp), [bpsY], [berem])
                wcs, bwcs = sm_r.next()
                P.op("act", lambda e, wcs=wcs, psY=psY: e.activation(out=wcs[0:64, 0:4], in_=psY[0:64, 256:260], func=AF.Exp), [bpsY], [bwcs])
                if RW_CUT < 2:
                    continue
                tt, btt = f256.next()
                P.op("dve", lambda e, tt=tt: e.scalar_tensor_tensor(out=tt[:], in0=a_, scalar=-1.0, in1=kab, op0=ALU.add, op1=ALU.mult), [basg, b_v256], [btt])
                kmod, bkmod = f256.next()
                P.op("dve", lambda e, tt=tt, kmod=kmod, k_=k_: e.scalar_tensor_tensor(out=kmod[:], in0=tt[:], scalar=1.0, in1=k_, op0=ALU.add, op1=ALU.mult), [btt, brk], [bkmod])
                bq, bbq = f256.next()
                P.op("pool", lambda e, bq=bq, kk=kk: e.tensor_tensor(out=bq[:], in0=kk[:], in1=a_, op=ALU.mult), [bkk, basg], [bbq])
                At, bAt = b256.next()
                P.op("dve", lambda e, At=At, kk=kk, e12=e12: e.scalar_tensor_tensor(out=At[:], in0=kk[:], scalar=-1.0, in1=e12[:, 256:512], op0=ALU.mult, op1=ALU.mult), [bkk, be12], [bAt])
                Rt, bRt = b256.next()
                P.op("dve", lambda e, Rt=Rt, e12=e12, r_=r_: e.tensor_tensor(out=Rt[:], in0=r_, in1=e12[:, 0:256], op=ALU.mult), [brk, be12], [bRt])
                Bt, bBt = b256.next()
                P.op("pool", lambda e, Bt=Bt, bq=bq, encw=encw: e.tensor_tensor(out=Bt[:], in0=bq[:], in1=encw[:], op=ALU.mult), [bbq, bencw], [bBt])
                Kt, bKt = b256.next()
                P.op("dve", lambda e, Kt=Kt, kmod=kmod, encw=encw: e.tensor_tensor(out=Kt[:], in0=kmod[:], in1=encw[:], op=ALU.mult), [bkmod, bencw], [bKt])
                Bb, bBb = b256.next()
                P.op("pool", lambda e, Bb=Bb, bq=bq, erem=erem: e.tensor_tensor(out=Bb[:], in0=bq[:], in1=erem[:], op=ALU.mult), [bbq, berem], [bBb])
                Kb, bKb = b256.next()
                P.op("pool", lambda e, Kb=Kb, kmod=kmod, erem=erem: e.tensor_tensor(out=Kb[:], in0=kmod[:], in1=erem[:], op=ALU.mult), [bkmod, berem], [bKb])
                if d == 0:
                    P.op("pool", lambda e, kmod=kmod, ch=ch: e.tensor_copy(out=Ksum[:, ch, :], in_=kmod[:]), [bkmod], [b_Ksum[ch]])
                if RW_CUT < 3:
                    continue
                pb, bpb = psB.next()
                for qi, (src, bsrc) in enumerate(((At, bAt), (Rt, bRt), (Bt, bBt), (Kt, bKt))):
                    for fb in range(2):
                        P.op("pe", lambda e, pb=pb, src=src, fb=fb, qi=qi: e.transpose(out=pb[:, (fb * 4 + qi) * 128:(fb * 4 + qi + 1) * 128], in_=src[:, fb * 128:(fb + 1) * 128], identity=identb[:]), [bsrc, b_identb], [bpb])
                FMz = [FMz_r[0].next(), FMz_r[1].next()]
                P.op("act", lambda e, FMz=FMz, pb=pb: e.activation(out=FMz[0][0][0:64], in_=pb[0:64, :].rearrange("p (c t) -> p c t", c=8), func=AF.Copy), [bpb], [FMz[0][1]])
                P.op("dve", lambda e, FMz=FMz, pb=pb: e.tensor_copy(out=FMz[1][0][64:128], in_=pb[64:128, :].rearrange("p (c t) -> p c t", c=8)), [bpb], [FMz[1][1]])
                if RW_CUT < 4:
                    continue
                XT, bXT = XT_r.next()
                psL, bpsL = psF.next()
                for h in range(4):
                    fb = h // 2
                    FMq, bFMq = FMz[h % 2]
                    pq = slice(0, 128)
                    ps, bps = psF.next()
                    P.op("pe", lambda e, ps=ps, FMq=FMq, fb=fb, pq=pq: e.matmul(ps[:, 0:256], lhsT=FMq[pq, fb * 4 + 2, :], rhs=FMq[pq, fb * 4:fb * 4 + 2, :], start=True, stop=True), [bFMq], [bps])
                    P.op("pe", lambda e, ps=ps, FMq=FMq, fb=fb, pq=pq: e.matmul(ps[:, 256:512], lhsT=FMq[pq, fb * 4 + 3, :], rhs=FMq[pq, fb * 4:fb * 4 + 2, :], start=True, stop=True), [bFMq], [bps])
                    if RW_SUB >= 1:
                        P.op("dve", lambda e, ps=ps, XT=XT, h=h: e.tensor_tensor(out=XT[:, h, :], in0=ps[:, :], in1=mask4[:, d, :], op=ALU.mult), [bps, b_mask4], [bXT])
                    if RW_SUB >= 2:
                      pq2 = slice(0, 64) if RW_VAR == 1 else pq
                      P.op("pe", lambda e, psL=psL, FMq=FMq, fb=fb, pq2=pq2, h=h: e.matmul(psL[:, h * 128:(h + 1) * 128], lhsT=FMq[pq2, fb * 4 + 0, :], rhs=FMq[pq2, fb * 4 + 2, :], start=True, stop=True), [bFMq], [bpsL])
                Lp, bLp = Lp_r.next()
                if RW_SUB >= 3:
                  P.op("dve", lambda e, Lp=Lp, psL=psL: e.tensor_tensor(out=Lp[:].rearrange("p h t -> p (h t)"), in0=psL[:, :], in1=maskL[:, d, :], op=ALU.mult), [bpsL, b_maskL], [bLp])
                if RW_CUT < 5:
                    continue
                psP, bpsP = psF.next()
                for h in range(4):
                    P.op("pe", lambda e, psP=psP, XT=XT, h=h, rk=rk: e.matmul(psP[:, h * 64:(h + 1) * 64], lhsT=XT[:, h, 256:384], rhs=rk[:, 512 + h * 64:512 + (h + 1) * 64], start=True, stop=True), [bXT, brk], [bpsP])
                Z, bZ = Z_r.next()
                P.op("pool", lambda e, Z=Z, At=At: e.tensor_copy(out=Z[:, :, 0:64], in_=v3(At[:])), [bAt], [bZ])
                P.op("act", lambda e, Z=Z, psP=psP: e.activation(out=Z[:, :, 64:128], in_=v3(psP[:, 0:256]), func=AF.Copy), [bpsP], [bZ])
                if RW_CUT < 6:
                    continue
                LpT_first = True
                LpT, bLpT = None, None
                for j in range(7):
                    psZ, bpsZ = psF.next()
                    for h in range(4):
                        lt = XT[:, h, 0:128] if LpT_first else LpT[:, h, :]
                        blt = bXT if LpT_first else bLpT
                        P.op("pe", lambda e, psZ=psZ, lt=lt, Z=Z, h=h: e.matmul(psZ[:, h * 128:(h + 1) * 128], lhsT=lt, rhs=Z[:, h, :], start=True, stop=True), [blt, bZ], [bpsZ])
                    P.op("dve", lambda e, Z=Z, psZ=psZ: e.tensor_tensor(out=Z[:].rearrange("p h t -> p (h t)"), in0=psZ[:, :], in1=Z[:].rearrange("p h t -> p (h t)"), op=ALU.add), [bpsZ, bZ], [bZ])
                    if j < 6:
                        ps1, bps1 = psF.next()
                        ps2, bps2 = psF.next()
                        for h in range(4):
                            lt = XT[:, h, 0:128] if LpT_first else LpT[:, h, :]
                            blt = bXT if LpT_first else bLpT
                            P.op("pe", lambda e, ps1=ps1, lt=lt, Lp=Lp, h=h: e.matmul(ps1[:, h * 128:(h + 1) * 128], lhsT=lt, rhs=Lp[:, h, :], start=True, stop=True), [blt, bLp], [bps1])
                            P.op("pe", lambda e, ps2=ps2, lt=lt, Lp=Lp, h=h: e.matmul(ps2[:, h * 128:(h + 1) * 128], lhsT=Lp[:, h, :], rhs=lt, start=True, stop=True), [blt, bLp], [bps2])
                        nLp, bnLp = Lp_r.next()
                        nLpT, bnLpT = LpT_r.next()
                        P.op("act", lambda e, nLp=nLp, ps1=ps1: e.activation(out=nLp[:].rearrange("p h t -> p (h t)"), in_=ps1[:, :], func=AF.Copy), [bps1], [bnLp])
                        P.op("dve", lambda e, nLpT=nLpT, ps2=ps2: e.tensor_copy(out=nLpT[:].rearrange("p h t -> p (h t)"), in_=ps2[:, :]), [bps2], [bnLpT])
                        Lp, bLp, LpT, bLpT = nLp, bnLp, nLpT, bnLpT
                        LpT_first = False
                if RW_CUT < 7:
                    continue
                psFh, bpsFh = psF.next()
                psR, bpsR = psF.next()
                for h in range(4):
                    P.op("pe", lambda e, psFh=psFh, Z=Z, Bb=Bb, h=h: e.matmul(psFh[0:64, h * 64:(h + 1) * 64], lhsT=Z[:, h, 0:64], rhs=Bb[:, h * 64:(h + 1) * 64], start=True, stop=True), [bZ, bBb], [bpsFh])
                    P.op("pe", lambda e, psR=psR, Z=Z, XT=XT, h=h: e.matmul(psR[0:64, h * 128:(h + 1) * 128], lhsT=Z[:, h, 0:64], rhs=XT[:, h, 128:256], start=True, stop=False), [bZ, bXT], [bpsR])
                    P.op("pe", lambda e, psR=psR, Rt=Rt, h=h: e.matmul(psR[0:64, h * 128:(h + 1) * 128], lhsT=Rt[:, h * 64:(h + 1) * 64], rhs=identb[:], start=False, stop=True), [bRt, b_identb], [bpsR])
                PhiT, bPhiT = PhiT_r.next()
                for h in range(4):
                    P.op("dve", lambda e, PhiT=PhiT, psFh=psFh, wcs=wcs, h=h: e.scalar_tensor_tensor(out=PhiT[:, h, :], in0=identf[0:64, 0:64], scalar=wcs[0:64, h:h + 1], in1=psFh[0:64, h * 64:(h + 1) * 64], op0=ALU.mult, op1=ALU.add), [bpsFh, bwcs, b_identf], [bPhiT])
                RhT, bRhT = RhT_r.next()
                P.op("act", lambda e, RhT=RhT, psR=psR: e.activation(out=RhT[:].rearrange("p h t -> p (h t)"), in_=psR[0:64, :], func=AF.Copy), [bpsR], [bRhT])
                if RW_CUT < 8:
                    continue
                psYo, bpsYo = psF.next()
                psH, bpsH = psF.next()
                for h in range(4):
                    vh = rk[:, 512 + h * 64:512 + (h + 1) * 64]
                    P.op("pe", lambda e, psYo=psYo, XT=XT, Z=Z, h=h: e.matmul(psYo[:, h * 64:(h + 1) * 64], lhsT=XT[:, h, 128:256], rhs=Z[:, h, 64:128], start=True, stop=False), [bXT, bZ], [bpsYo])
                    P.op("pe", lambda e, psYo=psYo, XT=XT, vh=vh, h=h: e.matmul(psYo[:, h * 64:(h + 1) * 64], lhsT=XT[:, h, 384:512], rhs=vh, start=False, stop=False), [bXT, brk], [bpsYo])
                    P.op("pe", lambda e, psYo=psYo, RhT=RhT, H=H, h=h: e.matmul(psYo[:, h * 64:(h + 1) * 64], lhsT=RhT[:, h, :], rhs=H[:, h, :], start=False, stop=True), [bRhT, bH], [bpsYo])
                    P.op("pe", lambda e, psH=psH, PhiT=PhiT, H=H, h=h: e.matmul(psH[0:64, h * 64:(h + 1) * 64], lhsT=PhiT[:, h, :], rhs=H[:, h, :], start=True, stop=False), [bPhiT, bH], [bpsH])
                    P.op("pe", lambda e, psH=psH, Bb=Bb, Z=Z, h=h: e.matmul(psH[0:64, h * 64:(h + 1) * 64], lhsT=Bb[:, h * 64:(h + 1) * 64], rhs=Z[:, h, 64:128], start=False, stop=False), [bBb, bZ], [bpsH])
                    P.op("pe", lambda e, psH=psH, Kb=Kb, vh=vh, h=h: e.matmul(psH[0:64, h * 64:(h + 1) * 64], lhsT=Kb[:, h * 64:(h + 1) * 64], rhs=vh, start=False, stop=True), [bKb, brk], [bpsH])
                nH, bnH = H_r.next()
                P.op("act", lambda e, nH=nH, psH=psH: e.activation(out=nH[:].rearrange("p h t -> p (h t)"), in_=psH[0:64, 0:256], func=AF.Copy), [bpsH], [bnH])
                H, bH = nH, bnH
                if d == 0:
                    P.op("dve", lambda e, psYo=psYo, ch=ch: e.tensor_copy(out=Yf[:, ch, :], in_=psYo[:, 0:256]), [bpsYo], [b_Yf[ch]])
                    continue
                if l == NL - 1 and ch < 2:
                    continue
                if RW_CUT < 9:
                    continue
                y, by = f256.next()
                P.op("dve", lambda e, y=y, psYo=psYo, ch=ch: e.tensor_tensor(out=y[:], in0=psYo[:, 0:256], in1=Yf[:, ch, :], op=ALU.add), [bpsYo, b_Yf[ch]], [by])
                s1, bs1 = sm_r.next()
                P.op("dve", lambda e, s1=s1, y=y: e.reduce_sum(out=s1[:, 0:4], in_=v3(y[:]), axis=AX.X), [by], [bs1])
                P.op("dve", lambda e, s1=s1: e.tensor_scalar_mul(out=s1[:, 0:4], in0=s1[:, 0:4], scalar1=-1.0 / 64.0), [bs1], [bs1])
                yc, byc = f256.next()
                P.op("dve", lambda e, yc=yc, y=y, s1=s1: e.tensor_tensor(out=v3(yc[:]), in0=v3(y[:]), in1=bc4(s1[:, 0:4]), op=ALU.add), [by, bs1], [byc])
                sq, bsq = f256.next()
                P.op("pool", lambda e, sq=sq, yc=yc: e.tensor_tensor(out=sq[:], in0=yc[:], in1=yc[:], op=ALU.mult), [byc], [bsq])
                P.op("dve", lambda e, s1=s1, sq=sq: e.reduce_sum(out=s1[:, 4:8], in_=v3(sq[:]), axis=AX.X), [bsq], [bs1])
                P.op("act", lambda e, s1=s1: e.activation(out=s1[:, 4:8], in_=s1[:, 4:8], func=AF.Ln, bias=epsc[:, 1:2], scale=1.0 / 64.0), [bs1, b_eps], [bs1])
                P.op("act", lambda e, s1=s1: e.activation(out=s1[:, 4:8], in_=s1[:, 4:8], func=AF.Exp, scale=-0.5), [bs1], [bs1])
                yn, byn = f256.next()
                P.op("dve", lambda e, yn=yn, yc=yc, s1=s1: e.tensor_tensor(out=v3(yn[:]), in0=v3(yc[:]), in1=bc4(s1[:, 4:8]), op=ALU.mult), [byc, bs1], [byn])
                P.op("pool", lambda e, yn=yn: e.tensor_tensor(out=yn[:], in0=yn[:], in1=lngb, op=ALU.mult), [byn, b_v256], [byn])
                P.op("pool", lambda e, yn=yn: e.tensor_tensor(out=yn[:], in0=yn[:], in1=lnbb, op=ALU.add), [byn, b_v256], [byn])
                ks, bks = f256.next()
                P.op("pool", lambda e, ks=ks, kmod=kmod, ch=ch: e.tensor_tensor(out=ks[:], in0=kmod[:], in1=Ksum[:, ch, :], op=ALU.add), [bkmod, b_Ksum[ch]], [bks])
                P.op("pool", lambda e, ks=ks, r_=r_: e.tensor_tensor(out=ks[:], in0=ks[:], in1=r_, op=ALU.mult), [bks, brk], [bks])
                P.op("pool", lambda e, ks=ks: e.tensor_tensor(out=ks[:], in0=ks[:], in1=rkb, op=ALU.mult), [bks, b_v256], [bks])
                s2, bs2 = sm_r.next()
                P.op("dve", lambda e, s2=s2, ks=ks: e.reduce_sum(out=s2[:, 0:4], in_=v3(ks[:]), axis=AX.X), [bks], [bs2])
                bv, bbv = f256.next()
                P.op("dve", lambda e, bv=bv, v_=v_, s2=s2: e.tensor_tensor(out=v3(bv[:]), in0=v3(v_), in1=bc4(s2[:, 0:4]), op=ALU.mult), [brk, bs2], [bbv])
                P.op("pool", lambda e, yn=yn, bv=bv: e.tensor_tensor(out=yn[:], in0=yn[:], in1=bv[:], op=ALU.add), [byn, bbv], [byn])
                sz, bsz = f256.next()
                P.op("act", lambda e, sz=sz, z_=z_: e.activation(out=sz[:], in_=z_, func=AF.Silu), [brk], [bsz])
                yb, byb = b256.next()
                P.op("dve", lambda e, yb=yb, yn=yn, sz=sz: e.tensor_tensor(out=yb[:], in0=yn[:], in1=sz[:], op=ALU.mult), [byn, bsz], [byb])
                pb2, bpb2 = psB.next()
                for fb in range(2):
                    P.op("pe", lambda e, pb2=pb2, yb=yb, fb=fb: e.transpose(out=pb2[:, fb * 128:(fb + 1) * 128], in_=yb[:, fb * 128:(fb + 1) * 128], identity=identb[:]), [byb, b_identb], [bpb2])
                ro, bro = ro_r.next()
                P.op("act", lambda e, ro=ro, pb2=pb2: e.activation(out=ro[:].rearrange("p a t -> p (a t)"), in_=pb2[:, 0:256], func=AF.Copy), [bpb2], [bro])
                P.dma("pool", lambda e, ro=ro, ch=ch: e.dma_start(out=mixT[b, 0:2, :, ch * 128:(ch + 1) * 128].rearrange("k p t -> p k t"), in_=ro[:]), [bro], [B_mixR[b][ch]])

    def phase_attn(l, b):
        AR.reset()
        o128 = AR.rot(6, [128, 128], F32)
        kTt = AR.alloc([128, 4, T], BF16); b_kTt = Buf()
        Vt = AR.alloc([128, NT, 520], BF16); b_Vt = Buf()
        qz = [AR.alloc([128, 4, 512], BF16) for _ in range(2)]
        bqg = Buf()
        for j in range(2):
            P.op("pool", lambda e, j=j: e.memset(qz[j], 0.0), [], [bqg])
        E_r = AR.rot(2, [128, 512], BF16)
        szq_r = AR.rot(1, [128, 4, 512], BF16)
        oall_r = AR.rot(1, [128, 4, 512], BF16)
        ast_r = AR.rot(2, [128, 4, 128], BF16)
        junk_r = AR.rot(1, [128, 128], BF16)
        P.op("pool", lambda e: e.memset(Vt.rearrange("p k (h e) -> p k h e", e=130)[:, :, :, 128:130], 1.0), [], [b_Vt])
        scoreR = Rot(psF.tiles[4:6], psF.bufs[4:6])
        oj0 = AR.alloc([128, 4, 132], F32); boj0 = Buf()
        P.dma("sp", lambda e: e.dma_start(out=kTt[:], in_=qkT[b, 1, :, :, :].rearrange("h p t -> p h t")), list(B_k[b]), [b_kTt])
        for kt in range(NT):
            P.dma("sp", lambda e, kt=kt: e.dma_start(out=Vt[:, kt, :].rearrange("p (h e) -> p h e", e=130)[:, :, 0:128], in_=avd[b, kt * 128:(kt + 1) * 128, :].rearrange("p (h e) -> p h e", e=128)), [B_av[b][kt]], [b_Vt])
        qgroups = [([2, 3, 4, 5], list(range(NT))), ([6, 7, 8, 9], list(range(NT))), ([10, 11, 12, 13], list(range(NT))), ([14, 15, 16, 17], list(range(NT)))]
        if l < NL - 1:
            qgroups = [([0, 1], [0, 1])] + qgroups
        for (qt, kts) in qgroups[:AT_NG]:
            nq = len(qt)
            ntok = nq * 128
            tok0 = qt[0] * 128
            for j in range(2):
                P.dma("sp", lambda e, j=j, tok0=tok0, ntok=ntok: e.dma_start(out=qz[j][j * 64:(j + 1) * 64, :, 0:ntok], in_=qkT[b, 0, :, j * 64:(j + 1) * 64, tok0:tok0 + ntok].rearrange("h p t -> p h t")), [B_q[b][t] for t in qt], [bqg])
            szq, bszq = szq_r.next()
            for qi, t in enumerate(qt):
                P.dma("sp", lambda e, szq=szq, qi=qi, t=t: e.dma_start(out=szq[:, qi, :], in_=aszd[b, t * 128:(t + 1) * 128, :]), [B_asz[b][t]], [bszq])
            oall, boall = oall_r.next()
            for h in range(4):
                for j in range(2):
                    acc = [(psF.tiles[i_], psF.bufs[i_]) for i_ in range(nq)]
                    pend = None

                    def pv(E, bE, kt, first, last):
                        for qi in range(nq):
                            ab, bab = acc[qi]
                            P.op("pe", lambda e, ab=ab, E=E, qi=qi, kt=kt, first=first, last=last: e.matmul(ab[:, 0:129], lhsT=E[:, qi * 128:(qi + 1) * 128], rhs=Vt[:, kt, h * 130:h * 130 + 129], start=first, stop=last), [bE, b_Vt], [bab])
                    for ki, kt in enumerate(kts):
                        ps, bps = scoreR.next()
                        P.op("pe", lambda e, ps=ps, kt=kt, j=j, ntok=ntok: e.matmul(ps[:, 0:ntok], lhsT=kTt[:, h, kt * 128:(kt + 1) * 128], rhs=qz[j][:, h, 0:ntok], start=True, stop=True), [b_kTt, bqg], [bps])
                        E, bE = E_r.next()
                        P.op("act", lambda e, E=E, ps=ps, ntok=ntok: e.activation(out=E[:, 0:ntok], in_=ps[:, 0:ntok], func=AF.Exp, scale=0.125), [bps], [bE])
                        if pend is not None:
                            pv(*pend)
                        pend = (E, bE, kt, ki == 0, ki == len(kts) - 1)
                    pv(*pend)
                    if j == 0:
                        for qi in range(nq):
                            P.op("dve", lambda e, qi=qi: e.tensor_copy(out=oj0[:, qi, 0:129], in_=acc[qi][0][:, 0:129]), [acc[qi][1]], [boj0])
                for qi in range(nq):
                    a1, ba1 = acc[qi]
                    sm, bsm = sm_r.next()
                    P.op("dve", lambda e, sm=sm, qi=qi: e.reciprocal(out=sm[:, 0:1], in_=oj0[:, qi, 128:129]), [boj0], [bsm])
                    P.op("dve", lambda e, sm=sm, a1=a1: e.reciprocal(out=sm[:, 1:2], in_=a1[:, 128:129]), [ba1], [bsm])
                    P.op("dve", lambda e, sm=sm: e.tensor_tensor(out=sm[:, 1:2], in0=sm[:, 1:2], in1=neglam[:, 0:1], op=ALU.mult), [bsm, b_neglam], [bsm])
                    o1, bo1 = o128.next()
                    P.op("dve", lambda e, o1=o1, a1=a1, sm=sm: e.tensor_scalar(out=o1, in0=a1[:, 0:128], scalar1=sm[:, 1:2], scalar2=None, op0=ALU.mult), [ba1, bsm], [bo1])
                    o0, bo0 = o128.next()
                    P.op("dve", lambda e, o0=o0, o1=o1, qi=qi, sm=sm: e.scalar_tensor_tensor(out=o0, in0=oj0[:, qi, 0:128], scalar=sm[:, 0:1], in1=o1, op0=ALU.mult, op1=ALU.add), [boj0, bsm, bo1], [bo0])
                    jk, bjk = junk_r.next()
                    P.op("act", lambda e, jk=jk, o0=o0, sm=sm: e.activation(out=jk[:, 0:128], in_=o0, func=AF.Square, scale=128.0 ** -0.5, accum_out=sm[:, 2:3]), [bo0], [bjk, bsm])
                    rstd_from_ms(sm[:, 2:3], bsm, 0)
                    o2, bo2 = o128.next()
                    P.op("dve", lambda e, o2=o2, o0=o0, sm=sm: e.scalar_tensor_tensor(out=o2, in0=o0, scalar=sm[:, 2:3], in1=gsub[:], op0=ALU.mult, op1=ALU.mult), [bo0, bsm, b_gsub], [bo2])
                    P.op("pool", lambda e, oall=oall, o2=o2, szq=szq, qi=qi: e.tensor_tensor(out=oall[:, qi, h * 128:(h + 1) * 128], in0=o2, in1=szq[:, qi, h * 128:(h + 1) * 128], op=ALU.mult), [bo2, bszq], [boall])
            for qi, t in enumerate(qt if AT_CUT >= 4 else []):
                pb, bpb = psB.next()
                for h in range(4):
                    P.op("pe", lambda e, pb=pb, oall=oall, qi=qi, h=h: e.transpose(out=pb[:, h * 128:(h + 1) * 128], in_=oall[:, qi, h * 128:(h + 1) * 128], identity=identb[:]), [boall, b_identb], [bpb])
                ast, bast = ast_r.next()
                P.op("act", lambda e, ast=ast, pb=pb: e.activation(out=ast[:].rearrange("p h t -> p (h t)"), in_=pb[:, 0:512], func=AF.Copy), [bpb], [bast])
                P.dma("pool", lambda e, ast=ast, t=t: e.dma_start(out=mixT[b, 4:8, :, t * 128:(t + 1) * 128].rearrange("k p t -> p k t"), in_=ast[:]), [bast], [B_mixA[b][t]])

    def phase_out(l, b):
        AR.reset()
        xt_r = AR.rot(2, [128, D], F32)
        xo_r = AR.rot(2, [128, D], F32)
        t512 = AR.rot(4, [128, 512], F32)
        mx_r = AR.rot(2, [128, 8, 128], BF16)
        junk_r = AR.rot(2, [128, 512], BF16)
        woutb = AR.alloc([128, 8, D], BF16); b_woutb = Buf()
        P.dma("sp", lambda e: e.dma_start(out=woutb, in_=woutbf[l, :, :, :]), [B_woutbf[l]], [b_woutb])
        xsrc, Bx = (xall, B_x[0]) if l == 0 else (x1, B_x[1])
        tiles = list(range(NT)) if l < NL - 1 else list(range(2, NT))
        for t in tiles:
            G = (bcC, b_bcC) if t < 2 else (bcB, b_bcB)
            mx, bmx = mx_r.next()
            P.dma("sp", lambda e, mx=mx, t=t: e.dma_start(out=mx[:], in_=mixT[b, :, :, t * 128:(t + 1) * 128].rearrange("k p t -> p k t")), [B_mixR[b][t], B_mixC[b][t], B_mixA[b][t]], [bmx])
            xt, bxt = xt_r.next()
            P.dma("sp", lambda e, xt=xt, t=t: e.dma_start(out=xt[:], in_=xsrc[b, t * 128:(t + 1) * 128, :]), [Bx[b][t]], [bxt])
            pss = []
            sm, bsm = sm_r.next()
            for hf in range(2):
                ps, bps = psF.next()
                pss.append((ps, bps))
                for k in range(8):
                    P.op("pe", lambda e, ps=ps, mx=mx, k=k, hf=hf: e.matmul(ps[:, :], lhsT=mx[:, k, :], rhs=woutb[:, k, hf * 512:(hf + 1) * 512], start=(k == 0), stop=(k == 7)), [bmx, b_woutb], [bps])
                jk, bjk = junk_r.next()
                P.op("act", lambda e, jk=jk, ps=ps, sm=sm, hf=hf: e.activation(out=jk[:, 0:512], in_=ps[:, :], func=AF.Square, scale=1.0 / 32.0, accum_out=sm[:, hf:hf + 1]), [bps], [bjk, bsm])
            P.op("dve", lambda e, sm=sm: e.tensor_tensor(out=sm[:, 2:3], in0=sm[:, 0:1], in1=sm[:, 1:2], op=ALU.add), [bsm], [bsm])
            rstd_from_ms(sm[:, 2:3], bsm, 0)
            xo, bxo = xo_r.next()
            for hf in range(2):
                ps, bps = pss[hf]
                tq, btq = t512.next()
                P.op("dve", lambda e, tq=tq, ps=ps, sm=sm, hf=hf, G=G: e.scalar_tensor_tensor(out=tq[:], in0=ps[:, :], scalar=sm[:, 2:3], in1=G[0][2][:, hf * 512:(hf + 1) * 512], op0=ALU.mult, op1=ALU.mult), [bps, bsm, G[1][2]], [btq])
                P.op("pool", lambda e, xo=xo, tq=tq, xt=xt, hf=hf: e.tensor_tensor(out=xo[:, hf * 512:(hf + 1) * 512], in0=tq[:], in1=xt[:, hf * 512:(hf + 1) * 512], op=ALU.add), [btq, bxt], [bxo])
            if l < NL - 1:
                P.dma("pool", lambda e, xo=xo, t=t: e.dma_start(out=x1[b, t * 128:(t + 1) * 128, :], in_=xo[:]), [bxo], [B_x[1][b][t]])
                if dbg and b == 0:
                    P.dma("pool", lambda e, xo=xo, t=t: e.dma_start(out=dbgd["d_x1"][t * 128:(t + 1) * 128, :], in_=xo[:]), [bxo], [B_x[2][b][t]])
            else:
                P.dma("pool", lambda e, xo=xo, t=t: e.dma_start(out=outd[b, (t - 2) * 128:(t - 1) * 128, :], in_=xo[:]), [bxo], [B_x[2][b][t]])

    for l in range(NL):
        if upto >= 1:
            layer_setup(l)
        for b in range(NB):
            if upto >= 2:
                bcast_rows(b, bcB, b_bcB)
                phase_a(l, b)
            if upto >= 3:
                phase_conv(l, b)
            if upto >= 4:
                phase_rwkv(l, b)
            if upto >= 5:
                phase_attn(l, b)
            if upto >= 6:
                phase_out(l, b)
            if dbg and l == 0 and b == 0:
                bd = Buf()
                P.dma("sp", lambda e: e.dma_start(out=dbgd["d_mix"][:, :, :], in_=mixT[0, :, :, :]), B_mixR[0] + B_mixC[0] + B_mixA[0], [bd])
                P.dma("sp", lambda e: e.dma_start(out=dbgd["d_rkvz"][:, :], in_=rkvz[0, :, :]), B_rkvz[0], [bd])
                P.dma("sp", lambda e: e.dma_start(out=dbgd["d_q"][:, :, :, :], in_=qkT[0, :, :, :, :]), B_q[0] + B_k[0], [bd])
    stats = P.finish()
    return nc, stats


_CACHE = {}


def _prep(inputs, core, NB=NBF):
    cp = _CACHE.setdefault("colperm", _colperm())
    cst = _CACHE.setdefault("consts", _consts())
    f = lambda a: np.ascontiguousarray(np.asarray(a, dtype=np.float32))
    bs = slice(core * NB, (core + 1) * NB)
    x = np.asarray(inputs["x"])[bs]
    ctx = np.asarray(inputs["ctx"])[bs]
    m = {}
    m["xall"] = f(np.concatenate([ctx, x], axis=1))
    c5 = np.concatenate([np.asarray(inputs["c"])[bs], np.asarray(inputs["c_ctx"])[None, :]], axis=0)
    if c5.shape[0] < 5:
        c5 = np.concatenate([c5, np.zeros((5 - c5.shape[0], D), np.float32)], 0)
        c5[4] = np.asarray(inputs["c_ctx"])
    m["cc"] = f(c5)
    sh = _CACHE.get("shared")
    if sh is None:
        sh = {}
        sh["wext"] = f(np.asarray(inputs["w_in"])[:, :, cp])
        sh["wout"] = f(inputs["w_out"])
        sh["modw"] = f(inputs["mod_w"])
        sh["modb"] = f(np.asarray(inputs["mod_b"])[:, None, :])
        sh["preg"] = f(np.asarray(inputs["norm_pre_g"])[:, None, :])
        sh["postg"] = f(np.asarray(inputs["norm_post_g"])[:, None, :])
        sh["w0"] = f(np.asarray(inputs["rwkv_w0"]).reshape(L_FULL, 1, 512))
        sh["a0"] = f(np.asarray(inputs["rwkv_a0"]).reshape(L_FULL, 1, 512))
        sh["wup"] = f(np.asarray(inputs["rwkv_w_up"]).reshape(L_FULL, 128, 256))
        sh["aup"] = f(np.asarray(inputs["rwkv_a_up"]).reshape(L_FULL, 128, 256))
        sh["vec256"] = f(np.stack([np.asarray(inputs["rwkv_k_k"]), np.asarray(inputs["rwkv_k_a"]),
                                   np.asarray(inputs["rwkv_r_k"]).reshape(L_FULL, 256), np.asarray(inputs["rwkv_ln_g"]),
                                   np.asarray(inputs["rwkv_ln_b"]), np.asarray(inputs["diff_lambda"]).reshape(L_FULL, 256)], axis=1))
        sh["convw"] = f(inputs["conv_w"])
        sh["subg"] = f(np.asarray(inputs["diff_subln_g"])[:, None, :])
        for k, v in cst.items():
            sh[k] = f(v)
        _CACHE["shared"] = sh
    m.update(sh)
    return m


def kernel(**inputs):
    _CACHE.pop("shared", None)
    nc, stats = build()
    in_maps = [_prep(inputs, core) for core in range(8)]
    res = run_bass_kernel_spmd(nc, in_maps, core_ids=list(range(8)))
    out = np.concatenate([np.asarray(r["out"]) for r in res.results], axis=0)
    return out.astype(np.float32)
```

```python
import contextlib
import math
import numpy as np
import concourse.bass as bass
import concourse.mybir as mybir
from concourse.bass_utils import run_bass_kernel_spmd

F32 = mybir.dt.float32
BF16 = mybir.dt.bfloat16
AF = mybir.ActivationFunctionType
ALU = mybir.AluOpType
AX = mybir.AxisListType

EPOCH = 20000
RW_CUT = 99
RW_SUB = 99
RW_NCH = 99
RW_ND = 2
RW_VAR = 0
AT_CUT = 99
AT_NG = 99
N_DMA_SEMS = 8


class Buf:
    __slots__ = ("w", "r", "name")

    def __init__(self, name=""):
        self.w = None
        self.r = []
        self.name = name


class _Cap:
    def __init__(self):
        self.call = None

    def __getattr__(self, name):
        def f(*a, **k):
            self.call = (name, a, k)
            return self
        return f


class Rec:
    __slots__ = ("eng", "fn", "deps", "raw", "sig", "needed", "is_dma", "pos")

    def __init__(self, eng, fn, is_dma):
        self.eng = eng
        self.fn = fn
        self.deps = set()
        self.raw = set()
        self.sig = None
        self.needed = False
        self.is_dma = is_dma
        self.pos = 0


class Prog:
    ENGS = ("sp", "act", "pool", "dve", "pe")

    def __init__(self, nc):
        self.nc = nc
        self.streams = {e: [] for e in self.ENGS}
        self.stack = contextlib.ExitStack()
        self.dma_sems = {}
        self.dma_rr = {}
        self.dma_last = {}
        self.dma_cnt = {}
        self.nsem = 0
        self.all_dma = []
        self.fence = []

    def barrier(self):
        fence = []
        for e in self.ENGS:
            for rec in reversed(self.streams[e]):
                if not rec.is_dma:
                    fence.append(rec)
                    break
        fence += list(self.dma_last.values())
        self.fence = fence

    def sem(self, name):
        self.nsem += 1
        return self.stack.enter_context(self.nc.semaphore(name))

    def sbuf(self, name, shape, dt):
        return self.stack.enter_context(self.nc.sbuf_tensor("s_" + name, list(shape), dt))

    def psum(self, name, shape, dt):
        return self.stack.enter_context(self.nc.psum_tensor("p_" + name, list(shape), dt))

    def _track(self, rec, reads, writes):
        for b in reads:
            if b.w is not None:
                rec.deps.add(b.w)
                rec.raw.add(b.w)
        for b in writes:
            if b.w is not None:
                rec.deps.add(b.w)
            for r in b.r:
                rec.deps.add(r)
        for b in reads:
            if not rec.is_dma:
                b.r = [r for r in b.r if r.is_dma or r.eng != rec.eng]
            b.r.append(rec)
        for b in writes:
            b.w = rec
            b.r = []
        for f in self.fence:
            rec.deps.add(f)
            rec.raw.add(f)
        rec.deps.discard(rec)
        rec.raw.discard(rec)

    def op(self, eng, fn, reads=(), writes=()):
        cap = _Cap()
        fn(cap)
        rec = Rec(eng, cap.call, False)
        self._track(rec, reads, writes)
        self.streams[eng].append(rec)
        return rec

    def dma(self, q, fn, reads=(), writes=()):
        cap = _Cap()
        fn(cap)
        rec = Rec(q, cap.call, True)
        self._track(rec, reads, writes)
        if q not in self.dma_sems:
            self.dma_sems[q] = [self.sem(f"dma_{q}_{i}") for i in range(N_DMA_SEMS)]
            self.dma_rr[q] = 0
        i = self.dma_rr[q]
        self.dma_rr[q] = (i + 1) % N_DMA_SEMS
        s = self.dma_sems[q][i]
        key = (q, i)
        prev = self.dma_last.get(key)
        if prev is not None:
            rec.deps.add(prev)
        self.dma_last[key] = rec
        self.dma_cnt[key] = self.dma_cnt.get(key, 0) + 1
        rec.sig = (s, 16 * self.dma_cnt[key])
        self.streams[q].append(rec)
        self.all_dma.append(rec)
        return rec

    @staticmethod
    def _skip(rec, d):
        if d.is_dma or rec.is_dma or d.eng != rec.eng:
            return False
        if rec.eng == "pe":
            return True
        return d not in rec.raw

    def finish(self):
        nc = self.nc
        for e in self.ENGS:
            for rec in self.streams[e]:
                for d in rec.deps:
                    if d.is_dma or self._skip(rec, d):
                        continue
                    d.needed = True
        for e in self.ENGS:
            cnt = 0
            sem = None
            for rec in self.streams[e]:
                if rec.is_dma or not rec.needed:
                    continue
                if sem is None or cnt >= EPOCH:
                    sem = self.sem(f"c_{e}_{self.nsem}")
                    cnt = 0
                cnt += 1
                rec.sig = (sem, cnt)
            self.sigcnt = getattr(self, "sigcnt", {})
            self.sigcnt[e] = cnt
        final_waits = {}
        for rec in self.all_dma:
            s, v = rec.sig
            final_waits[id(s)] = (s, max(v, final_waits.get(id(s), (s, 0))[1]))
        streams = self.streams
        skip = self._skip

        def emit(ename, eng):
            waited = {}
            for rec in streams[ename]:
                for d in rec.deps:
                    if skip(rec, d):
                        continue
                    s, v = d.sig
                    if waited.get(id(s), 0) < v:
                        eng.wait_ge(s, v)
                        waited[id(s)] = v
                name, a_, k_ = rec.fn
                ins = getattr(eng, name)(*a_, **k_)
                if rec.is_dma:
                    ins.then_inc(rec.sig[0], 16)
                elif rec.sig is not None:
                    ins.then_inc(rec.sig[0], 1)
            if ename == "sp":
                for s, v in final_waits.values():
                    eng.wait_ge(s, v)

        with nc.Block() as block:
            @block.sync
            def _(e):
                emit("sp", e)

            @block.scalar
            def _(e):
                emit("act", e)

            @block.gpsimd
            def _(e):
                emit("pool", e)

            @block.vector
            def _(e):
                emit("dve", e)

            @block.tensor
            def _(e):
                emit("pe", e)
        self.stack.close()
        return {e: (len(self.streams[e]), self.sigcnt.get(e)) for e in self.ENGS}


class Rot:
    def __init__(self, tiles, bufs=None):
        self.tiles = tiles
        self.bufs = bufs if bufs is not None else [Buf() for _ in tiles]
        self.i = 0

    def next(self):
        i = self.i
        self.i = (i + 1) % len(self.tiles)
        return self.tiles[i], self.bufs[i]


D = 1024
L_FULL = 2
NBF = 4
CTX = 256
SEQ = 2048
T = CTX + SEQ
NT = T // 128
WC = 5376
NFM = 26
DECAY_C = -math.exp(-0.5)
NORM_EPS = 1e-6
GN_EPS = 64e-5


def _colperm():
    cols = []
    cols += list(range(768, 896)) + list(range(896, 1024))
    cols += list(range(1536, 1792)) + list(range(1792, 2048)) + list(range(2048, 2304)) + list(range(1280, 1536))

    def rot_src(base):
        out = []
        for jd in range(128):
            j, dd = divmod(jd, 64)
            g, i = divmod(dd, 32)
            out.append(base + j * 64 + g * 32 + (i + 16 if i < 16 else i - 16))
        return out

    for s0 in (2304, 2816):
        for h in range(4):
            base = s0 + h * 128
            cols += list(range(base, base + 128)) + rot_src(base)
    cols += list(range(0, 768)) + list(range(1024, 1280))
    cols += list(range(3328, 3840)) + list(range(3840, 4352))
    assert len(cols) == WC
    return np.array(cols)


def _consts():
    p = np.arange(128)[:, None]
    f = np.arange(128)[None, :]
    LE, GE, LT, GT = (p <= f), (p >= f), (p < f), (p > f)
    c = {}
    c["ident"] = np.eye(128, dtype=np.float32)
    c["cm"] = (np.stack([LE, GE, LT, GT], 1).astype(np.float32) * DECAY_C).astype(np.float32)
    m4 = np.zeros((128, 2, 512), np.float32)
    m4[:, 0] = np.concatenate([LT, LE, LT, LE], 1)
    m4[:, 1] = np.concatenate([GT, GE, GT, GE], 1)
    c["mask4"] = m4
    mL = np.zeros((128, 2, 512), np.float32)
    mL[:, 0] = np.concatenate([GT] * 4, 1)
    mL[:, 1] = np.concatenate([LT] * 4, 1)
    c["maskL"] = mL
    pos = np.arange(SEQ)
    row = (pos // 64).astype(np.float32)
    col = (pos % 64).astype(np.float32)
    inv = (10000.0 ** (-np.arange(0, 32, 2, dtype=np.float32) / 32)).astype(np.float32)
    cosT = np.ones((128, T), np.float32)
    sinT = np.zeros((128, T), np.float32)
    for pp in range(128):
        dd = pp % 64
        g, i = divmod(dd, 32)
        ang = (row if g == 0 else col) * inv[i % 16]
        cosT[pp, CTX:] = np.cos(ang)
        sinT[pp, CTX:] = np.sin(ang) * (-1.0 if i < 16 else 1.0)
    c["cosT"] = cosT
    c["sinT"] = sinT
    sel = np.zeros((5, 5, 128), np.float32)
    for b in range(5):
        sel[b, b, :] = 1.0
    c["sel"] = sel
    return c


def build(NB=NBF, NL=L_FULL, dbg=False, upto=9):
    nc = bass.Bass("TRN2", target_bir_lowering=False)
    P = Prog(nc)

    def din(name, shape, dt=F32):
        return nc.dram_tensor(name, list(shape), dt, kind="ExternalInput").ap()

    def dscr(name, shape, dt):
        return nc.dram_tensor(name, list(shape), dt, kind="Internal").ap()

    xall = din("xall", [NB, T, D])
    cc = din("cc", [5, D])
    wext = din("wext", [L_FULL, D, WC])
    woutd = din("wout", [L_FULL, D, D])
    modw = din("modw", [L_FULL, D, 3 * D])
    modb = din("modb", [L_FULL, 1, 3 * D])
    pregd = din("preg", [L_FULL, 1, D])
    postgd = din("postg", [L_FULL, 1, D])
    w0d = din("w0", [L_FULL, 1, 512])
    a0d = din("a0", [L_FULL, 1, 512])
    wupd = din("wup", [L_FULL, 128, 256])
    aupd = din("aup", [L_FULL, 128, 256])
    vec256 = din("vec256", [L_FULL, 6, 256])
    convwd = din("convw", [L_FULL, 3, 256])
    subgd = din("subg", [L_FULL, 1, 128])
    identd = din("ident", [128, 128])
    cmd = din("cm", [128, 4, 128])
    mask4d = din("mask4", [128, 2, 512])
    maskLd = din("maskL", [128, 2, 512])
    cosd = din("cosT", [128, T])
    sind = din("sinT", [128, T])
    seld = din("sel", [5, 5, 128])
    outd = nc.dram_tensor("out", [NB, SEQ, D], F32, kind="ExternalOutput").ap()
    dbgd = {}
    if dbg:
        for nm, shp, dt_ in (("d_rkvz", [T, 1024], BF16), ("d_mix", [8, 128, T], BF16), ("d_x1", [T, D], F32), ("d_q", [2, 4, 128, T], BF16)):
            dbgd[nm] = nc.dram_tensor(nm, shp, dt_, kind="ExternalOutput").ap()

    x1 = dscr("x1", [NB, T, D], F32)
    wbf = dscr("wbf", [L_FULL, 11, 128, 8, 512], BF16)
    rkvz = dscr("rkvz", [NB, T, 1024], BF16)
    avd = dscr("av", [NB, T, 512], BF16)
    aszd = dscr("asz", [NB, T, 512], BF16)
    qkT = dscr("qkT", [NB, 2, 4, 128, T], BF16)
    mixT = dscr("mixT", [NB, 8, 128, T], BF16)
    woutbf = dscr("woutbf", [L_FULL, 128, 8, D], BF16)
    B_woutbf = [Buf() for _ in range(L_FULL)]

    def bufs(n):
        return [Buf() for _ in range(n)]

    B_x = {0: [bufs(NT) for _ in range(NB)], 1: [bufs(NT) for _ in range(NB)], 2: [bufs(NT) for _ in range(NB)]}
    B_wbf = [bufs(11) for _ in range(L_FULL)]
    B_rkvz = [bufs(NT) for _ in range(NB)]
    B_av = [bufs(NT) for _ in range(NB)]
    B_asz = [bufs(NT) for _ in range(NB)]
    B_q = [bufs(NT) for _ in range(NB)]
    B_k = [bufs(NT) for _ in range(NB)]
    B_mixR = [bufs(NT) for _ in range(NB)]
    B_mixC = [bufs(NT) for _ in range(NB)]
    B_mixA = [bufs(NT) for _ in range(NB)]

    psAll = [P.psum(f"psF{i}", [128, 512], F32) for i in range(8)]
    psBufs = [Buf() for _ in range(8)]
    psF = Rot(psAll[0:6], psBufs[0:6])
    psB = Rot([t[:, :].bitcast(BF16) for t in psAll[6:8]], psBufs[6:8])

    def ctile(name, shape, dt=F32):
        return P.sbuf(name, shape, dt), Buf(name)

    identf, b_identf = ctile("identf", [128, 128])
    identb, b_identb = ctile("identb", [128, 128], BF16)
    cm, b_cm = ctile("cm", [128, 4, 128])
    mask4, b_mask4 = ctile("mask4", [128, 2, 512], BF16)
    maskL, b_maskL = ctile("maskL", [128, 2, 512], BF16)
    cosT, b_cos = ctile("cosT", [128, T], BF16)
    sinT, b_sin = ctile("sinT", [128, T], BF16)
    arF = P.sbuf("arF", [128, 8192], F32)
    arB = P.sbuf("arB", [128, 29696], BF16)

    class Arena:
        def __init__(self):
            self.o = {F32: 0, BF16: 0}

        def reset(self):
            self.o = {F32: 0, BF16: 0}
            P.barrier()

        def alloc(self, shape, dt):
            n = 1
            for x in shape[1:]:
                n *= x
            ar = arF if dt == F32 else arB
            o = self.o[dt]
            assert o + n <= (8192 if dt == F32 else 29696), (shape, dt, o)
            self.o[dt] = o + n
            v = ar[0:shape[0], o:o + n]
            if len(shape) == 3:
                v = v.rearrange("p (a b) -> p a b", a=shape[1])
            elif len(shape) == 4:
                v = v.rearrange("p (a b c) -> p a b c", a=shape[1], b=shape[2])
            return v

        def rot(self, n, shape, dt):
            return Rot([self.alloc(shape, dt) for _ in range(n)])

    AR = Arena()
    sel, b_sel = ctile("sel", [5, 5, 128])
    negcol, b_negcol = ctile("negcol", [128, 1])
    onesrow, b_ones = ctile("onesrow", [1, 128])
    epsc, b_eps = ctile("epsc", [128, 2])
    P.dma("sp", lambda e: e.dma_start(out=identf[:], in_=identd[:, :]), [], [b_identf])
    P.dma("sp", lambda e: e.dma_start(out=cm[:], in_=cmd[:, :, :]), [], [b_cm])
    for (dst, bdst, src, n) in ((mask4, b_mask4, mask4d.rearrange("p a b -> p (a b)"), 1024), (maskL, b_maskL, maskLd.rearrange("p a b -> p (a b)"), 1024),
                                (cosT, b_cos, cosd, T), (sinT, b_sin, sind, T)):
        AR.reset()
        tmpc = AR.alloc([128, n], F32)
        btmp = Buf()
        P.dma("sp", lambda e, tmpc=tmpc, src=src: e.dma_start(out=tmpc, in_=src), [], [btmp])
        dv = dst[:].rearrange("p a b -> p (a b)") if n == 1024 else dst[:]
        P.op("dve", lambda e, dv=dv, tmpc=tmpc: e.tensor_copy(out=dv, in_=tmpc), [btmp], [bdst])
    P.dma("sp", lambda e: e.dma_start(out=sel[:], in_=seld[:, :, :]), [], [b_sel])
    P.op("dve", lambda e: e.tensor_copy(out=identb[:], in_=identf[:]), [b_identf], [b_identb])
    P.op("pool", lambda e: e.memset(negcol[:], DECAY_C), [], [b_negcol])
    P.op("pool", lambda e: e.memset(onesrow[:], 1.0), [], [b_ones])
    P.op("pool", lambda e: e.memset(epsc[:, 0:1], NORM_EPS), [], [b_eps])
    P.op("pool", lambda e: e.memset(epsc[:, 1:2], GN_EPS), [], [b_eps])

    AR.reset()
    wstg = AR.rot(2, [128, 8, 256], F32)
    wcast = AR.rot(2, [128, 8, 256], BF16)
    cast_eng = ["dve", "pool", "act"]
    ci = 0
    for l in range(NL):
        for sb in range(11):
            wfull = 512 if sb != 6 else 256
            c0 = sb * 512 if sb < 6 else (3072 if sb == 6 else 3328 + (sb - 7) * 512)
            for hf in range(wfull // 256):
                st, bst = wstg.next()
                cb, bcb = wcast.next()
                P.dma("sp", lambda e, st=st, l=l, c0=c0, hf=hf: e.dma_start(
                    out=st, in_=wext[l, :, c0 + hf * 256:c0 + (hf + 1) * 256].rearrange("(c p) n -> p c n", p=128)), [], [bst])
                eng = cast_eng[ci % 3]
                ci += 1
                if eng == "act":
                    P.op("act", lambda e, st=st, cb=cb: e.activation(out=cb, in_=st, func=AF.Copy), [bst], [bcb])
                else:
                    P.op(eng, lambda e, st=st, cb=cb: e.tensor_copy(out=cb, in_=st), [bst], [bcb])
                P.dma("pool", lambda e, cb=cb, l=l, sb=sb, hf=hf: e.dma_start(out=wbf[l, sb, :, :, hf * 256:(hf + 1) * 256], in_=cb), [bcb], [B_wbf[l][sb]])

    srow = P.sbuf("srow", [5, D], F32); b_srow = Buf()
    arow = P.sbuf("arow", [5, D], F32); b_arow = Buf()
    grow = P.sbuf("grow", [5, D], F32); b_grow = Buf()
    bcC = [P.sbuf(f"bcC{i}", [128, D], F32) for i in range(3)]; b_bcC = bufs(3)
    bcB = [P.sbuf(f"bcB{i}", [128, D], F32) for i in range(3)]; b_bcB = bufs(3)
    v256 = P.sbuf("v256", [128, 6, 256], F32); b_v256 = Buf()
    wup = P.sbuf("wup", [128, 2, 256], F32); b_wup = Buf()
    aup = P.sbuf("aup", [128, 2, 256], F32); b_aup = Buf()
    w0r = P.sbuf("w0r", [1, 512], F32); b_w0r = Buf()
    a0r = P.sbuf("a0r", [1, 512], F32); b_a0r = Buf()
    cwc = P.sbuf("cwc", [128, 2, 3], F32); b_cwc = Buf()
    gsub = P.sbuf("gsub", [128, 128], F32); b_gsub = Buf()
    neglam = P.sbuf("neglam", [128, 1], F32); b_neglam = Buf()
    lamt = P.sbuf("lamt", [128, 132], F32); b_lamt = Buf()
    ures = P.sbuf("ures", [128, 2, T], BF16); b_ures = Buf()
    bzres = P.sbuf("bzres", [128, 2, T], BF16); b_bzres = Buf()
    lwla = P.sbuf("lwla", [128, 2, T], F32); b_lwla = Buf()
    sm_r = Rot([P.sbuf(f"sm{i}", [128, 8], F32) for i in range(12)])

    LAM_INIT = [0.8 - 0.6 * math.exp(-0.3 * l) for l in range(L_FULL)]

    def rstd_from_ms(ms, bms, eps_col, n=1):
        P.op("act", lambda e: e.activation(out=ms, in_=ms, func=AF.Ln, bias=epsc[:, eps_col:eps_col + 1], scale=1.0), [bms, b_eps], [bms])
        P.op("act", lambda e: e.activation(out=ms, in_=ms, func=AF.Exp, scale=-0.5), [bms], [bms])

    def layer_setup(l):
        AR.reset()
        scT = AR.alloc([128, 8, 5], F32); b_scT = Buf()
        modb5 = AR.rot(2, [5, 512], F32)
        preg5 = AR.alloc([5, D], F32); b_preg5 = Buf()
        postg5 = AR.alloc([5, D], F32); b_postg5 = Buf()
        wstg = AR.rot(2, [128, 8, 256], F32)
        for c in range(8):
            P.dma("sp", lambda e, c=c: e.dma_start(out=scT[:, c, :], in_=cc[:, c * 128:(c + 1) * 128].rearrange("b p -> p b"), allow_slow_non_contiguous=True), [], [b_scT])
        P.op("act", lambda e: e.activation(out=scT, in_=scT, func=AF.Silu), [b_scT], [b_scT])
        P.dma("sp", lambda e: e.dma_start(out=preg5, in_=pregd[l, 0:1, :].broadcast_to([5, D])), [], [b_preg5])
        P.dma("sp", lambda e: e.dma_start(out=postg5, in_=postgd[l, 0:1, :].broadcast_to([5, D])), [], [b_postg5])
        dsts = [(srow, b_srow), (arow, b_arow), (grow, b_grow)]
        for cb in range(6):
            mb, bmb = modb5.next()
            P.dma("sp", lambda e, mb=mb, cb=cb: e.dma_start(out=mb, in_=modb[l, 0:1, cb * 512:(cb + 1) * 512].broadcast_to([5, 512])), [], [bmb])
            ps, bps = psF.next()
            for hf in range(2):
                st, bst = wstg.next()
                P.dma("sp", lambda e, st=st, cb=cb, hf=hf: e.dma_start(
                    out=st, in_=modw[l, :, cb * 512 + hf * 256:cb * 512 + (hf + 1) * 256].rearrange("(c p) n -> p c n", p=128)), [], [bst])
                for c in range(8):
                    P.op("pe", lambda e, ps=ps, st=st, c=c, hf=hf: e.matmul(ps[0:5, hf * 256:(hf + 1) * 256], lhsT=scT[:, c, :], rhs=st[:, c, :], start=(c == 0), stop=(c == 7)), [b_scT, bst], [bps])
            dst, bd = dsts[cb // 2]
            P.op("dve", lambda e, ps=ps, cb=cb, dst=dst, mb=mb: e.tensor_tensor(out=dst[:, (cb % 2) * 512:(cb % 2 + 1) * 512], in0=ps[0:5, :], in1=mb, op=ALU.add), [bps, bmb], [bd])
        P.op("dve", lambda e: e.scalar_tensor_tensor(out=arow[:], in0=arow[:], scalar=1.0, in1=preg5, op0=ALU.add, op1=ALU.mult), [b_arow, b_preg5], [b_arow])
        P.op("dve", lambda e: e.tensor_tensor(out=grow[:], in0=grow[:], in1=postg5, op=ALU.mult), [b_grow, b_postg5], [b_grow])
        wcs_r = AR.rot(2, [128, 8, 256], BF16)
        for q4 in range(4):
            st, bst = wstg.next()
            cb, bcb = wcs_r.next()
            P.dma("sp", lambda e, st=st, q4=q4: e.dma_start(out=st, in_=woutd[l, :, q4 * 256:(q4 + 1) * 256].rearrange("(c p) n -> p c n", p=128)), [], [bst])
            P.op("pool", lambda e, st=st, cb=cb: e.tensor_copy(out=cb, in_=st), [bst], [bcb])
            P.dma("pool", lambda e, cb=cb, q4=q4: e.dma_start(out=woutbf[l, :, :, q4 * 256:(q4 + 1) * 256], in_=cb), [bcb], [B_woutbf[l]])
        P.dma("sp", lambda e: e.dma_start(out=v256[:], in_=vec256[l:l + 1, :, :].broadcast_to([128, 6, 256])), [], [b_v256])
        P.op("pool", lambda e: e.memset(wup[:], 0.0), [], [b_wup])
        P.op("pool", lambda e: e.memset(aup[:], 0.0), [], [b_aup])
        for dd in range(2):
            P.dma("sp", lambda e, dd=dd: e.dma_start(out=wup[dd * 64:(dd + 1) * 64, dd, :], in_=wupd[l, dd * 64:(dd + 1) * 64, :]), [], [b_wup])
            P.dma("sp", lambda e, dd=dd: e.dma_start(out=aup[dd * 64:(dd + 1) * 64, dd, :], in_=aupd[l, dd * 64:(dd + 1) * 64, :]), [], [b_aup])
        P.dma("sp", lambda e: e.dma_start(out=w0r[:], in_=w0d[l, 0:1, :]), [], [b_w0r])
        P.dma("sp", lambda e: e.dma_start(out=a0r[:], in_=a0d[l, 0:1, :]), [], [b_a0r])
        for fb in range(2):
            P.dma("sp", lambda e, fb=fb: e.dma_start(out=cwc[:, fb, :], in_=convwd[l, :, fb * 128:(fb + 1) * 128].rearrange("j p -> p j"), allow_slow_non_contiguous=True), [], [b_cwc])
        P.dma("sp", lambda e: e.dma_start(out=gsub[:], in_=subgd[l, 0:1, :].broadcast_to([128, 128])), [], [b_gsub])
        P.op("dve", lambda e: e.tensor_scalar_mul(out=gsub[:], in0=gsub[:], scalar1=1.0 - LAM_INIT[l]), [b_gsub], [b_gsub])
        P.op("dve", lambda e: e.tensor_tensor(out=lamt[:, 0:64], in0=v256[:, 5, 0:64], in1=v256[:, 5, 64:128], op=ALU.mult), [b_v256], [b_lamt])
        P.op("dve", lambda e: e.tensor_tensor(out=lamt[:, 64:128], in0=v256[:, 5, 128:192], in1=v256[:, 5, 192:256], op=ALU.mult), [b_v256], [b_lamt])
        P.op("dve", lambda e: e.reduce_sum(out=lamt[:, 128:130], in_=lamt[:, 0:128].rearrange("p (a b) -> p a b", a=2), axis=AX.X), [b_lamt], [b_lamt])
        P.op("act", lambda e: e.activation(out=lamt[:, 130:132], in_=lamt[:, 128:130], func=AF.Exp), [b_lamt], [b_lamt])
        P.op("dve", lambda e: e.tensor_tensor(out=neglam[:], in0=lamt[:, 131:132], in1=lamt[:, 130:131], op=ALU.subtract), [b_lamt], [b_neglam])
        P.op("dve", lambda e: e.tensor_scalar_add(out=neglam[:], in0=neglam[:], scalar1=-LAM_INIT[l]), [b_neglam], [b_neglam])
        bcast_rows(4, bcC, b_bcC)

    def bcast_rows(row, tiles, tb):
        srcs = [(arow, b_arow, None), (srow, b_srow, None), (grow, b_grow, None)]
        for qi, (src, bsrc, _) in enumerate(srcs):
            for hf in range(2):
                ps, bps = psF.next()
                P.op("pe", lambda e, ps=ps, src=src, hf=hf: e.matmul(ps[:, :], lhsT=sel[0:5, row, :], rhs=src[0:5, hf * 512:(hf + 1) * 512], start=True, stop=True), [b_sel, bsrc], [bps])
                P.op("act", lambda e, ps=ps, qi=qi, hf=hf: e.activation(out=tiles[qi][:, hf * 512:(hf + 1) * 512], in_=ps[:, :], func=AF.Copy), [bps], [tb[qi]])

    def phase_a(l, b):
        xsrc, Bx = (xall, B_x[0]) if l == 0 else (x1, B_x[1])
        AR.reset()
        hT = AR.alloc([128, 8, 1152], BF16); b_hT = Buf()
        xt_r = AR.rot(2, [128, D], F32)
        f32a = AR.rot(2, [128, D], F32)
        t512 = AR.rot(4, [128, 512], F32)
        hb_r = AR.rot(2, [128, D], BF16)
        junk_r = AR.rot(2, [128, D], BF16)
        wblk_r = AR.rot(2, [128, 8, 512], BF16)
        stg_r = AR.rot(3, [128, 1024], BF16)
        for part in range(2):
            t0 = part * 9
            for ti in range(9):
                t = t0 + ti
                A, S = (bcC, b_bcC) if t < 2 else (bcB, b_bcB)
                xt, bxt = xt_r.next()
                P.dma("sp", lambda e, xt=xt, t=t: e.dma_start(out=xt[:], in_=xsrc[b, t * 128:(t + 1) * 128, :]), [Bx[b][t]], [bxt])
                jk, bjk = junk_r.next()
                sm, bsm = sm_r.next()
                P.op("act", lambda e, xt=xt, jk=jk, sm=sm: e.activation(out=jk[:], in_=xt[:], func=AF.Square, scale=1.0 / 32.0, accum_out=sm[:, 0:1]), [bxt], [bjk, bsm])
                rstd_from_ms(sm[:, 0:1], bsm, 0)
                fa, bfa = f32a.next()
                P.op("dve", lambda e, fa=fa, xt=xt, sm=sm, A=A: e.scalar_tensor_tensor(out=fa[:], in0=xt[:], scalar=sm[:, 0:1], in1=A[0][:], op0=ALU.mult, op1=ALU.mult), [bxt, bsm, S[0]], [bfa])
                hb, bhb = hb_r.next()
                P.op("pool", lambda e, hb=hb, fa=fa, A=A: e.tensor_tensor(out=hb[:], in0=fa[:], in1=A[1][:], op=ALU.add), [bfa, S[1]], [bhb])
                pb, bpb = psB.next()
                for c in range(8):
                    P.op("pe", lambda e, pb=pb, hb=hb, c=c: e.transpose(out=pb[:, c * 128:(c + 1) * 128], in_=hb[:, c * 128:(c + 1) * 128], identity=identb[:]), [bhb, b_identb], [bpb])
                P.op("act", lambda e, pb=pb, ti=ti: e.activation(out=hT[:, :, ti * 128:(ti + 1) * 128], in_=pb[:, :].rearrange("p (c t) -> p c t", c=8), func=AF.Copy), [bpb], [b_hT])
            groups = [(0, 4), (4, 4), (8, 1)]
            for sb in range(11):
                wb, bwb = wblk_r.next()
                w = 512 if sb != 6 else 256
                P.dma("sp", lambda e, wb=wb, sb=sb, w=w: e.dma_start(out=wb[:, :, 0:w], in_=wbf[l, sb, :, :, 0:w]), [B_wbf[l][sb]], [bwb])
                if sb < 7:
                    nblk = 4 if sb < 6 else 2
                    k = 0
                    while k < nblk:
                        fm = sb * 4 + k
                        pair = fm >= 10
                        for (g0, gn) in groups:
                            ntok = gn * 128
                            tok0 = (t0 + g0) * 128
                            loc0 = g0 * 128
                            tiles_g = list(range(t0 + g0, t0 + g0 + gn))
                            ps, bps = psF.next()
                            for c in range(8):
                                P.op("pe", lambda e, ps=ps, wb=wb, c=c, k=k, loc0=loc0, ntok=ntok: e.matmul(ps[:, 0:ntok], lhsT=wb[:, c, k * 128:(k + 1) * 128], rhs=hT[:, c, loc0:loc0 + ntok], start=(c == 0), stop=(c == 7)), [bwb, b_hT], [bps])
                            if pair:
                                ps2, bps2 = psF.next()
                                for c in range(8):
                                    P.op("pe", lambda e, ps2=ps2, wb=wb, c=c, k=k, loc0=loc0, ntok=ntok: e.matmul(ps2[:, 0:ntok], lhsT=wb[:, c, (k + 1) * 128:(k + 2) * 128], rhs=hT[:, c, loc0:loc0 + ntok], start=(c == 0), stop=(c == 7)), [bwb, b_hT], [bps2])
                                ta, bta = t512.next()
                                tb_, btb = t512.next()
                                P.op("dve", lambda e, ta=ta, ps=ps, tok0=tok0, ntok=ntok: e.tensor_tensor(out=ta[:, 0:ntok], in0=ps[:, 0:ntok], in1=cosT[:, tok0:tok0 + ntok], op=ALU.mult), [bps, b_cos], [bta])
                                P.op("dve", lambda e, tb_=tb_, ps2=ps2, tok0=tok0, ntok=ntok: e.tensor_tensor(out=tb_[:, 0:ntok], in0=ps2[:, 0:ntok], in1=sinT[:, tok0:tok0 + ntok], op=ALU.mult), [bps2, b_sin], [btb])
                                sg, bsg = stg_r.next()
                                P.op("pool", lambda e, sg=sg, ta=ta, tb_=tb_, ntok=ntok: e.tensor_tensor(out=sg[:, 0:ntok], in0=ta[:, 0:ntok], in1=tb_[:, 0:ntok], op=ALU.add), [bta, btb], [bsg])
                                qk = 0 if fm < 18 else 1
                                hh = ((fm - 10) // 2) % 4
                                Bq = (B_q if qk == 0 else B_k)[b]
                                P.dma("pool", lambda e, sg=sg, qk=qk, hh=hh, tok0=tok0, ntok=ntok: e.dma_start(out=qkT[b, qk, hh, :, tok0:tok0 + ntok], in_=sg[:, 0:ntok]), [bsg], [Bq[t] for t in tiles_g])
                            elif fm == 0:
                                P.op("act", lambda e, ps=ps, tok0=tok0, ntok=ntok: e.activation(out=lwla[:, 0, tok0:tok0 + ntok], in_=ps[:, 0:ntok], func=AF.Tanh), [bps], [b_lwla])
                            elif fm == 1:
                                P.op("act", lambda e, ps=ps, tok0=tok0, ntok=ntok: e.activation(out=lwla[:, 1, tok0:tok0 + ntok], in_=ps[:, 0:ntok], func=AF.Copy), [bps], [b_lwla])
                            elif fm in (2, 3):
                                P.op("act", lambda e, ps=ps, fm=fm, tok0=tok0, ntok=ntok: e.activation(out=ures[:, fm - 2, tok0:tok0 + ntok], in_=ps[:, 0:ntok], func=AF.Copy), [bps], [b_ures])
                            elif fm in (4, 5):
                                P.op("dve", lambda e, ps=ps, fm=fm, tok0=tok0, ntok=ntok: e.tensor_tensor(out=ures[:, fm - 4, tok0:tok0 + ntok], in0=ps[:, 0:ntok], in1=ures[:, fm - 4, tok0:tok0 + ntok], op=ALU.mult), [bps, b_ures], [b_ures])
                            elif fm in (6, 7):
                                P.op("act", lambda e, ps=ps, fm=fm, tok0=tok0, ntok=ntok: e.activation(out=bzres[:, fm - 6, tok0:tok0 + ntok], in_=ps[:, 0:ntok], func=AF.Silu), [bps], [b_bzres])
                            elif fm in (8, 9):
                                P.op("dve", lambda e, ps=ps, fm=fm, tok0=tok0, ntok=ntok: e.tensor_tensor(out=bzres[:, fm - 8, tok0:tok0 + ntok], in0=ps[:, 0:ntok], in1=bzres[:, fm - 8, tok0:tok0 + ntok], op=ALU.mult), [bps, b_bzres], [b_bzres])
                        k += 2 if pair else 1
                else:
                    tmb = sb - 7
                    for ti in range(9):
                        t = t0 + ti
                        ps, bps = psF.next()
                        for c in range(8):
                            P.op("pe", lambda e, ps=ps, wb=wb, c=c, ti=ti: e.matmul(ps[:, :], lhsT=hT[:, c, ti * 128:(ti + 1) * 128], rhs=wb[:, c, :], start=(c == 0), stop=(c == 7)), [bwb, b_hT], [bps])
                        sg, bsg = stg_r.next()
                        if tmb == 3:
                            P.op("act", lambda e, sg=sg, ps=ps: e.activation(out=sg[:, 0:512], in_=ps[:, :], func=AF.Silu), [bps], [bsg])
                            P.dma("pool", lambda e, sg=sg, t=t: e.dma_start(out=aszd[b, t * 128:(t + 1) * 128, :], in_=sg[:, 0:512]), [bsg], [B_asz[b][t]])
                        elif tmb == 2:
                            P.op("dve", lambda e, sg=sg, ps=ps: e.tensor_copy(out=sg[:, 0:512], in_=ps[:, :]), [bps], [bsg])
                            P.dma("pool", lambda e, sg=sg, t=t: e.dma_start(out=avd[b, t * 128:(t + 1) * 128, :], in_=sg[:, 0:512]), [bsg], [B_av[b][t]])
                        else:
                            if tmb == 0:
                                P.op("act", lambda e, sg=sg, ps=ps: e.activation(out=sg[:, 0:512], in_=ps[:, :], func=AF.Copy), [bps], [bsg])
                            else:
                                P.op("dve", lambda e, sg=sg, ps=ps: e.tensor_copy(out=sg[:, 0:512], in_=ps[:, :]), [bps], [bsg])
                            P.dma("pool", lambda e, sg=sg, t=t, tmb=tmb: e.dma_start(out=rkvz[b, t * 128:(t + 1) * 128, tmb * 512:(tmb + 1) * 512], in_=sg[:, 0:512]), [bsg], [B_rkvz[b][t]])

    def phase_conv(l, b):
        AR.reset()
        cvt = AR.alloc([128, SEQ], F32); b_cvt = Buf()
        cvs = AR.alloc([128, SEQ], BF16); b_cvs = Buf()
        ranges = ([(0, CTX)] if l == 0 else []) + [(CTX, T)]
        for (r0, r1) in ranges:
            n = r1 - r0
            for fb in range(2):
                P.op("pool", lambda e, fb=fb, r0=r0, n=n: e.tensor_scalar(out=cvt[:, 0:n], in0=ures[:, fb, r0:r0 + n], scalar1=cwc[:, fb, 1:2], scalar2=None, op0=ALU.mult), [b_ures, b_cwc], [b_cvt])
                P.op("dve", lambda e, fb=fb, r0=r0, n=n: e.scalar_tensor_tensor(out=cvt[:, 1:n], in0=ures[:, fb, r0:r0 + n - 1], scalar=cwc[:, fb, 0:1], in1=cvt[:, 1:n], op0=ALU.mult, op1=ALU.add), [b_ures, b_cwc, b_cvt], [b_cvt])
                P.op("dve", lambda e, fb=fb, r0=r0, n=n: e.scalar_tensor_tensor(out=cvt[:, 0:n - 1], in0=ures[:, fb, r0 + 1:r0 + n], scalar=cwc[:, fb, 2:3], in1=cvt[:, 0:n - 1], op0=ALU.mult, op1=ALU.add), [b_ures, b_cwc, b_cvt], [b_cvt])
                P.op("pool", lambda e, fb=fb, r0=r0, n=n: e.tensor_tensor(out=cvs[:, 0:n], in0=cvt[:, 0:n], in1=bzres[:, fb, r0:r0 + n], op=ALU.mult), [b_cvt, b_bzres], [b_cvs])
                P.dma("pool", lambda e, fb=fb, r0=r0, n=n: e.dma_start(out=mixT[b, 2 + fb, :, r0:r0 + n], in_=cvs[:, 0:n]), [b_cvs], [B_mixC[b][t] for t in range(r0 // 128, r1 // 128)])

    def bc4(ap4):
        return ap4.unsqueeze(2).to_broadcast([128, 4, 64])

    def v3(ap):
        return ap.rearrange("p (h e) -> p h e", h=4)

    def phase_rwkv(l, b):
        AR.reset()
        Yf = AR.alloc([128, NT, 256], F32); b_Yf = bufs(NT)
        f256 = AR.rot(10, [128, 256], F32)
        e12_r = AR.rot(2, [128, 512], F32)
        Ksum = AR.alloc([128, NT, 256], BF16); b_Ksum = bufs(NT)
        rk_r = AR.rot(2, [128, 1024], BF16)
        b256 = AR.rot(14, [128, 256], BF16)
        FMz_r = [AR.rot(2, [128, 8, 128], BF16) for _ in range(2)]
        for par in range(2):
            for (tl, tb_) in zip(FMz_r[par].tiles, FMz_r[par].bufs):
                P.op("pool", lambda e, tl=tl: e.memset(tl, 0.0), [], [tb_])
        XT_r = AR.rot(2, [128, 4, 512], BF16)
        Lp_r = AR.rot(3, [128, 4, 128], BF16)
        LpT_r = AR.rot(3, [128, 4, 128], BF16)
        Z_r = AR.rot(2, [128, 4, 128], BF16)
        PhiT_r = AR.rot(2, [64, 4, 64], BF16)
        RhT_r = AR.rot(2, [64, 4, 128], BF16)
        H_r = AR.rot(2, [64, 4, 64], BF16)
        ro_r = AR.rot(2, [128, 2, 128], BF16)
        kkb, kab, rkb, lngb, lnbb = (v256[:, i, :] for i in range(5))
        for d in range(RW_ND):
            order = (list(range(NT)) if d == 0 else [1, 0] + list(range(NT - 1, 1, -1)))[:RW_NCH]
            H, bH = H_r.next()
            P.op("pool", lambda e, H=H: e.memset(H[:], 0.0), [], [bH])
            i_incl, i_strict, i_rem = (0, 2, 3) if d == 0 else (1, 3, 2)
            for ch in order:
                tk = slice(ch * 128, (ch + 1) * 128)
                rk, brk = rk_r.next()
                P.dma("sp", lambda e, rk=rk, ch=ch: e.dma_start(out=rk[:], in_=rkvz[b, ch * 128:(ch + 1) * 128, :]), [B_rkvz[b][ch]], [brk])
                r_, k_, v_, z_ = (rk[:, i * 256:(i + 1) * 256] for i in range(4))
                t1, bt1 = f256.next()
                P.op("dve", lambda e, t1=t1, k_=k_: e.tensor_tensor(out=t1[:], in0=k_, in1=kkb, op=ALU.mult), [brk, b_v256], [bt1])
                t2, bt2 = f256.next()
                P.op("pool", lambda e, t1=t1, t2=t2: e.tensor_tensor(out=t2[:], in0=t1[:], in1=t1[:], op=ALU.mult), [bt1], [bt2])
                sm, bsm = sm_r.next()
                P.op("dve", lambda e, sm=sm, t2=t2: e.reduce_sum(out=sm[:, 0:4], in_=v3(t2[:]), axis=AX.X), [bt2], [bsm])
                P.op("dve", lambda e, sm=sm: e.tensor_scalar_max(out=sm[:, 0:4], in0=sm[:, 0:4], scalar1=1e-24), [bsm], [bsm])
                P.op("act", lambda e, sm=sm: e.activation(out=sm[:, 0:4], in_=sm[:, 0:4], func=AF.Ln), [bsm], [bsm])
                P.op("act", lambda e, sm=sm: e.activation(out=sm[:, 0:4], in_=sm[:, 0:4], func=AF.Exp, scale=-0.5), [bsm], [bsm])
                kk, bkk = f256.next()
                P.op("dve", lambda e, kk=kk, t1=t1, sm=sm: e.tensor_tensor(out=v3(kk[:]), in0=v3(t1[:]), in1=bc4(sm[:, 0:4]), op=ALU.mult), [bt1, bsm], [bkk])
                if RW_CUT < 1:
                    continue
                psA, bpsA = psF.next()
                pp = slice(d * 64, (d + 1) * 64)
                P.op("pe", lambda e, psA=psA: e.matmul(psA[:, 0:256], lhsT=lwla[:, 1, tk], rhs=aup[:, d, :], start=True, stop=False), [b_lwla, b_aup], [bpsA])
                P.op("pe", lambda e, psA=psA: e.matmul(psA[:, 0:256], lhsT=onesrow[0:1, :], rhs=a0r[0:1, d * 256:(d + 1) * 256], start=False, stop=True), [b_ones, b_a0r], [bpsA])
                P.op("pe", lambda e, psA=psA: e.matmul(psA[:, 256:512], lhsT=lwla[:, 0, tk], rhs=wup[:, d, :], start=True, stop=False), [b_lwla, b_wup], [bpsA])
                P.op("pe", lambda e, psA=psA: e.matmul(psA[:, 256:512], lhsT=onesrow[0:1, :], rhs=w0r[0:1, d * 256:(d + 1) * 256], start=False, stop=True), [b_ones, b_w0r], [bpsA])
                asg, basg = e12_r.next()
                P.op("act", lambda e, asg=asg, psA=psA: e.activation(out=asg[:], in_=psA[:, :], func=AF.Sigmoid), [bpsA], [basg])
                a_ = asg[:, 0:256]
                sg_ = asg[:, 256:512]
                psX, bpsX = psF.next()
                psY, bpsY = psF.next()
                P.op("pe", lambda e, psX=psX: e.matmul(psX[:, 0:256], lhsT=cm[:, i_incl, :], rhs=sg_, start=True, stop=True), [b_cm, basg], [bpsX])
                P.op("pe", lambda e, psX=psX: e.matmul(psX[:, 256:512], lhsT=cm[:, i_strict, :], rhs=sg_, start=True, stop=True), [b_cm, basg], [bpsX])
                P.op("pe", lambda e, psY=psY: e.matmul(psY[:, 0:256], lhsT=cm[:, i_rem, :], rhs=sg_, start=True, stop=True), [b_cm, basg], [bpsY])
                for h in range(4):
                    P.op("pe", lambda e, psY=psY, h=h: e.matmul(psY[0:64, 256 + h:257 + h], lhsT=asg[:, 256 + h * 64:256 + (h + 1) * 64], rhs=negcol[:, 0:1], start=True, stop=True), [basg, b_negcol], [bpsY])
                e12, be12 = e12_r.next()
                P.op("act", lambda e, e12=e12, psX=psX: e.activation(out=e12[:], in_=psX[:, :], func=AF.Exp), [bpsX], [be12])
                encw, bencw = f256.next()
                P.op("act", lambda e, encw=encw, psX=psX: e.activation(out=encw[:], in_=psX[:, 0:256], func=AF.Exp, scale=-1.0), [bpsX], [bencw])
                erem, berem = f256.next()
                P.op("act", lambda e, erem=erem, psY=psY: e.activation(out=erem[:], in_=psY[:, 0:256], func=AF.Exp), [bpsY], [berem])
                wcs, bwcs = sm_r.next()
                P.op("act", lambda e, wcs=wcs, psY=psY: e.activation(out=wcs[0:64, 0:4], in_=psY[0:64, 256:260], func=AF.Exp), [bpsY], [bwcs])
                if RW_CUT < 2:
                    continue
                tt, btt = f256.next()
                P.op("dve", lambda e, tt=tt: e.scalar_tensor_tensor(out=tt[:], in0=a_, scalar=-1.0, in1=kab, op0=ALU.add, op1=ALU.mult), [basg, b_v256], [btt])
                kmod, bkmod = f256.next()
                P.op("dve", lambda e, tt=tt, kmod=kmod, k_=k_: e.scalar_tensor_tensor(out=kmod[:], in0=tt[:], scalar=1.0, in1=k_, op0=ALU.add, op1=ALU.mult), [btt, brk], [bkmod])
                bq, bbq = f256.next()
                P.op("pool", lambda e, bq=bq, kk=kk: e.tensor_tensor(out=bq[:], in0=kk[:], in1=a_, op=ALU.mult), [bkk, basg], [bbq])
                At, bAt = b256.next()
                P.op("dve", lambda e, At=At, kk=kk, e12=e12: e.scalar_tensor_tensor(out=At[:], in0=kk[:], scalar=-1.0, in1=e12[:, 256:512], op0=ALU.mult, op1=ALU.mult), [bkk, be12], [bAt])
                Rt, bRt = b256.next()
                P.op("dve", lambda e, Rt=Rt, e12=e12, r_=r_: e.tensor_tensor(out=Rt[:], in0=r_, in1=e12[:, 0:256], op=ALU.mult), [brk, be12], [bRt])
                Bt, bBt = b256.next()
                P.op("pool", lambda e, Bt=Bt, bq=bq, encw=encw: e.tensor_tensor(out=Bt[:], in0=bq[:], in1=encw[:], op=ALU.mult), [bbq, bencw], [bBt])
                Kt, bKt = b256.next()
                P.op("dve", lambda e, Kt=Kt, kmod=kmod, encw=encw: e.tensor_tensor(out=Kt[:], in0=kmod[:], in1=encw[:], op=ALU.mult), [bkmod, bencw], [bKt])
                Bb, bBb = b256.next()
                P.op("pool", lambda e, Bb=Bb, bq=bq, erem=erem: e.tensor_tensor(out=Bb[:], in0=bq[:], in1=erem[:], op=ALU.mult), [bbq, berem], [bBb])
                Kb, bKb = b256.next()
                P.op("pool", lambda e, Kb=Kb, kmod=kmod, erem=erem: e.tensor_tensor(out=Kb[:], in0=kmod[:], in1=erem[:], op=ALU.mult), [bkmod, berem], [bKb])
                if d == 0:
                    P.op("pool", lambda e, kmod=kmod, ch=ch: e.tensor_copy(out=Ksum[:, ch, :], in_=kmod[:]), [bkmod], [b_Ksum[ch]])
                if RW_CUT < 3:
                    continue
                pb, bpb = psB.next()
                for qi, (src, bsrc) in enumerate(((At, bAt), (Rt, bRt), (Bt, bBt), (Kt, bKt))):
                    for fb in range(2):
                        P.op("pe", lambda e, pb=pb, src=src, fb=fb, qi=qi: e.transpose(out=pb[:, (fb * 4 + qi) * 128:(fb * 4 + qi + 1) * 128], in_=src[:, fb * 128:(fb + 1) * 128], identity=identb[:]), [bsrc, b_identb], [bpb])
                FMz = [FMz_r[0].next(), FMz_r[1].next()]
                P.op("act", lambda e, FMz=FMz, pb=pb: e.activation(out=FMz[0][0][0:64], in_=pb[0:64, :].rearrange("p (c t) -> p c t", c=8), func=AF.Copy), [bpb], [FMz[0][1]])
                P.op("dve", lambda e, FMz=FMz, pb=pb: e.tensor_copy(out=FMz[1][0][64:128], in_=pb[64:128, :].rearrange("p (c t) -> p c t", c=8)), [bpb], [FMz[1][1]])
                if RW_CUT < 4:
                    continue
                XT, bXT = XT_r.next()
                psL, bpsL = psF.next()
                for h in range(4):
                    fb = h // 2
                    FMq, bFMq = FMz[h % 2]
                    pq = slice(0, 128)
                    ps, bps = psF.next()
                    P.op("pe", lambda e, ps=ps, FMq=FMq, fb=fb, pq=pq: e.matmul(ps[:, 0:256], lhsT=FMq[pq, fb * 4 + 2, :], rhs=FMq[pq, fb * 4:fb * 4 + 2, :], start=True, stop=True), [bFMq], [bps])
                    P.op("pe", lambda e, ps=ps, FMq=FMq, fb=fb, pq=pq: e.matmul(ps[:, 256:512], lhsT=FMq[pq, fb * 4 + 3, :], rhs=FMq[pq, fb * 4:fb * 4 + 2, :], start=True, stop=True), [bFMq], [bps])
                    if RW_SUB >= 1:
                        P.op("dve", lambda e, ps=ps, XT=XT, h=h: e.tensor_tensor(out=XT[:, h, :], in0=ps[:, :], in1=mask4[:, d, :], op=ALU.mult), [bps, b_mask4], [bXT])
                    if RW_SUB >= 2:
                      pq2 = slice(0, 64) if RW_VAR == 1 else pq
                      P.op("pe", lambda e, psL=psL, FMq=FMq, fb=fb, pq2=pq2, h=h: e.matmul(psL[:, h * 128:(h + 1) * 128], lhsT=FMq[pq2, fb * 4 + 0, :], rhs=FMq[pq2, fb * 4 + 2, :], start=True, stop=True), [bFMq], [bpsL])
                Lp, bLp = Lp_r.next()
                if RW_SUB >= 3:
                  P.op("dve", lambda e, Lp=Lp, psL=psL: e.tensor_tensor(out=Lp[:].rearrange("p h t -> p (h t)"), in0=psL[:, :], in1=maskL[:, d, :], op=ALU.mult), [bpsL, b_maskL], [bLp])
                if RW_CUT < 5:
                    continue
                psP, bpsP = psF.next()
                for h in range(4):
                    P.op("pe", lambda e, psP=psP, XT=XT, h=h, rk=rk: e.matmul(psP[:, h * 64:(h + 1) * 64], lhsT=XT[:, h, 256:384], rhs=rk[:, 512 + h * 64:512 + (h + 1) * 64], start=True, stop=True), [bXT, brk], [bpsP])
                Z, bZ = Z_r.next()
                P.op("pool", lambda e, Z=Z, At=At: e.tensor_copy(out=Z[:, :, 0:64], in_=v3(At[:])), [bAt], [bZ])
                P.op("act", lambda e, Z=Z, psP=psP: e.activation(out=Z[:, :, 64:128], in_=v3(psP[:, 0:256]), func=AF.Copy), [bpsP], [bZ])
                if RW_CUT < 6:
                    continue
                LpT_first = True
                LpT, bLpT = None, None
                for j in range(7):
                    psZ, bpsZ = psF.next()
                    for h in range(4):
                        lt = XT[:, h, 0:128] if LpT_first else LpT[:, h, :]
                        blt = bXT if LpT_first else bLpT
                        P.op("pe", lambda e, psZ=psZ, lt=lt, Z=Z, h=h: e.matmul(psZ[:, h * 128:(h + 1) * 128], lhsT=lt, rhs=Z[:, h, :], start=True, stop=True), [blt, bZ], [bpsZ])
                    P.op("dve", lambda e, Z=Z, psZ=psZ: e.tensor_tensor(out=Z[:].rearrange("p h t -> p (h t)"), in0=psZ[:, :], in1=Z[:].rearrange("p h t -> p (h t)"), op=ALU.add), [bpsZ, bZ], [bZ])
                    if j < 6:
                        ps1, bps1 = psF.next()
                        ps2, bps2 = psF.next()
                        for h in range(4):
                            lt = XT[:, h, 0:128] if LpT_first else LpT[:, h, :]
                            blt = bXT if LpT_first else bLpT
                            P.op("pe", lambda e, ps1=ps1, lt=lt, Lp=Lp, h=h: e.matmul(ps1[:, h * 128:(h + 1) * 128], lhsT=lt, rhs=Lp[:, h, :], start=True, stop=True), [blt, bLp], [bps1])
                            P.op("pe", lambda e, ps2=ps2, lt=lt, Lp=Lp, h=h: e.matmul(ps2[:, h * 128:(h + 1) * 128], lhsT=Lp[:, h, :], rhs=lt, start=True, stop=True), [blt, bLp], [bps2])
                        nLp, bnLp = Lp_r.next()
                        nLpT, bnLpT = LpT_r.next()
                        P.op("act", lambda e, nLp=nLp, ps1=ps1: e.activation(out=nLp[:].rearrange("p h t -> p (h t)"), in_=ps1[:, :], func=AF.Copy), [bps1], [bnLp])
                        P.op("dve", lambda e, nLpT=nLpT, ps2=ps2: e.tensor_copy(out=nLpT[:].rearrange("p h t -> p (h t)"), in_=ps2[:, :]), [bps2], [bnLpT])
                        Lp, bLp, LpT, bLpT = nLp, bnLp, nLpT, bnLpT
                        LpT_first = False
                if RW_CUT < 7:
                    continue
                psFh, bpsFh = psF.next()
                psR, bpsR = psF.next()
                for h in range(4):
                    P.op("pe", lambda e, psFh=psFh, Z=Z, Bb=Bb, h=h: e.matmul(psFh[0:64, h * 64:(h + 1) * 64], lhsT=Z[:, h, 0:64], rhs=Bb[:, h * 64:(h + 1) * 64], start=True, stop=True), [bZ, bBb], [bpsFh])
                    P.op("pe", lambda e, psR=psR, Z=Z, XT=XT, h=h: e.matmul(psR[0:64, h * 128:(h + 1) * 128], lhsT=Z[:, h, 0:64], rhs=XT[:, h, 128:256], start=True, stop=False), [bZ, bXT], [bpsR])
                    P.op("pe", lambda e, psR=psR, Rt=Rt, h=h: e.matmul(psR[0:64, h * 128:(h + 1) * 128], lhsT=Rt[:, h * 64:(h + 1) * 64], rhs=identb[:], start=False, stop=True), [bRt, b_identb], [bpsR])
                PhiT, bPhiT = PhiT_r.next()
                for h in range(4):
                    P.op("dve", lambda e, PhiT=PhiT, psFh=psFh, wcs=wcs, h=h: e.scalar_tensor_tensor(out=PhiT[:, h, :], in0=identf[0:64, 0:64], scalar=wcs[0:64, h:h + 1], in1=psFh[0:64, h * 64:(h + 1) * 64], op0=ALU.mult, op1=ALU.add), [bpsFh, bwcs, b_identf], [bPhiT])
                RhT, bRhT = RhT_r.next()
                P.op("act", lambda e, RhT=RhT, psR=psR: e.activation(out=RhT[:].rearrange("p h t -> p (h t)"), in_=psR[0:64, :], func=AF.Copy), [bpsR], [bRhT])
                if RW_CUT < 8:
                    continue
                psYo, bpsYo = psF.next()
                psH, bpsH = psF.next()
                for h in range(4):
                    vh = rk[:, 512 + h * 64:512 + (h + 1) * 64]
                    P.op("pe", lambda e, psYo=psYo, XT=XT, Z=Z, h=h: e.matmul(psYo[:, h * 64:(h + 1) * 64], lhsT=XT[:, h, 128:256], rhs=Z[:, h, 64:128], start=True, stop=False), [bXT, bZ], [bpsYo])
                    P.op("pe", lambda e, psYo=psYo, XT=XT, vh=vh, h=h: e.matmul(psYo[:, h * 64:(h + 1) * 64], lhsT=XT[:, h, 384:512], rhs=vh, start=False, stop=False), [bXT, brk], [bpsYo])
                    P.op("pe", lambda e, psYo=psYo, RhT=RhT, H=H, h=h: e.matmul(psYo[:, h * 64:(h + 1) * 64], lhsT=RhT[:, h, :], rhs=H[:, h, :], start=False, stop=True), [bRhT, bH], [bpsYo])
                    P.op("pe", lambda e, psH=psH, PhiT=PhiT, H=H, h=h: e.matmul(psH[0:64, h * 64:(h + 1) * 64], lhsT=PhiT[:, h, :], rhs=H[:, h, :], start=True, stop=False), [bPhiT, bH], [bpsH])
                    P.op("pe", lambda e, psH=psH, Bb=Bb, Z=Z, h=h: e.matmul(psH[0:64, h * 64:(h + 1) * 64], lhsT=Bb[:, h * 64:(h + 1) * 64], rhs=Z[:, h, 64:128], start=False, stop=False), [bBb, bZ], [bpsH])
                    P.op("pe", lambda e, psH=psH, Kb=Kb, vh=vh, h=h: e.matmul(psH[0:64, h * 64:(h + 1) * 64], lhsT=Kb[:, h * 64:(h + 1) * 64], rhs=vh, start=False, stop=True), [bKb, brk], [bpsH])
                nH, bnH = H_r.next()
                P.op("act", lambda e, nH=nH, psH=psH: e.activation(out=nH[:].rearrange("p h t -> p (h t)"), in_=psH[0:64, 0:256], func=AF.Copy), [bpsH], [bnH])
                H, bH = nH, bnH
                if d == 0:
                    P.op("dve", lambda e, psYo=psYo, ch=ch: e.tensor_copy(out=Yf[:, ch, :], in_=psYo[:, 0:256]), [bpsYo], [b_Yf[ch]])
                    continue
                if l == NL - 1 and ch < 2:
                    continue
                if RW_CUT < 9:
                    continue
                y, by = f256.next()
                P.op("dve", lambda e, y=y, psYo=psYo, ch=ch: e.tensor_tensor(out=y[:], in0=psYo[:, 0:256], in1=Yf[:, ch, :], op=ALU.add), [bpsYo, b_Yf[ch]], [by])
                s1, bs1 = sm_r.next()
                P.op("dve", lambda e, s1=s1, y=y: e.reduce_sum(out=s1[:, 0:4], in_=v3(y[:]), axis=AX.X), [by], [bs1])
                P.op("dve", lambda e, s1=s1: e.tensor_scalar_mul(out=s1[:, 0:4], in0=s1[:, 0:4], scalar1=-1.0 / 64.0), [bs1], [bs1])
                yc, byc = f256.next()
                P.op("dve", lambda e, yc=yc, y=y, s1=s1: e.tensor_tensor(out=v3(yc[:]), in0=v3(y[:]), in1=bc4(s1[:, 0:4]), op=ALU.add), [by, bs1], [byc])
                sq, bsq = f256.next()
                P.op("pool", lambda e, sq=sq, yc=yc: e.tensor_tensor(out=sq[:], in0=yc[:], in1=yc[:], op=ALU.mult), [byc], [bsq])
                P.op("dve", lambda e, s1=s1, sq=sq: e.reduce_sum(out=s1[:, 4:8], in_=v3(sq[:]), axis=AX.X), [bsq], [bs1])
                P.op("act", lambda e, s1=s1: e.activation(out=s1[:, 4:8], in_=s1[:, 4:8], func=AF.Ln, bias=epsc[:, 1:2], scale=1.0 / 64.0), [bs1, b_eps], [bs1])
                P.op("act", lambda e, s1=s1: e.activation(out=s1[:, 4:8], in_=s1[:, 4:8], func=AF.Exp, scale=-0.5), [bs1], [bs1])
                yn, byn = f256.next()
                P.op("dve", lambda e, yn=yn, yc=yc, s1=s1: e.tensor_tensor(out=v3(yn[:]), in0=v3(yc[:]), in1=bc4(s1[:, 4:8]), op=ALU.mult), [byc, bs1], [byn])
                P.op("pool", lambda e, yn=yn: e.tensor_tensor(out=yn[:], in0=yn[:], in1=lngb, op=ALU.mult), [byn, b_v256], [byn])
                P.op("pool", lambda e, yn=yn: e.tensor_tensor(out=yn[:], in0=yn[:], in1=lnbb, op=ALU.add), [byn, b_v256], [byn])
                ks, bks = f256.next()
                P.op("pool", lambda e, ks=ks, kmod=kmod, ch=ch: e.tensor_tensor(out=ks[:], in0=kmod[:], in1=Ksum[:, ch, :], op=ALU.add), [bkmod, b_Ksum[ch]], [bks])
                P.op("pool", lambda e, ks=ks, r_=r_: e.tensor_tensor(out=ks[:], in0=ks[:], in1=r_, op=ALU.mult), [bks, brk], [bks])
                P.op("pool", lambda e, ks=ks: e.tensor_tensor(out=ks[:], in0=ks[:], in1=rkb, op=ALU.mult), [bks, b_v256], [bks])
                s2, bs2 = sm_r.next()
                P.op("dve", lambda e, s2=s2, ks=ks: e.reduce_sum(out=s2[:, 0:4], in_=v3(ks[:]), axis=AX.X), [bks], [bs2])
                bv, bbv = f256.next()
                P.op("dve", lambda e, bv=bv, v_=v_, s2=s2: e.tensor_tensor(out=v3(bv[:]), in0=v3(v_), in1=bc4(s2[:, 0:4]), op=ALU.mult), [brk, bs2], [bbv])
                P.op("pool", lambda e, yn=yn, bv=bv: e.tensor_tensor(out=yn[:], in0=yn[:], in1=bv[:], op=ALU.add), [byn, bbv], [byn])
                sz, bsz = f256.next()
                P.op("act", lambda e, sz=sz, z_=z_: e.activation(out=sz[:], in_=z_, func=AF.Silu), [brk], [bsz])
                yb, byb = b256.next()
                P.op("dve", lambda e, yb=yb, yn=yn, sz=sz: e.tensor_tensor(out=yb[:], in0=yn[:], in1=sz[:], op=ALU.mult), [byn, bsz], [byb])
                pb2, bpb2 = psB.next()
                for fb in range(2):
                    P.op("pe", lambda e, pb2=pb2, yb=yb, fb=fb: e.transpose(out=pb2[:, fb * 128:(fb + 1) * 128], in_=yb[:, fb * 128:(fb + 1) * 128], identity=identb[:]), [byb, b_identb], [bpb2])
                ro, bro = ro_r.next()
                P.op("act", lambda e, ro=ro, pb2=pb2: e.activation(out=ro[:].rearrange("p a t -> p (a t)"), in_=pb2[:, 0:256], func=AF.Copy), [bpb2], [bro])
                P.dma("pool", lambda e, ro=ro, ch=ch: e.dma_start(out=mixT[b, 0:2, :, ch * 128:(ch + 1) * 128].rearrange("k p t -> p k t"), in_=ro[:]), [bro], [B_mixR[b][ch]])

    def phase_attn(l, b):
        AR.reset()
        kTt = AR.alloc([128, 4, T], BF16); b_kTt = Buf()
        Vt = AR.alloc([128, NT, 520], BF16); b_Vt = Buf()
        qz = [AR.alloc([128, 4, 512], BF16) for _ in range(2)]
        bqg = Buf()
        for j in range(2):
            P.op("pool", lambda e, j=j: e.memset(qz[j], 0.0), [], [bqg])
        E_r = AR.rot(3, [128, 512], BF16)
        szq_r = AR.rot(1, [128, 4, 512], BF16)
        oall_r = AR.rot(1, [128, 4, 512], BF16)
        ast_r = AR.rot(2, [128, 4, 128], BF16)
        junk_r = AR.rot(1, [128, 128], BF16)
        P.op("pool", lambda e: e.memset(Vt.rearrange("p k (h e) -> p k h e", e=130)[:, :, :, 128:130], 1.0), [], [b_Vt])
        scoreR = Rot(psAll[4:8], psBufs[4:8])
        oj_r = [AR.rot(2, [128, 4, 132], F32) for _ in range(2)]
        w_r = AR.rot(4, [128, 4, 128], F32)
        P.dma("sp", lambda e: e.dma_start(out=kTt[:], in_=qkT[b, 1, :, :, :].rearrange("h p t -> p h t")), list(B_k[b]), [b_kTt])
        for kt in range(NT):
            P.dma("sp", lambda e, kt=kt: e.dma_start(out=Vt[:, kt, :].rearrange("p (h e) -> p h e", e=130)[:, :, 0:128], in_=avd[b, kt * 128:(kt + 1) * 128, :].rearrange("p (h e) -> p h e", e=128)), [B_av[b][kt]], [b_Vt])
        qgroups = [([2, 3, 4, 5], list(range(NT))), ([6, 7, 8, 9], list(range(NT))), ([10, 11, 12, 13], list(range(NT))), ([14, 15, 16, 17], list(range(NT)))]
        if l < NL - 1:
            qgroups = [([0, 1], [0, 1])] + qgroups
        for (qt, kts) in qgroups[:AT_NG]:
            nq = len(qt)
            ntok = nq * 128
            tok0 = qt[0] * 128
            for j in range(2):
                P.dma("sp", lambda e, j=j, tok0=tok0, ntok=ntok: e.dma_start(out=qz[j][j * 64:(j + 1) * 64, :, 0:ntok], in_=qkT[b, 0, :, j * 64:(j + 1) * 64, tok0:tok0 + ntok].rearrange("h p t -> p h t")), [B_q[b][t] for t in qt], [bqg])
            szq, bszq = szq_r.next()
            for qi, t in enumerate(qt):
                P.dma("sp", lambda e, szq=szq, qi=qi, t=t: e.dma_start(out=szq[:, qi, :], in_=aszd[b, t * 128:(t + 1) * 128, :]), [B_asz[b][t]], [bszq])
            oall, boall = oall_r.next()
            from collections import deque
            items = [(h, j, ki, kt) for h in range(4) for j in range(2) for ki, kt in enumerate(kts)]
            acc = [(psAll[i_], psBufs[i_]) for i_ in range(nq)]
            pending = deque()
            ojs_h = {}

            def do_pv(item, E, bE):
                h, j, ki, kt = item
                first, last = (ki == 0), (ki == len(kts) - 1)
                for qi in range(nq):
                    ab, bab = acc[qi]
                    P.op("pe", lambda e, ab=ab, E=E, qi=qi: e.matmul(ab[:, 0:129], lhsT=E[:, qi * 128:(qi + 1) * 128], rhs=Vt[:, kt, h * 130:h * 130 + 129], start=first, stop=last), [bE, b_Vt], [bab])
                if not last:
                    return
                ojt, bojt = oj_r[j].next()
                ojs_h[j] = (ojt, bojt)
                for qi in range(nq):
                    if (qi + j) % 2 == 0:
                        P.op("dve", lambda e, qi=qi: e.tensor_copy(out=ojt[:, qi, 0:129], in_=acc[qi][0][:, 0:129]), [acc[qi][1]], [bojt])
                    else:
                        P.op("act", lambda e, qi=qi: e.activation(out=ojt[:, qi, 0:129], in_=acc[qi][0][:, 0:129], func=AF.Copy), [acc[qi][1]], [bojt])
                if j == 1:
                    combine(h)

            def combine(h):
                ojs = [ojs_h[0], ojs_h[1]]
                (oj0, boj0), (oj1, boj1) = ojs
                sm, bsm = sm_r.next()
                P.op("dve", lambda e, sm=sm, oj0=oj0: e.reciprocal(out=sm[:, 0:nq].unsqueeze(2), in_=oj0[:, 0:nq, 128:129]), [boj0], [bsm])
                P.op("dve", lambda e, sm=sm, oj1=oj1: e.reciprocal(out=sm[:, 4:4 + nq].unsqueeze(2), in_=oj1[:, 0:nq, 128:129]), [boj1], [bsm])
                P.op("dve", lambda e, sm=sm: e.tensor_scalar(out=sm[:, 4:4 + nq], in0=sm[:, 4:4 + nq], scalar1=neglam[:, 0:1], scalar2=None, op0=ALU.mult), [bsm, b_neglam], [bsm])
                t1, bt1 = w_r.next()
                t0, bt0 = w_r.next()
                P.op("dve", lambda e, t1=t1, oj1=oj1, sm=sm: e.tensor_tensor(out=t1[:, 0:nq, :], in0=oj1[:, 0:nq, 0:128], in1=sm[:, 4:4 + nq].unsqueeze(2).to_broadcast([128, nq, 128]), op=ALU.mult), [boj1, bsm], [bt1])
                P.op("dve", lambda e, t0=t0, oj0=oj0, sm=sm: e.tensor_tensor(out=t0[:, 0:nq, :], in0=oj0[:, 0:nq, 0:128], in1=sm[:, 0:nq].unsqueeze(2).to_broadcast([128, nq, 128]), op=ALU.mult), [boj0, bsm], [bt0])
                P.op("pool", lambda e, t0=t0, t1=t1: e.tensor_tensor(out=t0[:, 0:nq, :], in0=t0[:, 0:nq, :], in1=t1[:, 0:nq, :], op=ALU.add), [bt0, bt1], [bt0])
                P.op("pool", lambda e, t0=t0, t1=t1: e.tensor_tensor(out=t1[:, 0:nq, :], in0=t0[:, 0:nq, :], in1=t0[:, 0:nq, :], op=ALU.mult), [bt0], [bt1])
                sm2, bsm2 = sm_r.next()
                P.op("dve", lambda e, sm2=sm2, t1=t1: e.reduce_sum(out=sm2[:, 0:nq], in_=t1[:, 0:nq, :], axis=AX.X), [bt1], [bsm2])
                P.op("act", lambda e, sm2=sm2: e.activation(out=sm2[:, 0:nq], in_=sm2[:, 0:nq], func=AF.Ln, bias=epsc[:, 0:1], scale=1.0 / 128.0), [bsm2, b_eps], [bsm2])
                P.op("act", lambda e, sm2=sm2: e.activation(out=sm2[:, 0:nq], in_=sm2[:, 0:nq], func=AF.Exp, scale=-0.5), [bsm2], [bsm2])
                P.op("dve", lambda e, t0=t0, sm2=sm2: e.tensor_tensor(out=t0[:, 0:nq, :], in0=t0[:, 0:nq, :], in1=sm2[:, 0:nq].unsqueeze(2).to_broadcast([128, nq, 128]), op=ALU.mult), [bt0, bsm2], [bt0])
                P.op("dve", lambda e, t0=t0: e.tensor_tensor(out=t0[:, 0:nq, :], in0=t0[:, 0:nq, :], in1=gsub[:].unsqueeze(1).to_broadcast([128, nq, 128]), op=ALU.mult), [bt0, b_gsub], [bt0])
                P.op("pool", lambda e, t0=t0, oall=oall, szq=szq: e.tensor_tensor(out=oall[:, 0:nq, h * 128:(h + 1) * 128], in0=t0[:, 0:nq, :], in1=szq[:, 0:nq, h * 128:(h + 1) * 128], op=ALU.mult), [bt0, bszq], [boall])

            for item in items:
                h, j, ki, kt = item
                ps, bps = scoreR.next()
                P.op("pe", lambda e, ps=ps: e.matmul(ps[:, 0:ntok], lhsT=kTt[:, h, kt * 128:(kt + 1) * 128], rhs=qz[j][:, h, 0:ntok], start=True, stop=True), [b_kTt, bqg], [bps])
                E, bE = E_r.next()
                P.op("act", lambda e, E=E, ps=ps: e.activation(out=E[:, 0:ntok], in_=ps[:, 0:ntok], func=AF.Exp, scale=0.125), [bps], [bE])
                pending.append((item, E, bE))
                if len(pending) >= 3:
                    do_pv(*pending.popleft())
            while pending:
                do_pv(*pending.popleft())
            for qi, t in enumerate(qt if AT_CUT >= 4 else []):
                pb, bpb = psB.next()
                for h in range(4):
                    P.op("pe", lambda e, pb=pb, oall=oall, qi=qi, h=h: e.transpose(out=pb[:, h * 128:(h + 1) * 128], in_=oall[:, qi, h * 128:(h + 1) * 128], identity=identb[:]), [boall, b_identb], [bpb])
                ast, bast = ast_r.next()
                P.op("act", lambda e, ast=ast, pb=pb: e.activation(out=ast[:].rearrange("p h t -> p (h t)"), in_=pb[:, 0:512], func=AF.Copy), [bpb], [bast])
                P.dma("pool", lambda e, ast=ast, t=t: e.dma_start(out=mixT[b, 4:8, :, t * 128:(t + 1) * 128].rearrange("k p t -> p k t"), in_=ast[:]), [bast], [B_mixA[b][t]])

    def phase_out(l, b):
        AR.reset()
        xt_r = AR.rot(2, [128, D], F32)
        xo_r = AR.rot(2, [128, D], F32)
        t512 = AR.rot(4, [128, 512], F32)
        mx_r = AR.rot(2, [128, 8, 128], BF16)
        junk_r = AR.rot(2, [128, 512], BF16)
        woutb = AR.alloc([128, 8, D], BF16); b_woutb = Buf()
        P.dma("sp", lambda e: e.dma_start(out=woutb, in_=woutbf[l, :, :, :]), [B_woutbf[l]], [b_woutb])
        xsrc, Bx = (xall, B_x[0]) if l == 0 else (x1, B_x[1])
        tiles = list(range(NT)) if l < NL - 1 else list(range(2, NT))
        for t in tiles:
            G = (bcC, b_bcC) if t < 2 else (bcB, b_bcB)
            mx, bmx = mx_r.next()
            P.dma("sp", lambda e, mx=mx, t=t: e.dma_start(out=mx[:], in_=mixT[b, :, :, t * 128:(t + 1) * 128].rearrange("k p t -> p k t")), [B_mixR[b][t], B_mixC[b][t], B_mixA[b][t]], [bmx])
            xt, bxt = xt_r.next()
            P.dma("sp", lambda e, xt=xt, t=t: e.dma_start(out=xt[:], in_=xsrc[b, t * 128:(t + 1) * 128, :]), [Bx[b][t]], [bxt])
            pss = []
            sm, bsm = sm_r.next()
            for hf in range(2):
                ps, bps = psF.next()
                pss.append((ps, bps))
                for k in range(8):
                    P.op("pe", lambda e, ps=ps, mx=mx, k=k, hf=hf: e.matmul(ps[:, :], lhsT=mx[:, k, :], rhs=woutb[:, k, hf * 512:(hf + 1) * 512], start=(k == 0), stop=(k == 7)), [bmx, b_woutb], [bps])
                jk, bjk = junk_r.next()
                P.op("act", lambda e, jk=jk, ps=ps, sm=sm, hf=hf: e.activation(out=jk[:, 0:512], in_=ps[:, :], func=AF.Square, scale=1.0 / 32.0, accum_out=sm[:, hf:hf + 1]), [bps], [bjk, bsm])
            P.op("dve", lambda e, sm=sm: e.tensor_tensor(out=sm[:, 2:3], in0=sm[:, 0:1], in1=sm[:, 1:2], op=ALU.add), [bsm], [bsm])
            rstd_from_ms(sm[:, 2:3], bsm, 0)
            xo, bxo = xo_r.next()
            for hf in range(2):
                ps, bps = pss[hf]
                tq, btq = t512.next()
                P.op("dve", lambda e, tq=tq, ps=ps, sm=sm, hf=hf, G=G: e.scalar_tensor_tensor(out=tq[:], in0=ps[:, :], scalar=sm[:, 2:3], in1=G[0][2][:, hf * 512:(hf + 1) * 512], op0=ALU.mult, op1=ALU.mult), [bps, bsm, G[1][2]], [btq])
                P.op("pool", lambda e, xo=xo, tq=tq, xt=xt, hf=hf: e.tensor_tensor(out=xo[:, hf * 512:(hf + 1) * 512], in0=tq[:], in1=xt[:, hf * 512:(hf + 1) * 512], op=ALU.add), [btq, bxt], [bxo])
            if l < NL - 1:
                P.dma("pool", lambda e, xo=xo, t=t: e.dma_start(out=x1[b, t * 128:(t + 1) * 128, :], in_=xo[:]), [bxo], [B_x[1][b][t]])
                if dbg and b == 0:
                    P.dma("pool", lambda e, xo=xo, t=t: e.dma_start(out=dbgd["d_x1"][t * 128:(t + 1) * 128, :], in_=xo[:]), [bxo], [B_x[2][b][t]])
            else:
                P.dma("pool", lambda e, xo=xo, t=t: e.dma_start(out=outd[b, (t - 2) * 128:(t - 1) * 128, :], in_=xo[:]), [bxo], [B_x[2][b][t]])

    for l in range(NL):
        if upto >= 1:
            layer_setup(l)
        for b in range(NB):
            if upto >= 2:
                bcast_rows(b, bcB, b_bcB)
                phase_a(l, b)
            if upto >= 3:
                phase_conv(l, b)
            if upto >= 4:
                phase_rwkv(l, b)
            if upto >= 5:
                phase_attn(l, b)
            if upto >= 6:
                phase_out(l, b)
            if dbg and l == 0 and b == 0:
                bd = Buf()
                P.dma("sp", lambda e: e.dma_start(out=dbgd["d_mix"][:, :, :], in_=mixT[0, :, :, :]), B_mixR[0] + B_mixC[0] + B_mixA[0], [bd])
                P.dma("sp", lambda e: e.dma_start(out=dbgd["d_rkvz"][:, :], in_=rkvz[0, :, :]), B_rkvz[0], [bd])
                P.dma("sp", lambda e: e.dma_start(out=dbgd["d_q"][:, :, :, :], in_=qkT[0, :, :, :, :]), B_q[0] + B_k[0], [bd])
    stats = P.finish()
    return nc, stats


_CACHE = {}


def _prep(inputs, core, NB=NBF):
    cp = _CACHE.setdefault("colperm", _colperm())
    cst = _CACHE.setdefault("consts", _consts())
    f = lambda a: np.ascontiguousarray(np.asarray(a, dtype=np.float32))
    bs = slice(core * NB, (core + 1) * NB)
    x = np.asarray(inputs["x"])[bs]
    ctx = np.asarray(inputs["ctx"])[bs]
    m = {}
    m["xall"] = f(np.concatenate([ctx, x], axis=1))
    c5 = np.concatenate([np.asarray(inputs["c"])[bs], np.asarray(inputs["c_ctx"])[None, :]], axis=0)
    if c5.shape[0] < 5:
        c5 = np.concatenate([c5, np.zeros((5 - c5.shape[0], D), np.float32)], 0)
        c5[4] = np.asarray(inputs["c_ctx"])
    m["cc"] = f(c5)
    sh = _CACHE.get("shared")
    if sh is None:
        sh = {}
        sh["wext"] = f(np.asarray(inputs["w_in"])[:, :, cp])
        sh["wout"] = f(inputs["w_out"])
        sh["modw"] = f(inputs["mod_w"])
        sh["modb"] = f(np.asarray(inputs["mod_b"])[:, None, :])
        sh["preg"] = f(np.asarray(inputs["norm_pre_g"])[:, None, :])
        sh["postg"] = f(np.asarray(inputs["norm_post_g"])[:, None, :])
        sh["w0"] = f(np.asarray(inputs["rwkv_w0"]).reshape(L_FULL, 1, 512))
        sh["a0"] = f(np.asarray(inputs["rwkv_a0"]).reshape(L_FULL, 1, 512))
        sh["wup"] = f(np.asarray(inputs["rwkv_w_up"]).reshape(L_FULL, 128, 256))
        sh["aup"] = f(np.asarray(inputs["rwkv_a_up"]).reshape(L_FULL, 128, 256))
        sh["vec256"] = f(np.stack([np.asarray(inputs["rwkv_k_k"]), np.asarray(inputs["rwkv_k_a"]),
                                   np.asarray(inputs["rwkv_r_k"]).reshape(L_FULL, 256), np.asarray(inputs["rwkv_ln_g"]),
                                   np.asarray(inputs["rwkv_ln_b"]), np.asarray(inputs["diff_lambda"]).reshape(L_FULL, 256)], axis=1))
        sh["convw"] = f(inputs["conv_w"])
        sh["subg"] = f(np.asarray(inputs["diff_subln_g"])[:, None, :])
        for k, v in cst.items():
            sh[k] = f(v)
        _CACHE["shared"] = sh
    m.update(sh)
    return m


def kernel(**inputs):
    _CACHE.pop("shared", None)
    nc, stats = build()
    in_maps = [_prep(inputs, core) for core in range(8)]
    res = run_bass_kernel_spmd(nc, in_maps, core_ids=list(range(8)))
    out = np.concatenate([np.asarray(r["out"]) for r in res.results], axis=0)
    return out.astype(np.float32)
```

```python
import contextlib
import math
import numpy as np
import concourse.bass as bass
import concourse.mybir as mybir
from concourse.bass_utils import run_bass_kernel_spmd

F32 = mybir.dt.float32
BF16 = mybir.dt.bfloat16
AF = mybir.ActivationFunctionType
ALU = mybir.AluOpType
AX = mybir.AxisListType

EPOCH = 20000
RW_CUT = 99
RW_SUB = 99
RW_NCH = 99
RW_ND = 2
RW_VAR = 0
AT_CUT = 99
AT_NG = 99
N_DMA_SEMS = 8


class Buf:
    __slots__ = ("w", "r", "name")

    def __init__(self, name=""):
        self.w = None
        self.r = []
        self.name = name


class _Cap:
    def __init__(self):
        self.call = None

    def __getattr__(self, name):
        def f(*a, **k):
            self.call = (name, a, k)
            return self
        return f


class Rec:
    __slots__ = ("eng", "fn", "deps", "raw", "sig", "needed", "is_dma", "pos")

    def __init__(self, eng, fn, is_dma):
        self.eng = eng
        self.fn = fn
        self.deps = set()
        self.raw = set()
        self.sig = None
        self.needed = False
        self.is_dma = is_dma
        self.pos = 0


class Prog:
    ENGS = ("sp", "act", "pool", "dve", "pe")

    def __init__(self, nc):
        self.nc = nc
        self.streams = {e: [] for e in self.ENGS}
        self.stack = contextlib.ExitStack()
        self.dma_sems = {}
        self.dma_rr = {}
        self.dma_last = {}
        self.dma_cnt = {}
        self.nsem = 0
        self.all_dma = []
        self.fence = []

    def barrier(self):
        fence = []
        for e in self.ENGS:
            for rec in reversed(self.streams[e]):
                if not rec.is_dma:
                    fence.append(rec)
                    break
        fence += list(self.dma_last.values())
        self.fence = fence

    def sem(self, name):
        self.nsem += 1
        return self.stack.enter_context(self.nc.semaphore(name))

    def sbuf(self, name, shape, dt):
        return self.stack.enter_context(self.nc.sbuf_tensor("s_" + name, list(shape), dt))

    def psum(self, name, shape, dt):
        return self.stack.enter_context(self.nc.psum_tensor("p_" + name, list(shape), dt))

    def _track(self, rec, reads, writes):
        for b in reads:
            if b.w is not None:
                rec.deps.add(b.w)
                rec.raw.add(b.w)
        for b in writes:
            if b.w is not None:
                rec.deps.add(b.w)
            for r in b.r:
                rec.deps.add(r)
        for b in reads:
            if not rec.is_dma:
                b.r = [r for r in b.r if r.is_dma or r.eng != rec.eng]
            b.r.append(rec)
        for b in writes:
            b.w = rec
            b.r = []
        for f in self.fence:
            rec.deps.add(f)
            rec.raw.add(f)
        rec.deps.discard(rec)
        rec.raw.discard(rec)

    def op(self, eng, fn, reads=(), writes=()):
        cap = _Cap()
        fn(cap)
        rec = Rec(eng, cap.call, False)
        self._track(rec, reads, writes)
        self.streams[eng].append(rec)
        return rec

    def dma(self, q, fn, reads=(), writes=()):
        cap = _Cap()
        fn(cap)
        rec = Rec(q, cap.call, True)
        self._track(rec, reads, writes)
        if q not in self.dma_sems:
            self.dma_sems[q] = [self.sem(f"dma_{q}_{i}") for i in range(N_DMA_SEMS)]
            self.dma_rr[q] = 0
        i = self.dma_rr[q]
        self.dma_rr[q] = (i + 1) % N_DMA_SEMS
        s = self.dma_sems[q][i]
        key = (q, i)
        prev = self.dma_last.get(key)
        if prev is not None:
            rec.deps.add(prev)
        self.dma_last[key] = rec
        self.dma_cnt[key] = self.dma_cnt.get(key, 0) + 1
        rec.sig = (s, 16 * self.dma_cnt[key])
        self.streams[q].append(rec)
        self.all_dma.append(rec)
        return rec

    @staticmethod
    def _skip(rec, d):
        if d.is_dma or rec.is_dma or d.eng != rec.eng:
            return False
        if rec.eng == "pe":
            return True
        return d not in rec.raw

    def finish(self):
        nc = self.nc
        for e in self.ENGS:
            for rec in self.streams[e]:
                for d in rec.deps:
                    if d.is_dma or self._skip(rec, d):
                        continue
                    d.needed = True
        for e in self.ENGS:
            cnt = 0
            sem = None
            for rec in self.streams[e]:
                if rec.is_dma or not rec.needed:
                    continue
                if sem is None or cnt >= EPOCH:
                    sem = self.sem(f"c_{e}_{self.nsem}")
                    cnt = 0
                cnt += 1
                rec.sig = (sem, cnt)
            self.sigcnt = getattr(self, "sigcnt", {})
            self.sigcnt[e] = cnt
        final_waits = {}
        for rec in self.all_dma:
            s, v = rec.sig
            final_waits[id(s)] = (s, max(v, final_waits.get(id(s), (s, 0))[1]))
        streams = self.streams
        skip = self._skip

        def emit(ename, eng):
            waited = {}
            for rec in streams[ename]:
                for d in rec.deps:
                    if skip(rec, d):
                        continue
                    s, v = d.sig
                    if waited.get(id(s), 0) < v:
                        eng.wait_ge(s, v)
                        waited[id(s)] = v
                name, a_, k_ = rec.fn
                ins = getattr(eng, name)(*a_, **k_)
                if rec.is_dma:
                    ins.then_inc(rec.sig[0], 16)
                elif rec.sig is not None:
                    ins.then_inc(rec.sig[0], 1)
            if ename == "sp":
                for s, v in final_waits.values():
                    eng.wait_ge(s, v)

        with nc.Block() as block:
            @block.sync
            def _(e):
                emit("sp", e)

            @block.scalar
            def _(e):
                emit("act", e)

            @block.gpsimd
            def _(e):
                emit("pool", e)

            @block.vector
            def _(e):
                emit("dve", e)

            @block.tensor
            def _(e):
                emit("pe", e)
        self.stack.close()
        return {e: (len(self.streams[e]), self.sigcnt.get(e)) for e in self.ENGS}


class Rot:
    def __init__(self, tiles, bufs=None):
        self.tiles = tiles
        self.bufs = bufs if bufs is not None else [Buf() for _ in tiles]
        self.i = 0

    def next(self):
        i = self.i
        self.i = (i + 1) % len(self.tiles)
        return self.tiles[i], self.bufs[i]


D = 1024
L_FULL = 2
NBF = 4
CTX = 256
SEQ = 2048
T = CTX + SEQ
NT = T // 128
WC = 5376
NFM = 26
DECAY_C = -math.exp(-0.5)
NORM_EPS = 1e-6
GN_EPS = 64e-5


def _colperm():
    cols = []
    cols += list(range(768, 896)) + list(range(896, 1024))
    cols += list(range(1536, 1792)) + list(range(1792, 2048)) + list(range(2048, 2304)) + list(range(1280, 1536))

    def rot_src(base):
        out = []
        for jd in range(128):
            j, dd = divmod(jd, 64)
            g, i = divmod(dd, 32)
            out.append(base + j * 64 + g * 32 + (i + 16 if i < 16 else i - 16))
        return out

    for s0 in (2304, 2816):
        for h in range(4):
            base = s0 + h * 128
            cols += list(range(base, base + 128)) + rot_src(base)
    cols += list(range(0, 768)) + list(range(1024, 1280))
    cols += list(range(3328, 3840)) + list(range(3840, 4352))
    assert len(cols) == WC
    return np.array(cols)


def _consts():
    p = np.arange(128)[:, None]
    f = np.arange(128)[None, :]
    LE, GE, LT, GT = (p <= f), (p >= f), (p < f), (p > f)
    c = {}
    c["ident"] = np.eye(128, dtype=np.float32)
    c["cm"] = (np.stack([LE, GE, LT, GT], 1).astype(np.float32) * DECAY_C).astype(np.float32)
    m4 = np.zeros((128, 2, 512), np.float32)
    m4[:, 0] = np.concatenate([LT, LE, LT, LE], 1)
    m4[:, 1] = np.concatenate([GT, GE, GT, GE], 1)
    c["mask4"] = m4
    mL = np.zeros((128, 2, 512), np.float32)
    mL[:, 0] = np.concatenate([GT] * 4, 1)
    mL[:, 1] = np.concatenate([LT] * 4, 1)
    c["maskL"] = mL
    pos = np.arange(SEQ)
    row = (pos // 64).astype(np.float32)
    col = (pos % 64).astype(np.float32)
    inv = (10000.0 ** (-np.arange(0, 32, 2, dtype=np.float32) / 32)).astype(np.float32)
    cosT = np.ones((128, T), np.float32)
    sinT = np.zeros((128, T), np.float32)
    for pp in range(128):
        dd = pp % 64
        g, i = divmod(dd, 32)
        ang = (row if g == 0 else col) * inv[i % 16]
        cosT[pp, CTX:] = np.cos(ang)
        sinT[pp, CTX:] = np.sin(ang) * (-1.0 if i < 16 else 1.0)
    c["cosT"] = cosT
    c["sinT"] = sinT
    sel = np.zeros((5, 5, 128), np.float32)
    for b in range(5):
        sel[b, b, :] = 1.0
    c["sel"] = sel
    return c


def build(NB=NBF, NL=L_FULL, dbg=False, upto=9):
    nc = bass.Bass("TRN2", target_bir_lowering=False)
    P = Prog(nc)

    def din(name, shape, dt=F32):
        return nc.dram_tensor(name, list(shape), dt, kind="ExternalInput").ap()

    def dscr(name, shape, dt):
        return nc.dram_tensor(name, list(shape), dt, kind="Internal").ap()

    xall = din("xall", [NB, T, D])
    cc = din("cc", [5, D])
    wext = din("wext", [L_FULL, D, WC])
    woutd = din("wout", [L_FULL, D, D])
    modw = din("modw", [L_FULL, D, 3 * D])
    modb = din("modb", [L_FULL, 1, 3 * D])
    pregd = din("preg", [L_FULL, 1, D])
    postgd = din("postg", [L_FULL, 1, D])
    w0d = din("w0", [L_FULL, 1, 512])
    a0d = din("a0", [L_FULL, 1, 512])
    wupd = din("wup", [L_FULL, 128, 256])
    aupd = din("aup", [L_FULL, 128, 256])
    vec256 = din("vec256", [L_FULL, 6, 256])
    convwd = din("convw", [L_FULL, 3, 256])
    subgd = din("subg", [L_FULL, 1, 128])
    identd = din("ident", [128, 128])
    cmd = din("cm", [128, 4, 128])
    mask4d = din("mask4", [128, 2, 512])
    maskLd = din("maskL", [128, 2, 512])
    cosd = din("cosT", [128, T])
    sind = din("sinT", [128, T])
    seld = din("sel", [5, 5, 128])
    outd = nc.dram_tensor("out", [NB, SEQ, D], F32, kind="ExternalOutput").ap()
    dbgd = {}
    if dbg:
        for nm, shp, dt_ in (("d_rkvz", [T, 1024], BF16), ("d_mix", [8, 128, T], BF16), ("d_x1", [T, D], F32), ("d_q", [2, 4, 128, T], BF16)):
            dbgd[nm] = nc.dram_tensor(nm, shp, dt_, kind="ExternalOutput").ap()

    x1 = dscr("x1", [NB, T, D], F32)
    wbf = dscr("wbf", [L_FULL, 11, 128, 8, 512], BF16)
    rkvz = dscr("rkvz", [NB, T, 1024], BF16)
    avd = dscr("av", [NB, T, 512], BF16)
    aszd = dscr("asz", [NB, T, 512], BF16)
    qkT = dscr("qkT", [NB, 2, 4, 128, T], BF16)
    mixT = dscr("mixT", [NB, 8, 128, T], BF16)
    woutbf = dscr("woutbf", [L_FULL, 128, 8, D], BF16)
    lwlad = dscr("lwlad", [NB, 2, 128, T], F32)
    B_woutbf = [Buf() for _ in range(L_FULL)]

    def bufs(n):
        return [Buf() for _ in range(n)]

    B_x = {0: [bufs(NT) for _ in range(NB)], 1: [bufs(NT) for _ in range(NB)], 2: [bufs(NT) for _ in range(NB)]}
    B_wbf = [bufs(11) for _ in range(L_FULL)]
    B_rkvz = [bufs(NT) for _ in range(NB)]
    B_av = [bufs(NT) for _ in range(NB)]
    B_asz = [bufs(NT) for _ in range(NB)]
    B_q = [bufs(NT) for _ in range(NB)]
    B_k = [bufs(NT) for _ in range(NB)]
    B_mixR = [bufs(NT) for _ in range(NB)]
    B_mixC = [bufs(NT) for _ in range(NB)]
    B_mixA = [bufs(NT) for _ in range(NB)]
    B_lwla = [bufs(NT) for _ in range(NB)]

    psAll = [P.psum(f"psF{i}", [128, 512], F32) for i in range(8)]
    psBufs = [Buf() for _ in range(8)]
    psF = Rot(psAll[0:6], psBufs[0:6])
    psB = Rot([t[:, :].bitcast(BF16) for t in psAll[6:8]], psBufs[6:8])

    def ctile(name, shape, dt=F32):
        return P.sbuf(name, shape, dt), Buf(name)

    identf, b_identf = ctile("identf", [128, 128])
    identb, b_identb = ctile("identb", [128, 128], BF16)
    cm, b_cm = ctile("cm", [128, 4, 128])
    mask4, b_mask4 = ctile("mask4", [128, 2, 512], BF16)
    maskL, b_maskL = ctile("maskL", [128, 2, 512], BF16)
    cosT, b_cos = ctile("cosT", [128, T], BF16)
    sinT, b_sin = ctile("sinT", [128, T], BF16)
    arF = P.sbuf("arF", [128, 13824], F32)
    arB = P.sbuf("arB", [128, 54272], BF16)

    class Arena:
        def __init__(self):
            self.o = {F32: 0, BF16: 0}

        def reset(self):
            self.o = {F32: 0, BF16: 0}
            P.barrier()

        def alloc(self, shape, dt):
            n = 1
            for x in shape[1:]:
                n *= x
            ar = arF if dt == F32 else arB
            o = self.o[dt]
            assert o + n <= (13824 if dt == F32 else 54272), (shape, dt, o)
            self.o[dt] = o + n
            v = ar[0:shape[0], o:o + n]
            if len(shape) == 3:
                v = v.rearrange("p (a b) -> p a b", a=shape[1])
            elif len(shape) == 4:
                v = v.rearrange("p (a b c) -> p a b c", a=shape[1], b=shape[2])
            return v

        def rot(self, n, shape, dt):
            return Rot([self.alloc(shape, dt) for _ in range(n)])

    AR = Arena()
    sel, b_sel = ctile("sel", [5, 5, 128])
    negcol, b_negcol = ctile("negcol", [128, 1])
    onesrow, b_ones = ctile("onesrow", [1, 128])
    epsc, b_eps = ctile("epsc", [128, 2])
    P.dma("sp", lambda e: e.dma_start(out=identf[:], in_=identd[:, :]), [], [b_identf])
    P.dma("sp", lambda e: e.dma_start(out=cm[:], in_=cmd[:, :, :]), [], [b_cm])
    for (dst, bdst, src, n) in ((mask4, b_mask4, mask4d.rearrange("p a b -> p (a b)"), 1024), (maskL, b_maskL, maskLd.rearrange("p a b -> p (a b)"), 1024),
                                (cosT, b_cos, cosd, T), (sinT, b_sin, sind, T)):
        AR.reset()
        tmpc = AR.alloc([128, n], F32)
        btmp = Buf()
        P.dma("sp", lambda e, tmpc=tmpc, src=src: e.dma_start(out=tmpc, in_=src), [], [btmp])
        dv = dst[:].rearrange("p a b -> p (a b)") if n == 1024 else dst[:]
        P.op("dve", lambda e, dv=dv, tmpc=tmpc: e.tensor_copy(out=dv, in_=tmpc), [btmp], [bdst])
    P.dma("sp", lambda e: e.dma_start(out=sel[:], in_=seld[:, :, :]), [], [b_sel])
    P.op("dve", lambda e: e.tensor_copy(out=identb[:], in_=identf[:]), [b_identf], [b_identb])
    P.op("pool", lambda e: e.memset(negcol[:], DECAY_C), [], [b_negcol])
    P.op("pool", lambda e: e.memset(onesrow[:], 1.0), [], [b_ones])
    P.op("pool", lambda e: e.memset(epsc[:, 0:1], NORM_EPS), [], [b_eps])
    P.op("pool", lambda e: e.memset(epsc[:, 1:2], GN_EPS), [], [b_eps])

    AR.reset()
    wstg = AR.rot(2, [128, 8, 256], F32)
    wcast = AR.rot(2, [128, 8, 256], BF16)
    cast_eng = ["dve", "pool", "act"]
    ci = 0
    for l in range(NL):
        for sb in range(11):
            wfull = 512 if sb != 6 else 256
            c0 = sb * 512 if sb < 6 else (3072 if sb == 6 else 3328 + (sb - 7) * 512)
            for hf in range(wfull // 256):
                st, bst = wstg.next()
                cb, bcb = wcast.next()
                P.dma("sp", lambda e, st=st, l=l, c0=c0, hf=hf: e.dma_start(
                    out=st, in_=wext[l, :, c0 + hf * 256:c0 + (hf + 1) * 256].rearrange("(c p) n -> p c n", p=128)), [], [bst])
                eng = cast_eng[ci % 3]
                ci += 1
                if eng == "act":
                    P.op("act", lambda e, st=st, cb=cb: e.activation(out=cb, in_=st, func=AF.Copy), [bst], [bcb])
                else:
                    P.op(eng, lambda e, st=st, cb=cb: e.tensor_copy(out=cb, in_=st), [bst], [bcb])
                P.dma("pool", lambda e, cb=cb, l=l, sb=sb, hf=hf: e.dma_start(out=wbf[l, sb, :, :, hf * 256:(hf + 1) * 256], in_=cb), [bcb], [B_wbf[l][sb]])

    srow = P.sbuf("srow", [5, D], F32); b_srow = Buf()
    arow = P.sbuf("arow", [5, D], F32); b_arow = Buf()
    grow = P.sbuf("grow", [5, D], F32); b_grow = Buf()
    v256 = P.sbuf("v256", [128, 6, 256], F32); b_v256 = Buf()
    wup = P.sbuf("wup", [128, 2, 256], F32); b_wup = Buf()
    aup = P.sbuf("aup", [128, 2, 256], F32); b_aup = Buf()
    w0r = P.sbuf("w0r", [1, 512], F32); b_w0r = Buf()
    a0r = P.sbuf("a0r", [1, 512], F32); b_a0r = Buf()
    cwc = P.sbuf("cwc", [128, 2, 3], F32); b_cwc = Buf()
    gsub = P.sbuf("gsub", [128, 128], F32); b_gsub = Buf()
    neglam = P.sbuf("neglam", [128, 1], F32); b_neglam = Buf()
    lamt = P.sbuf("lamt", [128, 132], F32); b_lamt = Buf()
    sm_r = Rot([P.sbuf(f"sm{i}", [128, 8], F32) for i in range(12)])

    LAM_INIT = [0.8 - 0.6 * math.exp(-0.3 * l) for l in range(L_FULL)]

    def rstd_from_ms(ms, bms, eps_col, n=1):
        P.op("act", lambda e: e.activation(out=ms, in_=ms, func=AF.Ln, bias=epsc[:, eps_col:eps_col + 1], scale=1.0), [bms, b_eps], [bms])
        P.op("act", lambda e: e.activation(out=ms, in_=ms, func=AF.Exp, scale=-0.5), [bms], [bms])

    def layer_setup(l):
        AR.reset()
        scT = AR.alloc([128, 8, 5], F32); b_scT = Buf()
        modb5 = AR.rot(2, [5, 512], F32)
        preg5 = AR.alloc([5, D], F32); b_preg5 = Buf()
        postg5 = AR.alloc([5, D], F32); b_postg5 = Buf()
        wstg = AR.rot(2, [128, 8, 256], F32)
        for c in range(8):
            P.dma("sp", lambda e, c=c: e.dma_start(out=scT[:, c, :], in_=cc[:, c * 128:(c + 1) * 128].rearrange("b p -> p b"), allow_slow_non_contiguous=True), [], [b_scT])
        P.op("act", lambda e: e.activation(out=scT, in_=scT, func=AF.Silu), [b_scT], [b_scT])
        P.dma("sp", lambda e: e.dma_start(out=preg5, in_=pregd[l, 0:1, :].broadcast_to([5, D])), [], [b_preg5])
        P.dma("sp", lambda e: e.dma_start(out=postg5, in_=postgd[l, 0:1, :].broadcast_to([5, D])), [], [b_postg5])
        dsts = [(srow, b_srow), (arow, b_arow), (grow, b_grow)]
        for cb in range(6):
            mb, bmb = modb5.next()
            P.dma("sp", lambda e, mb=mb, cb=cb: e.dma_start(out=mb, in_=modb[l, 0:1, cb * 512:(cb + 1) * 512].broadcast_to([5, 512])), [], [bmb])
            ps, bps = psF.next()
            for hf in range(2):
                st, bst = wstg.next()
                P.dma("sp", lambda e, st=st, cb=cb, hf=hf: e.dma_start(
                    out=st, in_=modw[l, :, cb * 512 + hf * 256:cb * 512 + (hf + 1) * 256].rearrange("(c p) n -> p c n", p=128)), [], [bst])
                for c in range(8):
                    P.op("pe", lambda e, ps=ps, st=st, c=c, hf=hf: e.matmul(ps[0:5, hf * 256:(hf + 1) * 256], lhsT=scT[:, c, :], rhs=st[:, c, :], start=(c == 0), stop=(c == 7)), [b_scT, bst], [bps])
            dst, bd = dsts[cb // 2]
            P.op("dve", lambda e, ps=ps, cb=cb, dst=dst, mb=mb: e.tensor_tensor(out=dst[:, (cb % 2) * 512:(cb % 2 + 1) * 512], in0=ps[0:5, :], in1=mb, op=ALU.add), [bps, bmb], [bd])
        P.op("dve", lambda e: e.scalar_tensor_tensor(out=arow[:], in0=arow[:], scalar=1.0, in1=preg5, op0=ALU.add, op1=ALU.mult), [b_arow, b_preg5], [b_arow])
        P.op("dve", lambda e: e.tensor_tensor(out=grow[:], in0=grow[:], in1=postg5, op=ALU.mult), [b_grow, b_postg5], [b_grow])
        wcs_r = AR.rot(2, [128, 8, 256], BF16)
        for q4 in range(4):
            st, bst = wstg.next()
            cb, bcb = wcs_r.next()
            P.dma("sp", lambda e, st=st, q4=q4: e.dma_start(out=st, in_=woutd[l, :, q4 * 256:(q4 + 1) * 256].rearrange("(c p) n -> p c n", p=128)), [], [bst])
            P.op("pool", lambda e, st=st, cb=cb: e.tensor_copy(out=cb, in_=st), [bst], [bcb])
            P.dma("pool", lambda e, cb=cb, q4=q4: e.dma_start(out=woutbf[l, :, :, q4 * 256:(q4 + 1) * 256], in_=cb), [bcb], [B_woutbf[l]])
        P.dma("sp", lambda e: e.dma_start(out=v256[:], in_=vec256[l:l + 1, :, :].broadcast_to([128, 6, 256])), [], [b_v256])
        P.op("pool", lambda e: e.memset(wup[:], 0.0), [], [b_wup])
        P.op("pool", lambda e: e.memset(aup[:], 0.0), [], [b_aup])
        for dd in range(2):
            P.dma("sp", lambda e, dd=dd: e.dma_start(out=wup[dd * 64:(dd + 1) * 64, dd, :], in_=wupd[l, dd * 64:(dd + 1) * 64, :]), [], [b_wup])
            P.dma("sp", lambda e, dd=dd: e.dma_start(out=aup[dd * 64:(dd + 1) * 64, dd, :], in_=aupd[l, dd * 64:(dd + 1) * 64, :]), [], [b_aup])
        P.dma("sp", lambda e: e.dma_start(out=w0r[:], in_=w0d[l, 0:1, :]), [], [b_w0r])
        P.dma("sp", lambda e: e.dma_start(out=a0r[:], in_=a0d[l, 0:1, :]), [], [b_a0r])
        for fb in range(2):
            P.dma("sp", lambda e, fb=fb: e.dma_start(out=cwc[:, fb, :], in_=convwd[l, :, fb * 128:(fb + 1) * 128].rearrange("j p -> p j"), allow_slow_non_contiguous=True), [], [b_cwc])
        P.dma("sp", lambda e: e.dma_start(out=gsub[:], in_=subgd[l, 0:1, :].broadcast_to([128, 128])), [], [b_gsub])
        P.op("dve", lambda e: e.tensor_scalar_mul(out=gsub[:], in0=gsub[:], scalar1=1.0 - LAM_INIT[l]), [b_gsub], [b_gsub])
        P.op("dve", lambda e: e.tensor_tensor(out=lamt[:, 0:64], in0=v256[:, 5, 0:64], in1=v256[:, 5, 64:128], op=ALU.mult), [b_v256], [b_lamt])
        P.op("dve", lambda e: e.tensor_tensor(out=lamt[:, 64:128], in0=v256[:, 5, 128:192], in1=v256[:, 5, 192:256], op=ALU.mult), [b_v256], [b_lamt])
        P.op("dve", lambda e: e.reduce_sum(out=lamt[:, 128:130], in_=lamt[:, 0:128].rearrange("p (a b) -> p a b", a=2), axis=AX.X), [b_lamt], [b_lamt])
        P.op("act", lambda e: e.activation(out=lamt[:, 130:132], in_=lamt[:, 128:130], func=AF.Exp), [b_lamt], [b_lamt])
        P.op("dve", lambda e: e.tensor_tensor(out=neglam[:], in0=lamt[:, 131:132], in1=lamt[:, 130:131], op=ALU.subtract), [b_lamt], [b_neglam])
        P.op("dve", lambda e: e.tensor_scalar_add(out=neglam[:], in0=neglam[:], scalar1=-LAM_INIT[l]), [b_neglam], [b_neglam])

    def bcast_rows(row, tiles, tb, which=(0, 1, 2)):
        srcs = [(arow, b_arow, None), (srow, b_srow, None), (grow, b_grow, None)]
        for qi, (src, bsrc, _) in enumerate(srcs):
            if qi not in which:
                continue
            for hf in range(2):
                ps, bps = psF.next()
                P.op("pe", lambda e, ps=ps, src=src, hf=hf: e.matmul(ps[:, :], lhsT=sel[0:5, row, :], rhs=src[0:5, hf * 512:(hf + 1) * 512], start=True, stop=True), [b_sel, bsrc], [bps])
                P.op("act", lambda e, ps=ps, qi=qi, hf=hf: e.activation(out=tiles[qi][:, hf * 512:(hf + 1) * 512], in_=ps[:, :], func=AF.Copy), [bps], [tb[qi]])

    def phase_a(l, b):
        xsrc, Bx = (xall, B_x[0]) if l == 0 else (x1, B_x[1])
        AR.reset()
        hT = AR.alloc([128, 8, 1152], BF16); b_hT = Buf()
        xt_r = AR.rot(2, [128, D], F32)
        f32a = AR.rot(2, [128, D], F32)
        t512 = AR.rot(4, [128, 512], F32)
        hb_r = AR.rot(2, [128, D], BF16)
        junk_r = AR.rot(2, [128, D], BF16)
        wblk_r = AR.rot(2, [128, 8, 512], BF16)
        stg_r = AR.rot(3, [128, 1024], BF16)
        ures = AR.alloc([128, 2, T], BF16); b_ures = Buf()
        bzres = AR.alloc([128, 2, T], BF16); b_bzres = Buf()
        bcC = [AR.alloc([128, D], F32) if i < 2 else None for i in range(3)]; b_bcC = bufs(3)
        bcB = [AR.alloc([128, D], F32) if i < 2 else None for i in range(3)]; b_bcB = bufs(3)
        bcast_rows(4, bcC, b_bcC, which=(0, 1))
        bcast_rows(b, bcB, b_bcB, which=(0, 1))
        for part in range(2):
            t0 = part * 9
            for ti in range(9):
                t = t0 + ti
                A, S = (bcC, b_bcC) if t < 2 else (bcB, b_bcB)
                xt, bxt = xt_r.next()
                P.dma("sp", lambda e, xt=xt, t=t: e.dma_start(out=xt[:], in_=xsrc[b, t * 128:(t + 1) * 128, :]), [Bx[b][t]], [bxt])
                jk, bjk = junk_r.next()
                sm, bsm = sm_r.next()
                P.op("act", lambda e, xt=xt, jk=jk, sm=sm: e.activation(out=jk[:], in_=xt[:], func=AF.Square, scale=1.0 / 32.0, accum_out=sm[:, 0:1]), [bxt], [bjk, bsm])
                rstd_from_ms(sm[:, 0:1], bsm, 0)
                fa, bfa = f32a.next()
                P.op("dve", lambda e, fa=fa, xt=xt, sm=sm, A=A: e.scalar_tensor_tensor(out=fa[:], in0=xt[:], scalar=sm[:, 0:1], in1=A[0][:], op0=ALU.mult, op1=ALU.mult), [bxt, bsm, S[0]], [bfa])
                hb, bhb = hb_r.next()
                P.op("pool", lambda e, hb=hb, fa=fa, A=A: e.tensor_tensor(out=hb[:], in0=fa[:], in1=A[1][:], op=ALU.add), [bfa, S[1]], [bhb])
                pb, bpb = psB.next()
                for c in range(8):
                    P.op("pe", lambda e, pb=pb, hb=hb, c=c: e.transpose(out=pb[:, c * 128:(c + 1) * 128], in_=hb[:, c * 128:(c + 1) * 128], identity=identb[:]), [bhb, b_identb], [bpb])
                P.op("act", lambda e, pb=pb, ti=ti: e.activation(out=hT[:, :, ti * 128:(ti + 1) * 128], in_=pb[:, :].rearrange("p (c t) -> p c t", c=8), func=AF.Copy), [bpb], [b_hT])
            groups = [(0, 4), (4, 4), (8, 1)]
            for sb in range(11):
                wb, bwb = wblk_r.next()
                w = 512 if sb != 6 else 256
                P.dma("sp", lambda e, wb=wb, sb=sb, w=w: e.dma_start(out=wb[:, :, 0:w], in_=wbf[l, sb, :, :, 0:w]), [B_wbf[l][sb]], [bwb])
                if sb < 7:
                    nblk = 4 if sb < 6 else 2
                    k = 0
                    while k < nblk:
                        fm = sb * 4 + k
                        pair = fm >= 10
                        for (g0, gn) in groups:
                            ntok = gn * 128
                            tok0 = (t0 + g0) * 128
                            loc0 = g0 * 128
                            tiles_g = list(range(t0 + g0, t0 + g0 + gn))
                            ps, bps = psF.next()
                            for c in range(8):
                                P.op("pe", lambda e, ps=ps, wb=wb, c=c, k=k, loc0=loc0, ntok=ntok: e.matmul(ps[:, 0:ntok], lhsT=wb[:, c, k * 128:(k + 1) * 128], rhs=hT[:, c, loc0:loc0 + ntok], start=(c == 0), stop=(c == 7)), [bwb, b_hT], [bps])
                            if pair:
                                ps2, bps2 = psF.next()
                                for c in range(8):
                                    P.op("pe", lambda e, ps2=ps2, wb=wb, c=c, k=k, loc0=loc0, ntok=ntok: e.matmul(ps2[:, 0:ntok], lhsT=wb[:, c, (k + 1) * 128:(k + 2) * 128], rhs=hT[:, c, loc0:loc0 + ntok], start=(c == 0), stop=(c == 7)), [bwb, b_hT], [bps2])
                                ta, bta = t512.next()
                                tb_, btb = t512.next()
                                P.op("dve", lambda e, ta=ta, ps=ps, tok0=tok0, ntok=ntok: e.tensor_tensor(out=ta[:, 0:ntok], in0=ps[:, 0:ntok], in1=cosT[:, tok0:tok0 + ntok], op=ALU.mult), [bps, b_cos], [bta])
                                P.op("dve", lambda e, tb_=tb_, ps2=ps2, tok0=tok0, ntok=ntok: e.tensor_tensor(out=tb_[:, 0:ntok], in0=ps2[:, 0:ntok], in1=sinT[:, tok0:tok0 + ntok], op=ALU.mult), [bps2, b_sin], [btb])
                                sg, bsg = stg_r.next()
                                P.op("pool", lambda e, sg=sg, ta=ta, tb_=tb_, ntok=ntok: e.tensor_tensor(out=sg[:, 0:ntok], in0=ta[:, 0:ntok], in1=tb_[:, 0:ntok], op=ALU.add), [bta, btb], [bsg])
                                qk = 0 if fm < 18 else 1
                                hh = ((fm - 10) // 2) % 4
                                Bq = (B_q if qk == 0 else B_k)[b]
                                P.dma("pool", lambda e, sg=sg, qk=qk, hh=hh, tok0=tok0, ntok=ntok: e.dma_start(out=qkT[b, qk, hh, :, tok0:tok0 + ntok], in_=sg[:, 0:ntok]), [bsg], [Bq[t] for t in tiles_g])
                            elif fm in (0, 1):
                                ta, bta = t512.next()
                                P.op("act", lambda e, ps=ps, ta=ta: e.activation(out=ta[:, 0:ntok], in_=ps[:, 0:ntok], func=(AF.Tanh if fm == 0 else AF.Copy)), [bps], [bta])
                                P.dma("pool", lambda e, ta=ta: e.dma_start(out=lwlad[b, fm, :, tok0:tok0 + ntok], in_=ta[:, 0:ntok]), [bta], [B_lwla[b][t] for t in tiles_g])
                            elif fm in (2, 3):
                                P.op("act", lambda e, ps=ps, fm=fm, tok0=tok0, ntok=ntok: e.activation(out=ures[:, fm - 2, tok0:tok0 + ntok], in_=ps[:, 0:ntok], func=AF.Copy), [bps], [b_ures])
                            elif fm in (4, 5):
                                P.op("dve", lambda e, ps=ps, fm=fm, tok0=tok0, ntok=ntok: e.tensor_tensor(out=ures[:, fm - 4, tok0:tok0 + ntok], in0=ps[:, 0:ntok], in1=ures[:, fm - 4, tok0:tok0 + ntok], op=ALU.mult), [bps, b_ures], [b_ures])
                            elif fm in (6, 7):
                                P.op("act", lambda e, ps=ps, fm=fm, tok0=tok0, ntok=ntok: e.activation(out=bzres[:, fm - 6, tok0:tok0 + ntok], in_=ps[:, 0:ntok], func=AF.Silu), [bps], [b_bzres])
                            elif fm in (8, 9):
                                P.op("dve", lambda e, ps=ps, fm=fm, tok0=tok0, ntok=ntok: e.tensor_tensor(out=bzres[:, fm - 8, tok0:tok0 + ntok], in0=ps[:, 0:ntok], in1=bzres[:, fm - 8, tok0:tok0 + ntok], op=ALU.mult), [bps, b_bzres], [b_bzres])
                        k += 2 if pair else 1
                else:
                    tmb = sb - 7
                    for ti in range(9):
                        t = t0 + ti
                        ps, bps = psF.next()
                        for c in range(8):
                            P.op("pe", lambda e, ps=ps, wb=wb, c=c, ti=ti: e.matmul(ps[:, :], lhsT=hT[:, c, ti * 128:(ti + 1) * 128], rhs=wb[:, c, :], start=(c == 0), stop=(c == 7)), [bwb, b_hT], [bps])
                        sg, bsg = stg_r.next()
                        if tmb == 3:
                            P.op("act", lambda e, sg=sg, ps=ps: e.activation(out=sg[:, 0:512], in_=ps[:, :], func=AF.Silu), [bps], [bsg])
                            P.dma("pool", lambda e, sg=sg, t=t: e.dma_start(out=aszd[b, t * 128:(t + 1) * 128, :], in_=sg[:, 0:512]), [bsg], [B_asz[b][t]])
                        elif tmb == 2:
                            P.op("dve", lambda e, sg=sg, ps=ps: e.tensor_copy(out=sg[:, 0:512], in_=ps[:, :]), [bps], [bsg])
                            P.dma("pool", lambda e, sg=sg, t=t: e.dma_start(out=avd[b, t * 128:(t + 1) * 128, :], in_=sg[:, 0:512]), [bsg], [B_av[b][t]])
                        else:
                            if tmb == 0:
                                P.op("act", lambda e, sg=sg, ps=ps: e.activation(out=sg[:, 0:512], in_=ps[:, :], func=AF.Copy), [bps], [bsg])
                            else:
                                P.op("dve", lambda e, sg=sg, ps=ps: e.tensor_copy(out=sg[:, 0:512], in_=ps[:, :]), [bps], [bsg])
                            P.dma("pool", lambda e, sg=sg, t=t, tmb=tmb: e.dma_start(out=rkvz[b, t * 128:(t + 1) * 128, tmb * 512:(tmb + 1) * 512], in_=sg[:, 0:512]), [bsg], [B_rkvz[b][t]])

        return ures, b_ures, bzres, b_bzres

    def phase_conv(l, b, ures, b_ures, bzres, b_bzres):
        cvt = AR.alloc([128, SEQ], F32); b_cvt = Buf()
        cvs = AR.alloc([128, SEQ], BF16); b_cvs = Buf()
        ranges = ([(0, CTX)] if l == 0 else []) + [(CTX, T)]
        for (r0, r1) in ranges:
            n = r1 - r0
            for fb in range(2):
                P.op("pool", lambda e, fb=fb, r0=r0, n=n: e.tensor_scalar(out=cvt[:, 0:n], in0=ures[:, fb, r0:r0 + n], scalar1=cwc[:, fb, 1:2], scalar2=None, op0=ALU.mult), [b_ures, b_cwc], [b_cvt])
                P.op("dve", lambda e, fb=fb, r0=r0, n=n: e.scalar_tensor_tensor(out=cvt[:, 1:n], in0=ures[:, fb, r0:r0 + n - 1], scalar=cwc[:, fb, 0:1], in1=cvt[:, 1:n], op0=ALU.mult, op1=ALU.add), [b_ures, b_cwc, b_cvt], [b_cvt])
                P.op("dve", lambda e, fb=fb, r0=r0, n=n: e.scalar_tensor_tensor(out=cvt[:, 0:n - 1], in0=ures[:, fb, r0 + 1:r0 + n], scalar=cwc[:, fb, 2:3], in1=cvt[:, 0:n - 1], op0=ALU.mult, op1=ALU.add), [b_ures, b_cwc, b_cvt], [b_cvt])
                P.op("pool", lambda e, fb=fb, r0=r0, n=n: e.tensor_tensor(out=cvs[:, 0:n], in0=cvt[:, 0:n], in1=bzres[:, fb, r0:r0 + n], op=ALU.mult), [b_cvt, b_bzres], [b_cvs])
                P.dma("pool", lambda e, fb=fb, r0=r0, n=n: e.dma_start(out=mixT[b, 2 + fb, :, r0:r0 + n], in_=cvs[:, 0:n]), [b_cvs], [B_mixC[b][t] for t in range(r0 // 128, r1 // 128)])

    def bc4(ap4):
        return ap4.unsqueeze(2).to_broadcast([128, 4, 64])

    def v3(ap):
        return ap.rearrange("p (h e) -> p h e", h=4)

    def phase_rwkv(l, b):
        AR.reset()
        Yf = AR.alloc([128, NT, 256], F32); b_Yf = bufs(NT)
        Ksum = AR.alloc([128, NT, 256], BF16); b_Ksum = bufs(NT)
        done = set()

        def mkpools():
            pl = {}
            pl["f256"] = AR.rot(12, [128, 256], F32)
            pl["e12_r"] = AR.rot(2, [128, 512], F32)
            pl["lwc_r"] = AR.rot(2, [128, 2, 128], F32)
            pl["rk_r"] = AR.rot(2, [128, 1024], BF16)
            pl["b256"] = AR.rot(14, [128, 256], BF16)
            pl["FMz_r"] = [AR.rot(2, [128, 8, 128], BF16) for _ in range(2)]
            for par in range(2):
                for (tl, tb_) in zip(pl["FMz_r"][par].tiles, pl["FMz_r"][par].bufs):
                    P.op("pool", lambda e, tl=tl: e.memset(tl, 0.0), [], [tb_])
            pl["XT_r"] = AR.rot(2, [128, 4, 512], BF16)
            pl["Lp_r"] = AR.rot(3, [128, 4, 128], BF16)
            pl["LpT_r"] = AR.rot(3, [128, 4, 128], BF16)
            pl["Z_r"] = AR.rot(2, [128, 4, 128], BF16)
            pl["PhiT_r"] = AR.rot(2, [64, 4, 64], BF16)
            pl["RhT_r"] = AR.rot(2, [64, 4, 128], BF16)
            pl["H_r"] = AR.rot(2, [64, 4, 64], BF16)
            pl["ro_r"] = AR.rot(2, [128, 2, 128], BF16)
            return pl
        kkb, kab, rkb, lngb, lnbb = (v256[:, i, :] for i in range(5))

        def lane(d):
            pl = mkpools()
            f256, e12_r, lwc_r, rk_r, b256, FMz_r = pl["f256"], pl["e12_r"], pl["lwc_r"], pl["rk_r"], pl["b256"], pl["FMz_r"]
            XT_r, Lp_r, LpT_r, Z_r, PhiT_r, RhT_r, H_r, ro_r = pl["XT_r"], pl["Lp_r"], pl["LpT_r"], pl["Z_r"], pl["PhiT_r"], pl["RhT_r"], pl["H_r"], pl["ro_r"]
            lps = Rot(psAll[4 * d:4 * d + 4], psBufs[4 * d:4 * d + 4])

            def pn():
                return lps.next()

            def pnb():
                t_, b_ = lps.next()
                return t_[:, :].bitcast(BF16), b_
            order = (list(range(NT)) if d == 0 else [1, 0] + list(range(NT - 1, 1, -1)))[:RW_NCH]
            H, bH = H_r.next()
            P.op("pool", lambda e, H=H: e.memset(H[:], 0.0), [], [bH])
            i_incl, i_strict, i_rem = (0, 2, 3) if d == 0 else (1, 3, 2)
            for ch in order:
                tk = slice(ch * 128, (ch + 1) * 128)
                rk, brk = rk_r.next()
                P.dma("sp", lambda e, rk=rk, ch=ch: e.dma_start(out=rk[:], in_=rkvz[b, ch * 128:(ch + 1) * 128, :]), [B_rkvz[b][ch]], [brk])
                lwc, b_lwla = lwc_r.next()
                P.dma("sp", lambda e, lwc=lwc, ch=ch: e.dma_start(out=lwc, in_=lwlad[b, :, :, ch * 128:(ch + 1) * 128].rearrange("k p t -> p k t")), [B_lwla[b][ch]], [b_lwla])
                r_, k_, v_, z_ = (rk[:, i * 256:(i + 1) * 256] for i in range(4))
                t1, bt1 = f256.next()
                P.op("dve", lambda e, t1=t1, k_=k_: e.tensor_tensor(out=t1[:], in0=k_, in1=kkb, op=ALU.mult), [brk, b_v256], [bt1])
                t2, bt2 = f256.next()
                P.op("pool", lambda e, t1=t1, t2=t2: e.tensor_tensor(out=t2[:], in0=t1[:], in1=t1[:], op=ALU.mult), [bt1], [bt2])
                sm, bsm = sm_r.next()
                P.op("dve", lambda e, sm=sm, t2=t2: e.reduce_sum(out=sm[:, 0:4], in_=v3(t2[:]), axis=AX.X), [bt2], [bsm])
                P.op("dve", lambda e, sm=sm: e.tensor_scalar_max(out=sm[:, 0:4], in0=sm[:, 0:4], scalar1=1e-24), [bsm], [bsm])
                P.op("act", lambda e, sm=sm: e.activation(out=sm[:, 0:4], in_=sm[:, 0:4], func=AF.Ln), [bsm], [bsm])
                P.op("act", lambda e, sm=sm: e.activation(out=sm[:, 0:4], in_=sm[:, 0:4], func=AF.Exp, scale=-0.5), [bsm], [bsm])
                kk, bkk = f256.next()
                P.op("dve", lambda e, kk=kk, t1=t1, sm=sm: e.tensor_tensor(out=v3(kk[:]), in0=v3(t1[:]), in1=bc4(sm[:, 0:4]), op=ALU.mult), [bt1, bsm], [bkk])
                yield
                psA, bpsA = pn()
                pp = slice(d * 64, (d + 1) * 64)
                P.op("pe", lambda e, psA=psA: e.matmul(psA[:, 0:256], lhsT=lwc[:, 1, :], rhs=aup[:, d, :], start=True, stop=False), [b_lwla, b_aup], [bpsA])
                P.op("pe", lambda e, psA=psA: e.matmul(psA[:, 0:256], lhsT=onesrow[0:1, :], rhs=a0r[0:1, d * 256:(d + 1) * 256], start=False, stop=True), [b_ones, b_a0r], [bpsA])
                P.op("pe", lambda e, psA=psA: e.matmul(psA[:, 256:512], lhsT=lwc[:, 0, :], rhs=wup[:, d, :], start=True, stop=False), [b_lwla, b_wup], [bpsA])
                P.op("pe", lambda e, psA=psA: e.matmul(psA[:, 256:512], lhsT=onesrow[0:1, :], rhs=w0r[0:1, d * 256:(d + 1) * 256], start=False, stop=True), [b_ones, b_w0r], [bpsA])
                asg, basg = e12_r.next()
                P.op("act", lambda e, asg=asg, psA=psA: e.activation(out=asg[:], in_=psA[:, :], func=AF.Sigmoid), [bpsA], [basg])
                a_ = asg[:, 0:256]
                sg_ = asg[:, 256:512]
                psX, bpsX = pn()
                psY, bpsY = pn()
                P.op("pe", lambda e, psX=psX: e.matmul(psX[:, 0:256], lhsT=cm[:, i_incl, :], rhs=sg_, start=True, stop=True), [b_cm, basg], [bpsX])
                P.op("pe", lambda e, psX=psX: e.matmul(psX[:, 256:512], lhsT=cm[:, i_strict, :], rhs=sg_, start=True, stop=True), [b_cm, basg], [bpsX])
                P.op("pe", lambda e, psY=psY: e.matmul(psY[:, 0:256], lhsT=cm[:, i_rem, :], rhs=sg_, start=True, stop=True), [b_cm, basg], [bpsY])
                for h in range(4):
                    P.op("pe", lambda e, psY=psY, h=h: e.matmul(psY[0:64, 256 + h:257 + h], lhsT=asg[:, 256 + h * 64:256 + (h + 1) * 64], rhs=negcol[:, 0:1], start=True, stop=True), [basg, b_negcol], [bpsY])
                e12, be12 = e12_r.next()
                P.op("act", lambda e, e12=e12, psX=psX: e.activation(out=e12[:], in_=psX[:, :], func=AF.Exp), [bpsX], [be12])
                encw, bencw = f256.next()
                P.op("act", lambda e, encw=encw, psX=psX: e.activation(out=encw[:], in_=psX[:, 0:256], func=AF.Exp, scale=-1.0), [bpsX], [bencw])
                erem, berem = f256.next()
                P.op("act", lambda e, erem=erem, psY=psY: e.activation(out=erem[:], in_=psY[:, 0:256], func=AF.Exp), [bpsY], [berem])
                wcs, bwcs = sm_r.next()
                P.op("act", lambda e, wcs=wcs, psY=psY: e.activation(out=wcs[0:64, 0:4], in_=psY[0:64, 256:260], func=AF.Exp), [bpsY], [bwcs])
                yield
                tt, btt = f256.next()
                P.op("dve", lambda e, tt=tt: e.scalar_tensor_tensor(out=tt[:], in0=a_, scalar=-1.0, in1=kab, op0=ALU.add, op1=ALU.mult), [basg, b_v256], [btt])
                kmod, bkmod = f256.next()
                P.op("dve", lambda e, tt=tt, kmod=kmod, k_=k_: e.scalar_tensor_tensor(out=kmod[:], in0=tt[:], scalar=1.0, in1=k_, op0=ALU.add, op1=ALU.mult), [btt, brk], [bkmod])
                bq, bbq = f256.next()
                P.op("pool", lambda e, bq=bq, kk=kk: e.tensor_tensor(out=bq[:], in0=kk[:], in1=a_, op=ALU.mult), [bkk, basg], [bbq])
                At, bAt = b256.next()
                P.op("dve", lambda e, At=At, kk=kk, e12=e12: e.scalar_tensor_tensor(out=At[:], in0=kk[:], scalar=-1.0, in1=e12[:, 256:512], op0=ALU.mult, op1=ALU.mult), [bkk, be12], [bAt])
                Rt, bRt = b256.next()
                P.op("dve", lambda e, Rt=Rt, e12=e12, r_=r_: e.tensor_tensor(out=Rt[:], in0=r_, in1=e12[:, 0:256], op=ALU.mult), [brk, be12], [bRt])
                Bt, bBt = b256.next()
                P.op("pool", lambda e, Bt=Bt, bq=bq, encw=encw: e.tensor_tensor(out=Bt[:], in0=bq[:], in1=encw[:], op=ALU.mult), [bbq, bencw], [bBt])
                Kt, bKt = b256.next()
                P.op("dve", lambda e, Kt=Kt, kmod=kmod, encw=encw: e.tensor_tensor(out=Kt[:], in0=kmod[:], in1=encw[:], op=ALU.mult), [bkmod, bencw], [bKt])
                Bb, bBb = b256.next()
                P.op("pool", lambda e, Bb=Bb, bq=bq, erem=erem: e.tensor_tensor(out=Bb[:], in0=bq[:], in1=erem[:], op=ALU.mult), [bbq, berem], [bBb])
                Kb, bKb = b256.next()
                P.op("pool", lambda e, Kb=Kb, kmod=kmod, erem=erem: e.tensor_tensor(out=Kb[:], in0=kmod[:], in1=erem[:], op=ALU.mult), [bkmod, berem], [bKb])
                is_first = ch not in done
                if is_first:
                    P.op("pool", lambda e, kmod=kmod, ch=ch: e.tensor_copy(out=Ksum[:, ch, :], in_=kmod[:]), [bkmod], [b_Ksum[ch]])
                yield
                pb, bpb = pnb()
                for qi, (src, bsrc) in enumerate(((At, bAt), (Rt, bRt), (Bt, bBt), (Kt, bKt))):
                    for fb in range(2):
                        P.op("pe", lambda e, pb=pb, src=src, fb=fb, qi=qi: e.transpose(out=pb[:, (fb * 4 + qi) * 128:(fb * 4 + qi + 1) * 128], in_=src[:, fb * 128:(fb + 1) * 128], identity=identb[:]), [bsrc, b_identb], [bpb])
                FMz = [FMz_r[0].next(), FMz_r[1].next()]
                P.op("act", lambda e, FMz=FMz, pb=pb: e.activation(out=FMz[0][0][0:64], in_=pb[0:64, :].rearrange("p (c t) -> p c t", c=8), func=AF.Copy), [bpb], [FMz[0][1]])
                P.op("dve", lambda e, FMz=FMz, pb=pb: e.tensor_copy(out=FMz[1][0][64:128], in_=pb[64:128, :].rearrange("p (c t) -> p c t", c=8)), [bpb], [FMz[1][1]])
                yield
                XT, bXT = XT_r.next()
                for h in range(4):
                    fb = h // 2
                    FMq, bFMq = FMz[h % 2]
                    ps, bps = pn()
                    P.op("pe", lambda e, ps=ps, FMq=FMq, fb=fb: e.matmul(ps[:, 0:256], lhsT=FMq[:, fb * 4 + 2, :], rhs=FMq[:, fb * 4:fb * 4 + 2, :], start=True, stop=True), [bFMq], [bps])
                    P.op("pe", lambda e, ps=ps, FMq=FMq, fb=fb: e.matmul(ps[:, 256:512], lhsT=FMq[:, fb * 4 + 3, :], rhs=FMq[:, fb * 4:fb * 4 + 2, :], start=True, stop=True), [bFMq], [bps])
                    P.op("dve", lambda e, ps=ps, XT=XT, h=h: e.tensor_tensor(out=XT[:, h, :], in0=ps[:, :], in1=mask4[:, d, :], op=ALU.mult), [bps, b_mask4], [bXT])
                psL, bpsL = pn()
                for h in range(4):
                    fb = h // 2
                    FMq, bFMq = FMz[h % 2]
                    P.op("pe", lambda e, psL=psL, FMq=FMq, fb=fb, h=h: e.matmul(psL[:, h * 128:(h + 1) * 128], lhsT=FMq[:, fb * 4 + 0, :], rhs=FMq[:, fb * 4 + 2, :], start=True, stop=True), [bFMq], [bpsL])
                Lp, bLp = Lp_r.next()
                P.op("dve", lambda e, Lp=Lp, psL=psL: e.tensor_tensor(out=Lp[:].rearrange("p h t -> p (h t)"), in0=psL[:, :], in1=maskL[:, d, :], op=ALU.mult), [bpsL, b_maskL], [bLp])
                yield
                psP, bpsP = pn()
                for h in range(4):
                    P.op("pe", lambda e, psP=psP, XT=XT, h=h, rk=rk: e.matmul(psP[:, h * 64:(h + 1) * 64], lhsT=XT[:, h, 256:384], rhs=rk[:, 512 + h * 64:512 + (h + 1) * 64], start=True, stop=True), [bXT, brk], [bpsP])
                Z, bZ = Z_r.next()
                P.op("pool", lambda e, Z=Z, At=At: e.tensor_copy(out=Z[:, :, 0:64], in_=v3(At[:])), [bAt], [bZ])
                P.op("act", lambda e, Z=Z, psP=psP: e.activation(out=Z[:, :, 64:128], in_=v3(psP[:, 0:256]), func=AF.Copy), [bpsP], [bZ])
                yield
                LpT_first = True
                LpT, bLpT = None, None
                for j in range(7):
                    psZ, bpsZ = pn()
                    for h in range(4):
                        lt = XT[:, h, 0:128] if LpT_first else LpT[:, h, :]
                        blt = bXT if LpT_first else bLpT
                        P.op("pe", lambda e, psZ=psZ, lt=lt, Z=Z, h=h: e.matmul(psZ[:, h * 128:(h + 1) * 128], lhsT=lt, rhs=Z[:, h, :], start=True, stop=True), [blt, bZ], [bpsZ])
                    P.op("dve", lambda e, Z=Z, psZ=psZ: e.tensor_tensor(out=Z[:].rearrange("p h t -> p (h t)"), in0=psZ[:, :], in1=Z[:].rearrange("p h t -> p (h t)"), op=ALU.add), [bpsZ, bZ], [bZ])
                    if j < 6:
                        ps1, bps1 = pn()
                        ps2, bps2 = pn()
                        for h in range(4):
                            lt = XT[:, h, 0:128] if LpT_first else LpT[:, h, :]
                            blt = bXT if LpT_first else bLpT
                            P.op("pe", lambda e, ps1=ps1, lt=lt, Lp=Lp, h=h: e.matmul(ps1[:, h * 128:(h + 1) * 128], lhsT=lt, rhs=Lp[:, h, :], start=True, stop=True), [blt, bLp], [bps1])
                            P.op("pe", lambda e, ps2=ps2, lt=lt, Lp=Lp, h=h: e.matmul(ps2[:, h * 128:(h + 1) * 128], lhsT=Lp[:, h, :], rhs=lt, start=True, stop=True), [blt, bLp], [bps2])
                        nLp, bnLp = Lp_r.next()
                        nLpT, bnLpT = LpT_r.next()
                        P.op("act", lambda e, nLp=nLp, ps1=ps1: e.activation(out=nLp[:].rearrange("p h t -> p (h t)"), in_=ps1[:, :], func=AF.Copy), [bps1], [bnLp])
                        P.op("dve", lambda e, nLpT=nLpT, ps2=ps2: e.tensor_copy(out=nLpT[:].rearrange("p h t -> p (h t)"), in_=ps2[:, :]), [bps2], [bnLpT])
                        Lp, bLp, LpT, bLpT = nLp, bnLp, nLpT, bnLpT
                        LpT_first = False
                    yield
                yield
                psFh, bpsFh = pn()
                psR, bpsR = pn()
                for h in range(4):
                    P.op("pe", lambda e, psFh=psFh, Z=Z, Bb=Bb, h=h: e.matmul(psFh[0:64, h * 64:(h + 1) * 64], lhsT=Z[:, h, 0:64], rhs=Bb[:, h * 64:(h + 1) * 64], start=True, stop=True), [bZ, bBb], [bpsFh])
                    P.op("pe", lambda e, psR=psR, Z=Z, XT=XT, h=h: e.matmul(psR[0:64, h * 128:(h + 1) * 128], lhsT=Z[:, h, 0:64], rhs=XT[:, h, 128:256], start=True, stop=False), [bZ, bXT], [bpsR])
                    P.op("pe", lambda e, psR=psR, Rt=Rt, h=h: e.matmul(psR[0:64, h * 128:(h + 1) * 128], lhsT=Rt[:, h * 64:(h + 1) * 64], rhs=identb[:], start=False, stop=True), [bRt, b_identb], [bpsR])
                PhiT, bPhiT = PhiT_r.next()
                for h in range(4):
                    P.op("dve", lambda e, PhiT=PhiT, psFh=psFh, wcs=wcs, h=h: e.scalar_tensor_tensor(out=PhiT[:, h, :], in0=identf[0:64, 0:64], scalar=wcs[0:64, h:h + 1], in1=psFh[0:64, h * 64:(h + 1) * 64], op0=ALU.mult, op1=ALU.add), [bpsFh, bwcs, b_identf], [bPhiT])
                RhT, bRhT = RhT_r.next()
                P.op("act", lambda e, RhT=RhT, psR=psR: e.activation(out=RhT[:].rearrange("p h t -> p (h t)"), in_=psR[0:64, :], func=AF.Copy), [bpsR], [bRhT])
                yield
                psYo, bpsYo = pn()
                psH, bpsH = pn()
                for h in range(4):
                    vh = rk[:, 512 + h * 64:512 + (h + 1) * 64]
                    P.op("pe", lambda e, psYo=psYo, XT=XT, Z=Z, h=h: e.matmul(psYo[:, h * 64:(h + 1) * 64], lhsT=XT[:, h, 128:256], rhs=Z[:, h, 64:128], start=True, stop=False), [bXT, bZ], [bpsYo])
                    P.op("pe", lambda e, psYo=psYo, XT=XT, vh=vh, h=h: e.matmul(psYo[:, h * 64:(h + 1) * 64], lhsT=XT[:, h, 384:512], rhs=vh, start=False, stop=False), [bXT, brk], [bpsYo])
                    P.op("pe", lambda e, psYo=psYo, RhT=RhT, H=H, h=h: e.matmul(psYo[:, h * 64:(h + 1) * 64], lhsT=RhT[:, h, :], rhs=H[:, h, :], start=False, stop=True), [bRhT, bH], [bpsYo])
                    P.op("pe", lambda e, psH=psH, PhiT=PhiT, H=H, h=h: e.matmul(psH[0:64, h * 64:(h + 1) * 64], lhsT=PhiT[:, h, :], rhs=H[:, h, :], start=True, stop=False), [bPhiT, bH], [bpsH])
                    P.op("pe", lambda e, psH=psH, Bb=Bb, Z=Z, h=h: e.matmul(psH[0:64, h * 64:(h + 1) * 64], lhsT=Bb[:, h * 64:(h + 1) * 64], rhs=Z[:, h, 64:128], start=False, stop=False), [bBb, bZ], [bpsH])
                    P.op("pe", lambda e, psH=psH, Kb=Kb, vh=vh, h=h: e.matmul(psH[0:64, h * 64:(h + 1) * 64], lhsT=Kb[:, h * 64:(h + 1) * 64], rhs=vh, start=False, stop=True), [bKb, brk], [bpsH])
                nH, bnH = H_r.next()
                P.op("act", lambda e, nH=nH, psH=psH: e.activation(out=nH[:].rearrange("p h t -> p (h t)"), in_=psH[0:64, 0:256], func=AF.Copy), [bpsH], [bnH])
                H, bH = nH, bnH
                if is_first:
                    P.op("dve", lambda e, psYo=psYo, ch=ch: e.tensor_copy(out=Yf[:, ch, :], in_=psYo[:, 0:256]), [bpsYo], [b_Yf[ch]])
                    done.add(ch)
                    yield
                    continue
                if l == NL - 1 and ch < 2:
                    yield
                    continue
                yield
                y, by = f256.next()
                P.op("dve", lambda e, y=y, psYo=psYo, ch=ch: e.tensor_tensor(out=y[:], in0=psYo[:, 0:256], in1=Yf[:, ch, :], op=ALU.add), [bpsYo, b_Yf[ch]], [by])
                s1, bs1 = sm_r.next()
                P.op("dve", lambda e, s1=s1, y=y: e.reduce_sum(out=s1[:, 0:4], in_=v3(y[:]), axis=AX.X), [by], [bs1])
                P.op("dve", lambda e, s1=s1: e.tensor_scalar_mul(out=s1[:, 0:4], in0=s1[:, 0:4], scalar1=-1.0 / 64.0), [bs1], [bs1])
                yc, byc = f256.next()
                P.op("dve", lambda e, yc=yc, y=y, s1=s1: e.tensor_tensor(out=v3(yc[:]), in0=v3(y[:]), in1=bc4(s1[:, 0:4]), op=ALU.add), [by, bs1], [byc])
                sq, bsq = f256.next()
                P.op("pool", lambda e, sq=sq, yc=yc: e.tensor_tensor(out=sq[:], in0=yc[:], in1=yc[:], op=ALU.mult), [byc], [bsq])
                P.op("dve", lambda e, s1=s1, sq=sq: e.reduce_sum(out=s1[:, 4:8], in_=v3(sq[:]), axis=AX.X), [bsq], [bs1])
                P.op("act", lambda e, s1=s1: e.activation(out=s1[:, 4:8], in_=s1[:, 4:8], func=AF.Ln, bias=epsc[:, 1:2], scale=1.0 / 64.0), [bs1, b_eps], [bs1])
                P.op("act", lambda e, s1=s1: e.activation(out=s1[:, 4:8], in_=s1[:, 4:8], func=AF.Exp, scale=-0.5), [bs1], [bs1])
                yield
                yn, byn = f256.next()
                P.op("dve", lambda e, yn=yn, yc=yc, s1=s1: e.tensor_tensor(out=v3(yn[:]), in0=v3(yc[:]), in1=bc4(s1[:, 4:8]), op=ALU.mult), [byc, bs1], [byn])
                P.op("pool", lambda e, yn=yn: e.tensor_tensor(out=yn[:], in0=yn[:], in1=lngb, op=ALU.mult), [byn, b_v256], [byn])
                P.op("pool", lambda e, yn=yn: e.tensor_tensor(out=yn[:], in0=yn[:], in1=lnbb, op=ALU.add), [byn, b_v256], [byn])
                ks, bks = f256.next()
                P.op("pool", lambda e, ks=ks, kmod=kmod, ch=ch: e.tensor_tensor(out=ks[:], in0=kmod[:], in1=Ksum[:, ch, :], op=ALU.add), [bkmod, b_Ksum[ch]], [bks])
                P.op("pool", lambda e, ks=ks, r_=r_: e.tensor_tensor(out=ks[:], in0=ks[:], in1=r_, op=ALU.mult), [bks, brk], [bks])
                P.op("pool", lambda e, ks=ks: e.tensor_tensor(out=ks[:], in0=ks[:], in1=rkb, op=ALU.mult), [bks, b_v256], [bks])
                s2, bs2 = sm_r.next()
                P.op("dve", lambda e, s2=s2, ks=ks: e.reduce_sum(out=s2[:, 0:4], in_=v3(ks[:]), axis=AX.X), [bks], [bs2])
                bv, bbv = f256.next()
                P.op("dve", lambda e, bv=bv, v_=v_, s2=s2: e.tensor_tensor(out=v3(bv[:]), in0=v3(v_), in1=bc4(s2[:, 0:4]), op=ALU.mult), [brk, bs2], [bbv])
                P.op("pool", lambda e, yn=yn, bv=bv: e.tensor_tensor(out=yn[:], in0=yn[:], in1=bv[:], op=ALU.add), [byn, bbv], [byn])
                yield
                sz, bsz = f256.next()
                P.op("act", lambda e, sz=sz, z_=z_: e.activation(out=sz[:], in_=z_, func=AF.Silu), [brk], [bsz])
                yb, byb = b256.next()
                P.op("dve", lambda e, yb=yb, yn=yn, sz=sz: e.tensor_tensor(out=yb[:], in0=yn[:], in1=sz[:], op=ALU.mult), [byn, bsz], [byb])
                pb2, bpb2 = pnb()
                for fb in range(2):
                    P.op("pe", lambda e, pb2=pb2, yb=yb, fb=fb: e.transpose(out=pb2[:, fb * 128:(fb + 1) * 128], in_=yb[:, fb * 128:(fb + 1) * 128], identity=identb[:]), [byb, b_identb], [bpb2])
                ro, bro = ro_r.next()
                P.op("act", lambda e, ro=ro, pb2=pb2: e.activation(out=ro[:].rearrange("p a t -> p (a t)"), in_=pb2[:, 0:256], func=AF.Copy), [bpb2], [bro])
                P.dma("pool", lambda e, ro=ro, ch=ch: e.dma_start(out=mixT[b, 0:2, :, ch * 128:(ch + 1) * 128].rearrange("k p t -> p k t"), in_=ro[:]), [bro], [B_mixR[b][ch]])
                yield

        gens = [lane(d) for d in range(RW_ND)]
        while gens:
            for g in list(gens):
                try:
                    next(g)
                except StopIteration:
                    gens.remove(g)

    def phase_attn(l, b):
        AR.reset()
        kTt = AR.alloc([128, 4, T], BF16); b_kTt = Buf()
        Vt = AR.alloc([128, NT, 520], BF16); b_Vt = Buf()
        qz = [AR.alloc([128, 4, 512], BF16) for _ in range(2)]
        bqg = Buf()
        for j in range(2):
            P.op("pool", lambda e, j=j: e.memset(qz[j], 0.0), [], [bqg])
        E_r = AR.rot(3, [128, 512], BF16)
        szq_r = AR.rot(1, [128, 4, 512], BF16)
        oall_r = AR.rot(1, [128, 4, 512], BF16)
        ast_r = AR.rot(2, [128, 4, 128], BF16)
        junk_r = AR.rot(1, [128, 128], BF16)
        P.op("pool", lambda e: e.memset(Vt.rearrange("p k (h e) -> p k h e", e=130)[:, :, :, 128:130], 1.0), [], [b_Vt])
        scoreR = Rot(psAll[4:8], psBufs[4:8])
        oj_r = [AR.rot(2, [128, 4, 132], F32) for _ in range(2)]
        w_r = AR.rot(4, [128, 4, 128], F32)
        P.dma("sp", lambda e: e.dma_start(out=kTt[:], in_=qkT[b, 1, :, :, :].rearrange("h p t -> p h t")), list(B_k[b]), [b_kTt])
        for kt in range(NT):
            P.dma("sp", lambda e, kt=kt: e.dma_start(out=Vt[:, kt, :].rearrange("p (h e) -> p h e", e=130)[:, :, 0:128], in_=avd[b, kt * 128:(kt + 1) * 128, :].rearrange("p (h e) -> p h e", e=128)), [B_av[b][kt]], [b_Vt])
        qgroups = [([2, 3, 4, 5], list(range(NT))), ([6, 7, 8, 9], list(range(NT))), ([10, 11, 12, 13], list(range(NT))), ([14, 15, 16, 17], list(range(NT)))]
        if l < NL - 1:
            qgroups = [([0, 1], [0, 1])] + qgroups
        for (qt, kts) in qgroups[:AT_NG]:
            nq = len(qt)
            ntok = nq * 128
            tok0 = qt[0] * 128
            for j in range(2):
                P.dma("sp", lambda e, j=j, tok0=tok0, ntok=ntok: e.dma_start(out=qz[j][j * 64:(j + 1) * 64, :, 0:ntok], in_=qkT[b, 0, :, j * 64:(j + 1) * 64, tok0:tok0 + ntok].rearrange("h p t -> p h t")), [B_q[b][t] for t in qt], [bqg])
            szq, bszq = szq_r.next()
            for qi, t in enumerate(qt):
                P.dma("sp", lambda e, szq=szq, qi=qi, t=t: e.dma_start(out=szq[:, qi, :], in_=aszd[b, t * 128:(t + 1) * 128, :]), [B_asz[b][t]], [bszq])
            oall, boall = oall_r.next()
            from collections import deque
            items = [(h, j, ki, kt) for h in range(4) for j in range(2) for ki, kt in enumerate(kts)]
            acc = [(psAll[i_], psBufs[i_]) for i_ in range(nq)]
            pending = deque()
            ojs_h = {}

            def do_pv(item, E, bE):
                h, j, ki, kt = item
                first, last = (ki == 0), (ki == len(kts) - 1)
                for qi in range(nq):
                    ab, bab = acc[qi]
                    P.op("pe", lambda e, ab=ab, E=E, qi=qi: e.matmul(ab[:, 0:129], lhsT=E[:, qi * 128:(qi + 1) * 128], rhs=Vt[:, kt, h * 130:h * 130 + 129], start=first, stop=last), [bE, b_Vt], [bab])
                if not last:
                    return
                ojt, bojt = oj_r[j].next()
                ojs_h[j] = (ojt, bojt)
                for qi in range(nq):
                    if (qi + j) % 2 == 0:
                        P.op("dve", lambda e, qi=qi: e.tensor_copy(out=ojt[:, qi, 0:129], in_=acc[qi][0][:, 0:129]), [acc[qi][1]], [bojt])
                    else:
                        P.op("act", lambda e, qi=qi: e.activation(out=ojt[:, qi, 0:129], in_=acc[qi][0][:, 0:129], func=AF.Copy), [acc[qi][1]], [bojt])
                if j == 1:
                    combine(h)

            def combine(h):
                ojs = [ojs_h[0], ojs_h[1]]
                (oj0, boj0), (oj1, boj1) = ojs
                sm, bsm = sm_r.next()
                P.op("dve", lambda e, sm=sm, oj0=oj0: e.reciprocal(out=sm[:, 0:nq].unsqueeze(2), in_=oj0[:, 0:nq, 128:129]), [boj0], [bsm])
                P.op("dve", lambda e, sm=sm, oj1=oj1: e.reciprocal(out=sm[:, 4:4 + nq].unsqueeze(2), in_=oj1[:, 0:nq, 128:129]), [boj1], [bsm])
                P.op("dve", lambda e, sm=sm: e.tensor_scalar(out=sm[:, 4:4 + nq], in0=sm[:, 4:4 + nq], scalar1=neglam[:, 0:1], scalar2=None, op0=ALU.mult), [bsm, b_neglam], [bsm])
                t1, bt1 = w_r.next()
                t0, bt0 = w_r.next()
                P.op("dve", lambda e, t1=t1, oj1=oj1, sm=sm: e.tensor_tensor(out=t1[:, 0:nq, :], in0=oj1[:, 0:nq, 0:128], in1=sm[:, 4:4 + nq].unsqueeze(2).to_broadcast([128, nq, 128]), op=ALU.mult), [boj1, bsm], [bt1])
                P.op("dve", lambda e, t0=t0, oj0=oj0, sm=sm: e.tensor_tensor(out=t0[:, 0:nq, :], in0=oj0[:, 0:nq, 0:128], in1=sm[:, 0:nq].unsqueeze(2).to_broadcast([128, nq, 128]), op=ALU.mult), [boj0, bsm], [bt0])
                P.op("pool", lambda e, t0=t0, t1=t1: e.tensor_tensor(out=t0[:, 0:nq, :], in0=t0[:, 0:nq, :], in1=t1[:, 0:nq, :], op=ALU.add), [bt0, bt1], [bt0])
                P.op("pool", lambda e, t0=t0, t1=t1: e.tensor_tensor(out=t1[:, 0:nq, :], in0=t0[:, 0:nq, :], in1=t0[:, 0:nq, :], op=ALU.mult), [bt0], [bt1])
                sm2, bsm2 = sm_r.next()
                P.op("dve", lambda e, sm2=sm2, t1=t1: e.reduce_sum(out=sm2[:, 0:nq], in_=t1[:, 0:nq, :], axis=AX.X), [bt1], [bsm2])
                P.op("act", lambda e, sm2=sm2: e.activation(out=sm2[:, 0:nq], in_=sm2[:, 0:nq], func=AF.Ln, bias=epsc[:, 0:1], scale=1.0 / 128.0), [bsm2, b_eps], [bsm2])
                P.op("act", lambda e, sm2=sm2: e.activation(out=sm2[:, 0:nq], in_=sm2[:, 0:nq], func=AF.Exp, scale=-0.5), [bsm2], [bsm2])
                P.op("dve", lambda e, t0=t0, sm2=sm2: e.tensor_tensor(out=t0[:, 0:nq, :], in0=t0[:, 0:nq, :], in1=sm2[:, 0:nq].unsqueeze(2).to_broadcast([128, nq, 128]), op=ALU.mult), [bt0, bsm2], [bt0])
                P.op("dve", lambda e, t0=t0: e.tensor_tensor(out=t0[:, 0:nq, :], in0=t0[:, 0:nq, :], in1=gsub[:].unsqueeze(1).to_broadcast([128, nq, 128]), op=ALU.mult), [bt0, b_gsub], [bt0])
                P.op("pool", lambda e, t0=t0, oall=oall, szq=szq: e.tensor_tensor(out=oall[:, 0:nq, h * 128:(h + 1) * 128], in0=t0[:, 0:nq, :], in1=szq[:, 0:nq, h * 128:(h + 1) * 128], op=ALU.mult), [bt0, bszq], [boall])

            for item in items:
                h, j, ki, kt = item
                ps, bps = scoreR.next()
                P.op("pe", lambda e, ps=ps: e.matmul(ps[:, 0:ntok], lhsT=kTt[:, h, kt * 128:(kt + 1) * 128], rhs=qz[j][:, h, 0:ntok], start=True, stop=True), [b_kTt, bqg], [bps])
                E, bE = E_r.next()
                P.op("act", lambda e, E=E, ps=ps: e.activation(out=E[:, 0:ntok], in_=ps[:, 0:ntok], func=AF.Exp, scale=0.125), [bps], [bE])
                pending.append((item, E, bE))
                if len(pending) >= 3:
                    do_pv(*pending.popleft())
            while pending:
                do_pv(*pending.popleft())
            for qi, t in enumerate(qt if AT_CUT >= 4 else []):
                pb, bpb = psB.next()
                for h in range(4):
                    P.op("pe", lambda e, pb=pb, oall=oall, qi=qi, h=h: e.transpose(out=pb[:, h * 128:(h + 1) * 128], in_=oall[:, qi, h * 128:(h + 1) * 128], identity=identb[:]), [boall, b_identb], [bpb])
                ast, bast = ast_r.next()
                P.op("act", lambda e, ast=ast, pb=pb: e.activation(out=ast[:].rearrange("p h t -> p (h t)"), in_=pb[:, 0:512], func=AF.Copy), [bpb], [bast])
                P.dma("pool", lambda e, ast=ast, t=t: e.dma_start(out=mixT[b, 4:8, :, t * 128:(t + 1) * 128].rearrange("k p t -> p k t"), in_=ast[:]), [bast], [B_mixA[b][t]])

    def phase_out(l, b):
        AR.reset()
        xt_r = AR.rot(2, [128, D], F32)
        xo_r = AR.rot(2, [128, D], F32)
        t512 = AR.rot(4, [128, 512], F32)
        mx_r = AR.rot(2, [128, 8, 128], BF16)
        junk_r = AR.rot(2, [128, 512], BF16)
        bcC = [None, None, AR.alloc([128, D], F32)]; b_bcC = bufs(3)
        bcB = [None, None, AR.alloc([128, D], F32)]; b_bcB = bufs(3)
        bcast_rows(4, bcC, b_bcC, which=(2,))
        bcast_rows(b, bcB, b_bcB, which=(2,))
        woutb = AR.alloc([128, 8, D], BF16); b_woutb = Buf()
        P.dma("sp", lambda e: e.dma_start(out=woutb, in_=woutbf[l, :, :, :]), [B_woutbf[l]], [b_woutb])
        xsrc, Bx = (xall, B_x[0]) if l == 0 else (x1, B_x[1])
        tiles = list(range(NT)) if l < NL - 1 else list(range(2, NT))
        for t in tiles:
            G = (bcC, b_bcC) if t < 2 else (bcB, b_bcB)
            mx, bmx = mx_r.next()
            P.dma("sp", lambda e, mx=mx, t=t: e.dma_start(out=mx[:], in_=mixT[b, :, :, t * 128:(t + 1) * 128].rearrange("k p t -> p k t")), [B_mixR[b][t], B_mixC[b][t], B_mixA[b][t]], [bmx])
            xt, bxt = xt_r.next()
            P.dma("sp", lambda e, xt=xt, t=t: e.dma_start(out=xt[:], in_=xsrc[b, t * 128:(t + 1) * 128, :]), [Bx[b][t]], [bxt])
            pss = []
            sm, bsm = sm_r.next()
            for hf in range(2):
                ps, bps = psF.next()
                pss.append((ps, bps))
                for k in range(8):
                    P.op("pe", lambda e, ps=ps, mx=mx, k=k, hf=hf: e.matmul(ps[:, :], lhsT=mx[:, k, :], rhs=woutb[:, k, hf * 512:(hf + 1) * 512], start=(k == 0), stop=(k == 7)), [bmx, b_woutb], [bps])
                jk, bjk = junk_r.next()
                P.op("act", lambda e, jk=jk, ps=ps, sm=sm, hf=hf: e.activation(out=jk[:, 0:512], in_=ps[:, :], func=AF.Square, scale=1.0 / 32.0, accum_out=sm[:, hf:hf + 1]), [bps], [bjk, bsm])
            P.op("dve", lambda e, sm=sm: e.tensor_tensor(out=sm[:, 2:3], in0=sm[:, 0:1], in1=sm[:, 1:2], op=ALU.add), [bsm], [bsm])
            rstd_from_ms(sm[:, 2:3], bsm, 0)
            xo, bxo = xo_r.next()
            for hf in range(2):
                ps, bps = pss[hf]
                tq, btq = t512.next()
                P.op("dve", lambda e, tq=tq, ps=ps, sm=sm, hf=hf, G=G: e.scalar_tensor_tensor(out=tq[:], in0=ps[:, :], scalar=sm[:, 2:3], in1=G[0][2][:, hf * 512:(hf + 1) * 512], op0=ALU.mult, op1=ALU.mult), [bps, bsm, G[1][2]], [btq])
                P.op("pool", lambda e, xo=xo, tq=tq, xt=xt, hf=hf: e.tensor_tensor(out=xo[:, hf * 512:(hf + 1) * 512], in0=tq[:], in1=xt[:, hf * 512:(hf + 1) * 512], op=ALU.add), [btq, bxt], [bxo])
            if l < NL - 1:
                P.dma("pool", lambda e, xo=xo, t=t: e.dma_start(out=x1[b, t * 128:(t + 1) * 128, :], in_=xo[:]), [bxo], [B_x[1][b][t]])
                if dbg and b == 0:
                    P.dma("pool", lambda e, xo=xo, t=t: e.dma_start(out=dbgd["d_x1"][t * 128:(t + 1) * 128, :], in_=xo[:]), [bxo], [B_x[2][b][t]])
            else:
                P.dma("pool", lambda e, xo=xo, t=t: e.dma_start(out=outd[b, (t - 2) * 128:(t - 1) * 128, :], in_=xo[:]), [bxo], [B_x[2][b][t]])

    for l in range(NL):
        if upto >= 1:
            layer_setup(l)
        for b in range(NB):
            if upto >= 2:
                cv = phase_a(l, b)
            if upto >= 3:
                phase_conv(l, b, *cv)
            if upto >= 4:
                phase_rwkv(l, b)
            if upto >= 5:
                phase_attn(l, b)
            if upto >= 6:
                phase_out(l, b)
            if dbg and l == 0 and b == 0:
                bd = Buf()
                P.dma("sp", lambda e: e.dma_start(out=dbgd["d_mix"][:, :, :], in_=mixT[0, :, :, :]), B_mixR[0] + B_mixC[0] + B_mixA[0], [bd])
                P.dma("sp", lambda e: e.dma_start(out=dbgd["d_rkvz"][:, :], in_=rkvz[0, :, :]), B_rkvz[0], [bd])
                P.dma("sp", lambda e: e.dma_start(out=dbgd["d_q"][:, :, :, :], in_=qkT[0, :, :, :, :]), B_q[0] + B_k[0], [bd])
    stats = P.finish()
    return nc, stats


_CACHE = {}


def _prep(inputs, core, NB=NBF):
    cp = _CACHE.setdefault("colperm", _colperm())
    cst = _CACHE.setdefault("consts", _consts())
    f = lambda a: np.ascontiguousarray(np.asarray(a, dtype=np.float32))
    bs = slice(core * NB, (core + 1) * NB)
    x = np.asarray(inputs["x"])[bs]
    ctx = np.asarray(inputs["ctx"])[bs]
    m = {}
    m["xall"] = f(np.concatenate([ctx, x], axis=1))
    c5 = np.concatenate([np.asarray(inputs["c"])[bs], np.asarray(inputs["c_ctx"])[None, :]], axis=0)
    if c5.shape[0] < 5:
        c5 = np.concatenate([c5, np.zeros((5 - c5.shape[0], D), np.float32)], 0)
        c5[4] = np.asarray(inputs["c_ctx"])
    m["cc"] = f(c5)
    sh = _CACHE.get("shared")
    if sh is None:
        sh = {}
        sh["wext"] = f(np.asarray(inputs["w_in"])[:, :, cp])
        sh["wout"] = f(inputs["w_out"])
        sh["modw"] = f(inputs["mod_w"])
        sh["modb"] = f(np.asarray(inputs["mod_b"])[:, None, :])
        sh["preg"] = f(np.asarray(inputs["norm_pre_g"])[:, None, :])
        sh["postg"] = f(np.asarray(inputs["norm_post_g"])[:, None, :])
        sh["w0"] = f(np.asarray(inputs["rwkv_w0"]).reshape(L_FULL, 1, 512))
        sh["a0"] = f(np.asarray(inputs["rwkv_a0"]).reshape(L_FULL, 1, 512))
        sh["wup"] = f(np.asarray(inputs["rwkv_w_up"]).reshape(L_FULL, 128, 256))
        sh["aup"] = f(np.asarray(inputs["rwkv_a_up"]).reshape(L_FULL, 128, 256))
        sh["vec256"] = f(np.stack([np.asarray(inputs["rwkv_k_k"]), np.asarray(inputs["rwkv_k_a"]),
                                   np.asarray(inputs["rwkv_r_k"]).reshape(L_FULL, 256), np.asarray(inputs["rwkv_ln_g"]),
                                   np.asarray(inputs["rwkv_ln_b"]), np.asarray(inputs["diff_lambda"]).reshape(L_FULL, 256)], axis=1))
        sh["convw"] = f(inputs["conv_w"])
        sh["subg"] = f(np.asarray(inputs["diff_subln_g"])[:, None, :])
        for k, v in cst.items():
            sh[k] = f(v)
        _CACHE["shared"] = sh
    m.update(sh)
    return m


def kernel(**inputs):
    _CACHE.pop("shared", None)
    nc, stats = build()
    in_maps = [_prep(inputs, core) for core in range(8)]
    res = run_bass_kernel_spmd(nc, in_maps, core_ids=list(range(8)))
    out = np.concatenate([np.asarray(r["out"]) for r in res.results], axis=0)
    return out.astype(np.float32)
```

```python
import contextlib
import math
import numpy as np
import concourse.bass as bass
import concourse.mybir as mybir
from concourse.bass_utils import run_bass_kernel_spmd

F32 = mybir.dt.float32
BF16 = mybir.dt.bfloat16
AF = mybir.ActivationFunctionType
ALU = mybir.AluOpType
AX = mybir.AxisListType

EPOCH = 20000
RW_CUT = 99
RW_SUB = 99
RW_NCH = 99
RW_ND = 2
RW_VAR = 0
AT_CUT = 99
AT_NG = 99
N_DMA_SEMS = 8


class Buf:
    __slots__ = ("w", "r", "dw", "name")

    def __init__(self, name=""):
        self.w = None
        self.r = []
        self.dw = []
        self.name = name


class _Cap:
    def __init__(self):
        self.call = None

    def __getattr__(self, name):
        def f(*a, **k):
            self.call = (name, a, k)
            return self
        return f


class Rec:
    __slots__ = ("eng", "fn", "deps", "raw", "sig", "needed", "is_dma", "pos")

    def __init__(self, eng, fn, is_dma):
        self.eng = eng
        self.fn = fn
        self.deps = set()
        self.raw = set()
        self.sig = None
        self.needed = False
        self.is_dma = is_dma
        self.pos = 0


class Prog:
    ENGS = ("sp", "act", "pool", "dve", "pe")

    def __init__(self, nc):
        self.nc = nc
        self.streams = {e: [] for e in self.ENGS}
        self.stack = contextlib.ExitStack()
        self.dma_sems = {}
        self.dma_rr = {}
        self.dma_last = {}
        self.dma_cnt = {}
        self.nsem = 0
        self.all_dma = []
        self.fence = []

    def barrier(self):
        fence = []
        for e in self.ENGS:
            for rec in reversed(self.streams[e]):
                if not rec.is_dma:
                    fence.append(rec)
                    break
        fence += list(self.dma_last.values())
        self.fence = fence

    def sem(self, name):
        self.nsem += 1
        return self.stack.enter_context(self.nc.semaphore(name))

    def sbuf(self, name, shape, dt):
        return self.stack.enter_context(self.nc.sbuf_tensor("s_" + name, list(shape), dt))

    def psum(self, name, shape, dt):
        return self.stack.enter_context(self.nc.psum_tensor("p_" + name, list(shape), dt))

    def _track(self, rec, reads, writes):
        for b in reads:
            if b.w is not None:
                rec.deps.add(b.w)
                rec.raw.add(b.w)
            for w_ in b.dw:
                rec.deps.add(w_)
                rec.raw.add(w_)
        for b in writes:
            if b.w is not None:
                rec.deps.add(b.w)
            for w_ in b.dw:
                rec.deps.add(w_)
            for r in b.r:
                rec.deps.add(r)
        for b in reads:
            if not rec.is_dma:
                b.r = [r for r in b.r if r.is_dma or r.eng != rec.eng]
            b.r.append(rec)
        for b in writes:
            if rec.is_dma:
                b.dw = [w_ for w_ in b.dw if w_ not in rec.deps or True][-63:] + [rec]
            else:
                b.dw = []
            b.w = rec
            b.r = []
        for f in self.fence:
            rec.deps.add(f)
            rec.raw.add(f)
        rec.deps.discard(rec)
        rec.raw.discard(rec)

    def op(self, eng, fn, reads=(), writes=()):
        cap = _Cap()
        fn(cap)
        rec = Rec(eng, cap.call, False)
        self._track(rec, reads, writes)
        self.streams[eng].append(rec)
        return rec

    def dma(self, q, fn, reads=(), writes=()):
        cap = _Cap()
        fn(cap)
        rec = Rec(q, cap.call, True)
        self._track(rec, reads, writes)
        if q not in self.dma_sems:
            self.dma_sems[q] = [self.sem(f"dma_{q}_{i}") for i in range(N_DMA_SEMS)]
            self.dma_rr[q] = 0
        i = self.dma_rr[q]
        self.dma_rr[q] = (i + 1) % N_DMA_SEMS
        s = self.dma_sems[q][i]
        key = (q, i)
        prev = self.dma_last.get(key)
        if prev is not None:
            rec.deps.add(prev)
        self.dma_last[key] = rec
        self.dma_cnt[key] = self.dma_cnt.get(key, 0) + 1
        rec.sig = (s, 16 * self.dma_cnt[key])
        self.streams[q].append(rec)
        self.all_dma.append(rec)
        return rec

    @staticmethod
    def _skip(rec, d):
        if d.is_dma or rec.is_dma or d.eng != rec.eng:
            return False
        if rec.eng == "pe":
            return True
        return d not in rec.raw

    def finish(self):
        nc = self.nc
        for e in self.ENGS:
            for rec in self.streams[e]:
                for d in rec.deps:
                    if d.is_dma or self._skip(rec, d):
                        continue
                    d.needed = True
        for e in self.ENGS:
            cnt = 0
            sem = None
            for rec in self.streams[e]:
                if rec.is_dma or not rec.needed:
                    continue
                if sem is None or cnt >= EPOCH:
                    sem = self.sem(f"c_{e}_{self.nsem}")
                    cnt = 0
                cnt += 1
                rec.sig = (sem, cnt)
            self.sigcnt = getattr(self, "sigcnt", {})
            self.sigcnt[e] = cnt
        final_waits = {}
        for rec in self.all_dma:
            s, v = rec.sig
            final_waits[id(s)] = (s, max(v, final_waits.get(id(s), (s, 0))[1]))
        streams = self.streams
        skip = self._skip

        def emit(ename, eng):
            waited = {}
            for rec in streams[ename]:
                for d in rec.deps:
                    if skip(rec, d):
                        continue
                    s, v = d.sig
                    if waited.get(id(s), 0) < v:
                        eng.wait_ge(s, v)
                        waited[id(s)] = v
                name, a_, k_ = rec.fn
                ins = getattr(eng, name)(*a_, **k_)
                if rec.is_dma:
                    ins.then_inc(rec.sig[0], 16)
                elif rec.sig is not None:
                    ins.then_inc(rec.sig[0], 1)
            if ename == "sp":
                for s, v in final_waits.values():
                    eng.wait_ge(s, v)

        with nc.Block() as block:
            @block.sync
            def _(e):
                emit("sp", e)

            @block.scalar
            def _(e):
                emit("act", e)

            @block.gpsimd
            def _(e):
                emit("pool", e)

            @block.vector
            def _(e):
                emit("dve", e)

            @block.tensor
            def _(e):
                emit("pe", e)
        self.stack.close()
        return {e: (len(self.streams[e]), self.sigcnt.get(e)) for e in self.ENGS}


class Rot:
    def __init__(self, tiles, bufs=None):
        self.tiles = tiles
        self.bufs = bufs if bufs is not None else [Buf() for _ in tiles]
        self.i = 0

    def next(self):
        i = self.i
        self.i = (i + 1) % len(self.tiles)
        return self.tiles[i], self.bufs[i]


D = 1024
L_FULL = 2
NBF = 4
CTX = 256
SEQ = 2048
T = CTX + SEQ
NT = T // 128
WC = 5376
NFM = 26
DECAY_C = -math.exp(-0.5)
NORM_EPS = 1e-6
GN_EPS = 64e-5


def _colperm():
    cols = []
    cols += list(range(768, 896)) + list(range(896, 1024))
    cols += list(range(1536, 1792)) + list(range(1792, 2048)) + list(range(2048, 2304)) + list(range(1280, 1536))

    def rot_src(base):
        out = []
        for jd in range(128):
            j, dd = divmod(jd, 64)
            g, i = divmod(dd, 32)
            out.append(base + j * 64 + g * 32 + (i + 16 if i < 16 else i - 16))
        return out

    for s0 in (2304, 2816):
        for h in range(4):
            base = s0 + h * 128
            cols += list(range(base, base + 128)) + rot_src(base)
    cols += list(range(0, 768)) + list(range(1024, 1280))
    cols += list(range(3328, 3840)) + list(range(3840, 4352))
    assert len(cols) == WC
    return np.array(cols)


def _consts():
    p = np.arange(128)[:, None]
    f = np.arange(128)[None, :]
    LE, GE, LT, GT = (p <= f), (p >= f), (p < f), (p > f)
    c = {}
    c["ident"] = np.eye(128, dtype=np.float32)
    c["cm"] = (np.stack([LE, GE, LT, GT], 1).astype(np.float32) * DECAY_C).astype(np.float32)
    m4 = np.zeros((128, 2, 512), np.float32)
    m4[:, 0] = np.concatenate([LT, LE, LT, LE], 1)
    m4[:, 1] = np.concatenate([GT, GE, GT, GE], 1)
    c["mask4"] = m4
    mL = np.zeros((128, 2, 512), np.float32)
    mL[:, 0] = np.concatenate([GT] * 4, 1)
    mL[:, 1] = np.concatenate([LT] * 4, 1)
    c["maskL"] = mL
    pos = np.arange(SEQ)
    row = (pos // 64).astype(np.float32)
    col = (pos % 64).astype(np.float32)
    inv = (10000.0 ** (-np.arange(0, 32, 2, dtype=np.float32) / 32)).astype(np.float32)
    cosT = np.ones((128, T), np.float32)
    sinT = np.zeros((128, T), np.float32)
    for pp in range(128):
        dd = pp % 64
        g, i = divmod(dd, 32)
        ang = (row if g == 0 else col) * inv[i % 16]
        cosT[pp, CTX:] = np.cos(ang)
        sinT[pp, CTX:] = np.sin(ang) * (-1.0 if i < 16 else 1.0)
    c["cosT"] = cosT
    c["sinT"] = sinT
    sel = np.zeros((5, 5, 128), np.float32)
    for b in range(5):
        sel[b, b, :] = 1.0
    c["sel"] = sel
    return c


def build(NB=NBF, NL=L_FULL, dbg=False, upto=9):
    nc = bass.Bass("TRN2", target_bir_lowering=False)
    P = Prog(nc)

    def din(name, shape, dt=F32):
        return nc.dram_tensor(name, list(shape), dt, kind="ExternalInput").ap()

    def dscr(name, shape, dt):
        return nc.dram_tensor(name, list(shape), dt, kind="Internal").ap()

    xall = din("xall", [NB, T, D])
    cc = din("cc", [5, D])
    wext = din("wext", [L_FULL, D, WC])
    woutd = din("wout", [L_FULL, D, D])
    modw = din("modw", [L_FULL, D, 3 * D])
    modb = din("modb", [L_FULL, 1, 3 * D])
    pregd = din("preg", [L_FULL, 1, D])
    postgd = din("postg", [L_FULL, 1, D])
    w0d = din("w0", [L_FULL, 1, 512])
    a0d = din("a0", [L_FULL, 1, 512])
    wupd = din("wup", [L_FULL, 128, 256])
    aupd = din("aup", [L_FULL, 128, 256])
    vec256 = din("vec256", [L_FULL, 6, 256])
    convwd = din("convw", [L_FULL, 3, 256])
    subgd = din("subg", [L_FULL, 1, 128])
    identd = din("ident", [128, 128])
    cmd = din("cm", [128, 4, 128])
    mask4d = din("mask4", [128, 2, 512])
    maskLd = din("maskL", [128, 2, 512])
    cosd = din("cosT", [128, T])
    sind = din("sinT", [128, T])
    seld = din("sel", [5, 5, 128])
    outd = nc.dram_tensor("out", [NB, SEQ, D], F32, kind="ExternalOutput").ap()
    dbgd = {}
    if dbg:
        for nm, shp, dt_ in (("d_rkvz", [T, 1024], BF16), ("d_mix", [8, 128, T], BF16), ("d_x1", [T, D], F32), ("d_q", [2, 4, 128, T], BF16)):
            dbgd[nm] = nc.dram_tensor(nm, shp, dt_, kind="ExternalOutput").ap()

    x1 = dscr("x1", [NB, T, D], F32)
    wbf = dscr("wbf", [L_FULL, 11, 128, 8, 512], BF16)
    rkvz = dscr("rkvz", [NB, T, 1024], BF16)
    avd = dscr("av", [NB, T, 512], BF16)
    aszd = dscr("asz", [NB, T, 512], BF16)
    qkT = dscr("qkT", [NB, 2, 4, 128, T], BF16)
    mixT = dscr("mixT", [NB, 8, 128, T], BF16)
    woutbf = dscr("woutbf", [L_FULL, 128, 8, D], BF16)
    lwlad = dscr("lwlad", [NB, 2, 128, T], F32)
    B_woutbf = [Buf() for _ in range(L_FULL)]

    def bufs(n):
        return [Buf() for _ in range(n)]

    B_x = {0: [bufs(NT) for _ in range(NB)], 1: [bufs(NT) for _ in range(NB)], 2: [bufs(NT) for _ in range(NB)]}
    B_wbf = [bufs(11) for _ in range(L_FULL)]
    B_rkvz = [bufs(NT) for _ in range(NB)]
    B_av = [bufs(NT) for _ in range(NB)]
    B_asz = [bufs(NT) for _ in range(NB)]
    B_q = [bufs(NT) for _ in range(NB)]
    B_k = [bufs(NT) for _ in range(NB)]
    B_mixR = [bufs(NT) for _ in range(NB)]
    B_mixC = [bufs(NT) for _ in range(NB)]
    B_mixA = [bufs(NT) for _ in range(NB)]
    B_lwla = [bufs(NT) for _ in range(NB)]

    psAll = [P.psum(f"psF{i}", [128, 512], F32) for i in range(8)]
    psBufs = [Buf() for _ in range(8)]
    psF = Rot(psAll[0:6], psBufs[0:6])
    psB = Rot([t[:, :].bitcast(BF16) for t in psAll[6:8]], psBufs[6:8])

    def ctile(name, shape, dt=F32):
        return P.sbuf(name, shape, dt), Buf(name)

    identf, b_identf = ctile("identf", [128, 128])
    identb, b_identb = ctile("identb", [128, 128], BF16)
    cm, b_cm = ctile("cm", [128, 4, 128])
    mask4, b_mask4 = ctile("mask4", [128, 2, 512], BF16)
    maskL, b_maskL = ctile("maskL", [128, 2, 512], BF16)
    cosT, b_cos = ctile("cosT", [128, T], BF16)
    sinT, b_sin = ctile("sinT", [128, T], BF16)
    arF = P.sbuf("arF", [128, 14848], F32)
    arB = P.sbuf("arB", [128, 48128], BF16)

    class Arena:
        def __init__(self):
            self.o = {F32: 0, BF16: 0}

        def reset(self):
            self.o = {F32: 0, BF16: 0}
            P.barrier()

        def alloc(self, shape, dt):
            n = 1
            for x in shape[1:]:
                n *= x
            ar = arF if dt == F32 else arB
            o = self.o[dt]
            assert o + n <= (14848 if dt == F32 else 48128), (shape, dt, o)
            self.o[dt] = o + n
            v = ar[0:shape[0], o:o + n]
            if len(shape) == 3:
                v = v.rearrange("p (a b) -> p a b", a=shape[1])
            elif len(shape) == 4:
                v = v.rearrange("p (a b c) -> p a b c", a=shape[1], b=shape[2])
            return v

        def rot(self, n, shape, dt):
            return Rot([self.alloc(shape, dt) for _ in range(n)])

    AR = Arena()
    sel, b_sel = ctile("sel", [5, 5, 128])
    negcol, b_negcol = ctile("negcol", [128, 1])
    onesrow, b_ones = ctile("onesrow", [1, 128])
    epsc, b_eps = ctile("epsc", [128, 2])
    P.dma("sp", lambda e: e.dma_start(out=identf[:], in_=identd[:, :]), [], [b_identf])
    P.dma("sp", lambda e: e.dma_start(out=cm[:], in_=cmd[:, :, :]), [], [b_cm])
    for (dst, bdst, src, n) in ((mask4, b_mask4, mask4d.rearrange("p a b -> p (a b)"), 1024), (maskL, b_maskL, maskLd.rearrange("p a b -> p (a b)"), 1024),
                                (cosT, b_cos, cosd, T), (sinT, b_sin, sind, T)):
        AR.reset()
        tmpc = AR.alloc([128, n], F32)
        btmp = Buf()
        P.dma("sp", lambda e, tmpc=tmpc, src=src: e.dma_start(out=tmpc, in_=src), [], [btmp])
        dv = dst[:].rearrange("p a b -> p (a b)") if n == 1024 else dst[:]
        P.op("dve", lambda e, dv=dv, tmpc=tmpc: e.tensor_copy(out=dv, in_=tmpc), [btmp], [bdst])
    P.dma("sp", lambda e: e.dma_start(out=sel[:], in_=seld[:, :, :]), [], [b_sel])
    P.op("dve", lambda e: e.tensor_copy(out=identb[:], in_=identf[:]), [b_identf], [b_identb])
    P.op("pool", lambda e: e.memset(negcol[:], DECAY_C), [], [b_negcol])
    P.op("pool", lambda e: e.memset(onesrow[:], 1.0), [], [b_ones])
    P.op("pool", lambda e: e.memset(epsc[:, 0:1], NORM_EPS), [], [b_eps])
    P.op("pool", lambda e: e.memset(epsc[:, 1:2], GN_EPS), [], [b_eps])

    AR.reset()
    wstg = AR.rot(2, [128, 8, 256], F32)
    wcast = AR.rot(2, [128, 8, 256], BF16)
    cast_eng = ["dve", "pool", "act"]
    ci = 0
    for l in range(NL):
        for sb in range(11):
            wfull = 512 if sb != 6 else 256
            c0 = sb * 512 if sb < 6 else (3072 if sb == 6 else 3328 + (sb - 7) * 512)
            for hf in range(wfull // 256):
                st, bst = wstg.next()
                cb, bcb = wcast.next()
                P.dma("sp", lambda e, st=st, l=l, c0=c0, hf=hf: e.dma_start(
                    out=st, in_=wext[l, :, c0 + hf * 256:c0 + (hf + 1) * 256].rearrange("(c p) n -> p c n", p=128)), [], [bst])
                eng = cast_eng[ci % 3]
                ci += 1
                if eng == "act":
                    P.op("act", lambda e, st=st, cb=cb: e.activation(out=cb, in_=st, func=AF.Copy), [bst], [bcb])
                else:
                    P.op(eng, lambda e, st=st, cb=cb: e.tensor_copy(out=cb, in_=st), [bst], [bcb])
                P.dma("pool", lambda e, cb=cb, l=l, sb=sb, hf=hf: e.dma_start(out=wbf[l, sb, :, :, hf * 256:(hf + 1) * 256], in_=cb), [bcb], [B_wbf[l][sb]])

    srow = P.sbuf("srow", [5, D], F32); b_srow = Buf()
    arow = P.sbuf("arow", [5, D], F32); b_arow = Buf()
    grow = P.sbuf("grow", [5, D], F32); b_grow = Buf()
    v256 = P.sbuf("v256", [128, 6, 256], F32); b_v256 = Buf()
    wup = P.sbuf("wup", [128, 2, 256], F32); b_wup = Buf()
    aup = P.sbuf("aup", [128, 2, 256], F32); b_aup = Buf()
    w0r = P.sbuf("w0r", [1, 512], F32); b_w0r = Buf()
    a0r = P.sbuf("a0r", [1, 512], F32); b_a0r = Buf()
    cwc = P.sbuf("cwc", [128, 2, 3], F32); b_cwc = Buf()
    gsub = P.sbuf("gsub", [128, 128], F32); b_gsub = Buf()
    neglam = P.sbuf("neglam", [128, 1], F32); b_neglam = Buf()
    lamt = P.sbuf("lamt", [128, 132], F32); b_lamt = Buf()
    sm_r = Rot([P.sbuf(f"sm{i}", [128, 8], F32) for i in range(12)])

    LAM_INIT = [0.8 - 0.6 * math.exp(-0.3 * l) for l in range(L_FULL)]

    def rstd_from_ms(ms, bms, eps_col, n=1):
        P.op("act", lambda e: e.activation(out=ms, in_=ms, func=AF.Ln, bias=epsc[:, eps_col:eps_col + 1], scale=1.0), [bms, b_eps], [bms])
        P.op("act", lambda e: e.activation(out=ms, in_=ms, func=AF.Exp, scale=-0.5), [bms], [bms])

    def layer_setup(l):
        AR.reset()
        scT = AR.alloc([128, 8, 5], F32); b_scT = Buf()
        modb5 = AR.rot(2, [5, 512], F32)
        preg5 = AR.alloc([5, D], F32); b_preg5 = Buf()
        postg5 = AR.alloc([5, D], F32); b_postg5 = Buf()
        wstg = AR.rot(2, [128, 8, 256], F32)
        for c in range(8):
            P.dma("sp", lambda e, c=c: e.dma_start(out=scT[:, c, :], in_=cc[:, c * 128:(c + 1) * 128].rearrange("b p -> p b"), allow_slow_non_contiguous=True), [], [b_scT])
        P.op("act", lambda e: e.activation(out=scT, in_=scT, func=AF.Silu), [b_scT], [b_scT])
        P.dma("sp", lambda e: e.dma_start(out=preg5, in_=pregd[l, 0:1, :].broadcast_to([5, D])), [], [b_preg5])
        P.dma("sp", lambda e: e.dma_start(out=postg5, in_=postgd[l, 0:1, :].broadcast_to([5, D])), [], [b_postg5])
        dsts = [(srow, b_srow), (arow, b_arow), (grow, b_grow)]
        for cb in range(6):
            mb, bmb = modb5.next()
            P.dma("sp", lambda e, mb=mb, cb=cb: e.dma_start(out=mb, in_=modb[l, 0:1, cb * 512:(cb + 1) * 512].broadcast_to([5, 512])), [], [bmb])
            ps, bps = psF.next()
            for hf in range(2):
                st, bst = wstg.next()
                P.dma("sp", lambda e, st=st, cb=cb, hf=hf: e.dma_start(
                    out=st, in_=modw[l, :, cb * 512 + hf * 256:cb * 512 + (hf + 1) * 256].rearrange("(c p) n -> p c n", p=128)), [], [bst])
                for c in range(8):
                    P.op("pe", lambda e, ps=ps, st=st, c=c, hf=hf: e.matmul(ps[0:5, hf * 256:(hf + 1) * 256], lhsT=scT[:, c, :], rhs=st[:, c, :], start=(c == 0), stop=(c == 7)), [b_scT, bst], [bps])
            dst, bd = dsts[cb // 2]
            P.op("dve", lambda e, ps=ps, cb=cb, dst=dst, mb=mb: e.tensor_tensor(out=dst[:, (cb % 2) * 512:(cb % 2 + 1) * 512], in0=ps[0:5, :], in1=mb, op=ALU.add), [bps, bmb], [bd])
        P.op("dve", lambda e: e.scalar_tensor_tensor(out=arow[:], in0=arow[:], scalar=1.0, in1=preg5, op0=ALU.add, op1=ALU.mult), [b_arow, b_preg5], [b_arow])
        P.op("dve", lambda e: e.tensor_tensor(out=grow[:], in0=grow[:], in1=postg5, op=ALU.mult), [b_grow, b_postg5], [b_grow])
        wcs_r = AR.rot(2, [128, 8, 256], BF16)
        for q4 in range(4):
            st, bst = wstg.next()
            cb, bcb = wcs_r.next()
            P.dma("sp", lambda e, st=st, q4=q4: e.dma_start(out=st, in_=woutd[l, :, q4 * 256:(q4 + 1) * 256].rearrange("(c p) n -> p c n", p=128)), [], [bst])
            P.op("pool", lambda e, st=st, cb=cb: e.tensor_copy(out=cb, in_=st), [bst], [bcb])
            P.dma("pool", lambda e, cb=cb, q4=q4: e.dma_start(out=woutbf[l, :, :, q4 * 256:(q4 + 1) * 256], in_=cb), [bcb], [B_woutbf[l]])
        P.dma("sp", lambda e: e.dma_start(out=v256[:], in_=vec256[l:l + 1, :, :].broadcast_to([128, 6, 256])), [], [b_v256])
        P.op("pool", lambda e: e.memset(wup[:], 0.0), [], [b_wup])
        P.op("pool", lambda e: e.memset(aup[:], 0.0), [], [b_aup])
        for dd in range(2):
            P.dma("sp", lambda e, dd=dd: e.dma_start(out=wup[dd * 64:(dd + 1) * 64, dd, :], in_=wupd[l, dd * 64:(dd + 1) * 64, :]), [], [b_wup])
            P.dma("sp", lambda e, dd=dd: e.dma_start(out=aup[dd * 64:(dd + 1) * 64, dd, :], in_=aupd[l, dd * 64:(dd + 1) * 64, :]), [], [b_aup])
        P.dma("sp", lambda e: e.dma_start(out=w0r[:], in_=w0d[l, 0:1, :]), [], [b_w0r])
        P.dma("sp", lambda e: e.dma_start(out=a0r[:], in_=a0d[l, 0:1, :]), [], [b_a0r])
        for fb in range(2):
            P.dma("sp", lambda e, fb=fb: e.dma_start(out=cwc[:, fb, :], in_=convwd[l, :, fb * 128:(fb + 1) * 128].rearrange("j p -> p j"), allow_slow_non_contiguous=True), [], [b_cwc])
        P.dma("sp", lambda e: e.dma_start(out=gsub[:], in_=subgd[l, 0:1, :].broadcast_to([128, 128])), [], [b_gsub])
        P.op("dve", lambda e: e.tensor_scalar_mul(out=gsub[:], in0=gsub[:], scalar1=1.0 - LAM_INIT[l]), [b_gsub], [b_gsub])
        P.op("dve", lambda e: e.tensor_tensor(out=lamt[:, 0:64], in0=v256[:, 5, 0:64], in1=v256[:, 5, 64:128], op=ALU.mult), [b_v256], [b_lamt])
        P.op("dve", lambda e: e.tensor_tensor(out=lamt[:, 64:128], in0=v256[:, 5, 128:192], in1=v256[:, 5, 192:256], op=ALU.mult), [b_v256], [b_lamt])
        P.op("dve", lambda e: e.reduce_sum(out=lamt[:, 128:130], in_=lamt[:, 0:128].rearrange("p (a b) -> p a b", a=2), axis=AX.X), [b_lamt], [b_lamt])
        P.op("act", lambda e: e.activation(out=lamt[:, 130:132], in_=lamt[:, 128:130], func=AF.Exp), [b_lamt], [b_lamt])
        P.op("dve", lambda e: e.tensor_tensor(out=neglam[:], in0=lamt[:, 131:132], in1=lamt[:, 130:131], op=ALU.subtract), [b_lamt], [b_neglam])
        P.op("dve", lambda e: e.tensor_scalar_add(out=neglam[:], in0=neglam[:], scalar1=-LAM_INIT[l]), [b_neglam], [b_neglam])

    def bcast_rows(row, tiles, tb, which=(0, 1, 2)):
        srcs = [(arow, b_arow, None), (srow, b_srow, None), (grow, b_grow, None)]
        for qi, (src, bsrc, _) in enumerate(srcs):
            if qi not in which:
                continue
            for hf in range(2):
                ps, bps = psF.next()
                P.op("pe", lambda e, ps=ps, src=src, hf=hf: e.matmul(ps[:, :], lhsT=sel[0:5, row, :], rhs=src[0:5, hf * 512:(hf + 1) * 512], start=True, stop=True), [b_sel, bsrc], [bps])
                P.op("act", lambda e, ps=ps, qi=qi, hf=hf: e.activation(out=tiles[qi][:, hf * 512:(hf + 1) * 512], in_=ps[:, :], func=AF.Copy), [bps], [tb[qi]])

    def phase_a(l, b):
        xsrc, Bx = (xall, B_x[0]) if l == 0 else (x1, B_x[1])
        AR.reset()
        hT = AR.alloc([128, 8, 1152], BF16); b_hT = Buf()
        xt_r = AR.rot(2, [128, D], F32)
        f32a = AR.rot(2, [128, D], F32)
        t512 = AR.rot(4, [128, 512], F32)
        hb_r = AR.rot(2, [128, D], BF16)
        junk_r = AR.rot(2, [128, D], BF16)
        wblk_r = AR.rot(2, [128, 8, 512], BF16)
        stg_r = AR.rot(3, [128, 1024], BF16)
        ures = AR.alloc([128, 2, T], BF16); b_ures = Buf()
        bzres = AR.alloc([128, 2, T], BF16); b_bzres = Buf()
        bcC = [AR.alloc([128, D], F32) if i < 2 else None for i in range(3)]; b_bcC = bufs(3)
        bcB = [AR.alloc([128, D], F32) if i < 2 else None for i in range(3)]; b_bcB = bufs(3)
        bcast_rows(4, bcC, b_bcC, which=(0, 1))
        bcast_rows(b, bcB, b_bcB, which=(0, 1))
        for part in range(2):
            t0 = part * 9
            for ti in range(9):
                t = t0 + ti
                A, S = (bcC, b_bcC) if t < 2 else (bcB, b_bcB)
                xt, bxt = xt_r.next()
                P.dma("sp", lambda e, xt=xt, t=t: e.dma_start(out=xt[:], in_=xsrc[b, t * 128:(t + 1) * 128, :]), [Bx[b][t]], [bxt])
                jk, bjk = junk_r.next()
                sm, bsm = sm_r.next()
                P.op("act", lambda e, xt=xt, jk=jk, sm=sm: e.activation(out=jk[:], in_=xt[:], func=AF.Square, scale=1.0 / 32.0, accum_out=sm[:, 0:1]), [bxt], [bjk, bsm])
                rstd_from_ms(sm[:, 0:1], bsm, 0)
                fa, bfa = f32a.next()
                P.op("dve", lambda e, fa=fa, xt=xt, sm=sm, A=A: e.scalar_tensor_tensor(out=fa[:], in0=xt[:], scalar=sm[:, 0:1], in1=A[0][:], op0=ALU.mult, op1=ALU.mult), [bxt, bsm, S[0]], [bfa])
                hb, bhb = hb_r.next()
                P.op("pool", lambda e, hb=hb, fa=fa, A=A: e.tensor_tensor(out=hb[:], in0=fa[:], in1=A[1][:], op=ALU.add), [bfa, S[1]], [bhb])
                pb, bpb = psB.next()
                for c in range(8):
                    P.op("pe", lambda e, pb=pb, hb=hb, c=c: e.transpose(out=pb[:, c * 128:(c + 1) * 128], in_=hb[:, c * 128:(c + 1) * 128], identity=identb[:]), [bhb, b_identb], [bpb])
                P.op("act", lambda e, pb=pb, ti=ti: e.activation(out=hT[:, :, ti * 128:(ti + 1) * 128], in_=pb[:, :].rearrange("p (c t) -> p c t", c=8), func=AF.Copy), [bpb], [b_hT])
            groups = [(0, 4), (4, 4), (8, 1)]
            for sb in range(11):
                wb, bwb = wblk_r.next()
                w = 512 if sb != 6 else 256
                P.dma("sp", lambda e, wb=wb, sb=sb, w=w: e.dma_start(out=wb[:, :, 0:w], in_=wbf[l, sb, :, :, 0:w]), [B_wbf[l][sb]], [bwb])
                if sb < 7:
                    nblk = 4 if sb < 6 else 2
                    k = 0
                    while k < nblk:
                        fm = sb * 4 + k
                        pair = fm >= 10
                        for (g0, gn) in groups:
                            ntok = gn * 128
                            tok0 = (t0 + g0) * 128
                            loc0 = g0 * 128
                            tiles_g = list(range(t0 + g0, t0 + g0 + gn))
                            ps, bps = psF.next()
                            for c in range(8):
                                P.op("pe", lambda e, ps=ps, wb=wb, c=c, k=k, loc0=loc0, ntok=ntok: e.matmul(ps[:, 0:ntok], lhsT=wb[:, c, k * 128:(k + 1) * 128], rhs=hT[:, c, loc0:loc0 + ntok], start=(c == 0), stop=(c == 7)), [bwb, b_hT], [bps])
                            if pair:
                                ps2, bps2 = psF.next()
                                for c in range(8):
                                    P.op("pe", lambda e, ps2=ps2, wb=wb, c=c, k=k, loc0=loc0, ntok=ntok: e.matmul(ps2[:, 0:ntok], lhsT=wb[:, c, (k + 1) * 128:(k + 2) * 128], rhs=hT[:, c, loc0:loc0 + ntok], start=(c == 0), stop=(c == 7)), [bwb, b_hT], [bps2])
                                ta, bta = t512.next()
                                tb_, btb = t512.next()
                                P.op("dve", lambda e, ta=ta, ps=ps, tok0=tok0, ntok=ntok: e.tensor_tensor(out=ta[:, 0:ntok], in0=ps[:, 0:ntok], in1=cosT[:, tok0:tok0 + ntok], op=ALU.mult), [bps, b_cos], [bta])
                                P.op("dve", lambda e, tb_=tb_, ps2=ps2, tok0=tok0, ntok=ntok: e.tensor_tensor(out=tb_[:, 0:ntok], in0=ps2[:, 0:ntok], in1=sinT[:, tok0:tok0 + ntok], op=ALU.mult), [bps2, b_sin], [btb])
                                sg, bsg = stg_r.next()
                                P.op("pool", lambda e, sg=sg, ta=ta, tb_=tb_, ntok=ntok: e.tensor_tensor(out=sg[:, 0:ntok], in0=ta[:, 0:ntok], in1=tb_[:, 0:ntok], op=ALU.add), [bta, btb], [bsg])
                                qk = 0 if fm < 18 else 1
                                hh = ((fm - 10) // 2) % 4
                                Bq = (B_q if qk == 0 else B_k)[b]
                                P.dma("pool", lambda e, sg=sg, qk=qk, hh=hh, tok0=tok0, ntok=ntok: e.dma_start(out=qkT[b, qk, hh, :, tok0:tok0 + ntok], in_=sg[:, 0:ntok]), [bsg], [Bq[t] for t in tiles_g])
                            elif fm in (0, 1):
                                ta, bta = t512.next()
                                P.op("act", lambda e, ps=ps, ta=ta: e.activation(out=ta[:, 0:ntok], in_=ps[:, 0:ntok], func=(AF.Tanh if fm == 0 else AF.Copy)), [bps], [bta])
                                P.dma("pool", lambda e, ta=ta: e.dma_start(out=lwlad[b, fm, :, tok0:tok0 + ntok], in_=ta[:, 0:ntok]), [bta], [B_lwla[b][t] for t in tiles_g])
                            elif fm in (2, 3):
                                P.op("act", lambda e, ps=ps, fm=fm, tok0=tok0, ntok=ntok: e.activation(out=ures[:, fm - 2, tok0:tok0 + ntok], in_=ps[:, 0:ntok], func=AF.Copy), [bps], [b_ures])
                            elif fm in (4, 5):
                                P.op("dve", lambda e, ps=ps, fm=fm, tok0=tok0, ntok=ntok: e.tensor_tensor(out=ures[:, fm - 4, tok0:tok0 + ntok], in0=ps[:, 0:ntok], in1=ures[:, fm - 4, tok0:tok0 + ntok], op=ALU.mult), [bps, b_ures], [b_ures])
                            elif fm in (6, 7):
                                P.op("act", lambda e, ps=ps, fm=fm, tok0=tok0, ntok=ntok: e.activation(out=bzres[:, fm - 6, tok0:tok0 + ntok], in_=ps[:, 0:ntok], func=AF.Silu), [bps], [b_bzres])
                            elif fm in (8, 9):
                                P.op("dve", lambda e, ps=ps, fm=fm, tok0=tok0, ntok=ntok: e.tensor_tensor(out=bzres[:, fm - 8, tok0:tok0 + ntok], in0=ps[:, 0:ntok], in1=bzres[:, fm - 8, tok0:tok0 + ntok], op=ALU.mult), [bps, b_bzres], [b_bzres])
                        k += 2 if pair else 1
                else:
                    tmb = sb - 7
                    for ti in range(9):
                        t = t0 + ti
                        ps, bps = psF.next()
                        for c in range(8):
                            P.op("pe", lambda e, ps=ps, wb=wb, c=c, ti=ti: e.matmul(ps[:, :], lhsT=hT[:, c, ti * 128:(ti + 1) * 128], rhs=wb[:, c, :], start=(c == 0), stop=(c == 7)), [bwb, b_hT], [bps])
                        sg, bsg = stg_r.next()
                        if tmb == 3:
                            P.op("act", lambda e, sg=sg, ps=ps: e.activation(out=sg[:, 0:512], in_=ps[:, :], func=AF.Silu), [bps], [bsg])
                            P.dma("pool", lambda e, sg=sg, t=t: e.dma_start(out=aszd[b, t * 128:(t + 1) * 128, :], in_=sg[:, 0:512]), [bsg], [B_asz[b][t]])
                        elif tmb == 2:
                            P.op("dve", lambda e, sg=sg, ps=ps: e.tensor_copy(out=sg[:, 0:512], in_=ps[:, :]), [bps], [bsg])
                            P.dma("pool", lambda e, sg=sg, t=t: e.dma_start(out=avd[b, t * 128:(t + 1) * 128, :], in_=sg[:, 0:512]), [bsg], [B_av[b][t]])
                        else:
                            if tmb == 0:
                                P.op("act", lambda e, sg=sg, ps=ps: e.activation(out=sg[:, 0:512], in_=ps[:, :], func=AF.Copy), [bps], [bsg])
                            else:
                                P.op("dve", lambda e, sg=sg, ps=ps: e.tensor_copy(out=sg[:, 0:512], in_=ps[:, :]), [bps], [bsg])
                            P.dma("pool", lambda e, sg=sg, t=t, tmb=tmb: e.dma_start(out=rkvz[b, t * 128:(t + 1) * 128, tmb * 512:(tmb + 1) * 512], in_=sg[:, 0:512]), [bsg], [B_rkvz[b][t]])

        return ures, b_ures, bzres, b_bzres

    def phase_conv(l, b, ures, b_ures, bzres, b_bzres):
        cvt = AR.alloc([128, SEQ], F32); b_cvt = Buf()
        cvs = AR.alloc([128, SEQ], BF16); b_cvs = Buf()
        ranges = ([(0, CTX)] if l == 0 else []) + [(CTX, T)]
        for (r0, r1) in ranges:
            n = r1 - r0
            for fb in range(2):
                P.op("dve", lambda e, fb=fb, r0=r0, n=n: e.tensor_scalar(out=cvt[:, 0:n], in0=ures[:, fb, r0:r0 + n], scalar1=cwc[:, fb, 1:2], scalar2=None, op0=ALU.mult), [b_ures, b_cwc], [b_cvt])
                P.op("dve", lambda e, fb=fb, r0=r0, n=n: e.scalar_tensor_tensor(out=cvt[:, 1:n], in0=ures[:, fb, r0:r0 + n - 1], scalar=cwc[:, fb, 0:1], in1=cvt[:, 1:n], op0=ALU.mult, op1=ALU.add), [b_ures, b_cwc, b_cvt], [b_cvt])
                P.op("dve", lambda e, fb=fb, r0=r0, n=n: e.scalar_tensor_tensor(out=cvt[:, 0:n - 1], in0=ures[:, fb, r0 + 1:r0 + n], scalar=cwc[:, fb, 2:3], in1=cvt[:, 0:n - 1], op0=ALU.mult, op1=ALU.add), [b_ures, b_cwc, b_cvt], [b_cvt])
                P.op("pool", lambda e, fb=fb, r0=r0, n=n: e.tensor_tensor(out=cvs[:, 0:n], in0=cvt[:, 0:n], in1=bzres[:, fb, r0:r0 + n], op=ALU.mult), [b_cvt, b_bzres], [b_cvs])
                P.dma("pool", lambda e, fb=fb, r0=r0, n=n: e.dma_start(out=mixT[b, 2 + fb, :, r0:r0 + n], in_=cvs[:, 0:n]), [b_cvs], [B_mixC[b][t] for t in range(r0 // 128, r1 // 128)])

    def bc4(ap4):
        return ap4.unsqueeze(2).to_broadcast([128, 4, 64])

    def v3(ap):
        return ap.rearrange("p (h e) -> p h e", h=4)

    def phase_rwkv(l, b):
        AR.reset()
        Yf = AR.alloc([128, NT, 256], F32); b_Yf = bufs(NT)
        Ksum = AR.alloc([128, NT, 256], BF16); b_Ksum = bufs(NT)
        done = set()
        ro_queue = []
        lps_shared = Rot(psAll, psBufs)

        def mkpools():
            pl = {}
            pl["f256"] = AR.rot(9, [128, 256], F32)
            pl["e12_r"] = AR.rot(2, [128, 512], F32)
            pl["lwc_r"] = AR.rot(2, [128, 2, 128], F32)
            pl["rk_r"] = AR.rot(2, [128, 1024], BF16)
            pl["b256"] = AR.rot(14, [128, 256], BF16)
            pl["FMz_r"] = [AR.rot(2, [128, 8, 128], BF16) for _ in range(2)]
            for par in range(2):
                for (tl, tb_) in zip(pl["FMz_r"][par].tiles, pl["FMz_r"][par].bufs):
                    P.op("pool", lambda e, tl=tl: e.memset(tl, 0.0), [], [tb_])
            pl["XT_r"] = AR.rot(2, [128, 4, 512], BF16)
            pl["Lp_r"] = AR.rot(3, [128, 4, 128], BF16)
            pl["LpT_r"] = AR.rot(3, [128, 4, 128], BF16)
            pl["Z_r"] = AR.rot(2, [128, 4, 128], BF16)
            pl["PhiT_r"] = AR.rot(2, [64, 4, 64], BF16)
            pl["RhT_r"] = AR.rot(2, [64, 4, 128], BF16)
            pl["H_r"] = AR.rot(2, [64, 4, 64], BF16)
            return pl
        kkb, kab, rkb, lngb, lnbb = (v256[:, i, :] for i in range(5))

        def lane(d):
            pl = mkpools()
            f256, e12_r, lwc_r, rk_r, b256, FMz_r = pl["f256"], pl["e12_r"], pl["lwc_r"], pl["rk_r"], pl["b256"], pl["FMz_r"]
            XT_r, Lp_r, LpT_r, Z_r, PhiT_r, RhT_r, H_r = pl["XT_r"], pl["Lp_r"], pl["LpT_r"], pl["Z_r"], pl["PhiT_r"], pl["RhT_r"], pl["H_r"]
            lps = lps_shared

            def pn():
                return lps.next()

            def pnb():
                t_, b_ = lps.next()
                return t_[:, :].bitcast(BF16), b_
            order = (list(range(NT)) if d == 0 else [1, 0] + list(range(NT - 1, 1, -1)))[:RW_NCH]
            H, bH = H_r.next()
            P.op("pool", lambda e, H=H: e.memset(H[:], 0.0), [], [bH])
            i_incl, i_strict, i_rem = (0, 2, 3) if d == 0 else (1, 3, 2)
            for ch in order:
                tk = slice(ch * 128, (ch + 1) * 128)
                rk, brk = rk_r.next()
                P.dma("sp", lambda e, rk=rk, ch=ch: e.dma_start(out=rk[:], in_=rkvz[b, ch * 128:(ch + 1) * 128, :]), [B_rkvz[b][ch]], [brk])
                lwc, b_lwla = lwc_r.next()
                P.dma("sp", lambda e, lwc=lwc, ch=ch: e.dma_start(out=lwc, in_=lwlad[b, :, :, ch * 128:(ch + 1) * 128].rearrange("k p t -> p k t")), [B_lwla[b][ch]], [b_lwla])
                r_, k_, v_, z_ = (rk[:, i * 256:(i + 1) * 256] for i in range(4))
                t1, bt1 = f256.next()
                P.op("dve", lambda e, t1=t1, k_=k_: e.tensor_tensor(out=t1[:], in0=k_, in1=kkb, op=ALU.mult), [brk, b_v256], [bt1])
                t2, bt2 = f256.next()
                P.op("pool", lambda e, t1=t1, t2=t2: e.tensor_tensor(out=t2[:], in0=t1[:], in1=t1[:], op=ALU.mult), [bt1], [bt2])
                sm, bsm = sm_r.next()
                P.op("dve", lambda e, sm=sm, t2=t2: e.reduce_sum(out=sm[:, 0:4], in_=v3(t2[:]), axis=AX.X), [bt2], [bsm])
                P.op("dve", lambda e, sm=sm: e.tensor_scalar_max(out=sm[:, 0:4], in0=sm[:, 0:4], scalar1=1e-24), [bsm], [bsm])
                P.op("act", lambda e, sm=sm: e.activation(out=sm[:, 0:4], in_=sm[:, 0:4], func=AF.Ln), [bsm], [bsm])
                P.op("act", lambda e, sm=sm: e.activation(out=sm[:, 0:4], in_=sm[:, 0:4], func=AF.Exp, scale=-0.5), [bsm], [bsm])
                kk, bkk = f256.next()
                P.op("dve", lambda e, kk=kk, t1=t1, sm=sm: e.tensor_tensor(out=v3(kk[:]), in0=v3(t1[:]), in1=bc4(sm[:, 0:4]), op=ALU.mult), [bt1, bsm], [bkk])
                yield
                psA, bpsA = pn()
                pp = slice(d * 64, (d + 1) * 64)
                P.op("pe", lambda e, psA=psA: e.matmul(psA[:, 0:256], lhsT=lwc[:, 1, :], rhs=aup[:, d, :], start=True, stop=False), [b_lwla, b_aup], [bpsA])
                P.op("pe", lambda e, psA=psA: e.matmul(psA[:, 0:256], lhsT=onesrow[0:1, :], rhs=a0r[0:1, d * 256:(d + 1) * 256], start=False, stop=True), [b_ones, b_a0r], [bpsA])
                P.op("pe", lambda e, psA=psA: e.matmul(psA[:, 256:512], lhsT=lwc[:, 0, :], rhs=wup[:, d, :], start=True, stop=False), [b_lwla, b_wup], [bpsA])
                P.op("pe", lambda e, psA=psA: e.matmul(psA[:, 256:512], lhsT=onesrow[0:1, :], rhs=w0r[0:1, d * 256:(d + 1) * 256], start=False, stop=True), [b_ones, b_w0r], [bpsA])
                asg, basg = e12_r.next()
                P.op("act", lambda e, asg=asg, psA=psA: e.activation(out=asg[:], in_=psA[:, :], func=AF.Sigmoid), [bpsA], [basg])
                a_ = asg[:, 0:256]
                sg_ = asg[:, 256:512]
                psX, bpsX = pn()
                psY, bpsY = pn()
                P.op("pe", lambda e, psX=psX: e.matmul(psX[:, 0:256], lhsT=cm[:, i_incl, :], rhs=sg_, start=True, stop=True), [b_cm, basg], [bpsX])
                P.op("pe", lambda e, psX=psX: e.matmul(psX[:, 256:512], lhsT=cm[:, i_strict, :], rhs=sg_, start=True, stop=True), [b_cm, basg], [bpsX])
                P.op("pe", lambda e, psY=psY: e.matmul(psY[:, 0:256], lhsT=cm[:, i_rem, :], rhs=sg_, start=True, stop=True), [b_cm, basg], [bpsY])
                for h in range(4):
                    P.op("pe", lambda e, psY=psY, h=h: e.matmul(psY[0:64, 256 + h:257 + h], lhsT=asg[:, 256 + h * 64:256 + (h + 1) * 64], rhs=negcol[:, 0:1], start=True, stop=True), [basg, b_negcol], [bpsY])
                e12, be12 = e12_r.next()
                P.op("act", lambda e, e12=e12, psX=psX: e.activation(out=e12[:], in_=psX[:, :], func=AF.Exp), [bpsX], [be12])
                encw, bencw = f256.next()
                P.op("act", lambda e, encw=encw, psX=psX: e.activation(out=encw[:], in_=psX[:, 0:256], func=AF.Exp, scale=-1.0), [bpsX], [bencw])
                erem, berem = f256.next()
                P.op("act", lambda e, erem=erem, psY=psY: e.activation(out=erem[:], in_=psY[:, 0:256], func=AF.Exp), [bpsY], [berem])
                wcs, bwcs = sm_r.next()
                P.op("act", lambda e, wcs=wcs, psY=psY: e.activation(out=wcs[0:64, 0:4], in_=psY[0:64, 256:260], func=AF.Exp), [bpsY], [bwcs])
                yield
                tt, btt = f256.next()
                P.op("dve", lambda e, tt=tt: e.scalar_tensor_tensor(out=tt[:], in0=a_, scalar=-1.0, in1=kab, op0=ALU.add, op1=ALU.mult), [basg, b_v256], [btt])
                kmod, bkmod = f256.next()
                P.op("dve", lambda e, tt=tt, kmod=kmod, k_=k_: e.scalar_tensor_tensor(out=kmod[:], in0=tt[:], scalar=1.0, in1=k_, op0=ALU.add, op1=ALU.mult), [btt, brk], [bkmod])
                bq, bbq = f256.next()
                P.op("pool", lambda e, bq=bq, kk=kk: e.tensor_tensor(out=bq[:], in0=kk[:], in1=a_, op=ALU.mult), [bkk, basg], [bbq])
                At, bAt = b256.next()
                P.op("dve", lambda e, At=At, kk=kk, e12=e12: e.scalar_tensor_tensor(out=At[:], in0=kk[:], scalar=-1.0, in1=e12[:, 256:512], op0=ALU.mult, op1=ALU.mult), [bkk, be12], [bAt])
                Rt, bRt = b256.next()
                P.op("dve", lambda e, Rt=Rt, e12=e12, r_=r_: e.tensor_tensor(out=Rt[:], in0=r_, in1=e12[:, 0:256], op=ALU.mult), [brk, be12], [bRt])
                Bt, bBt = b256.next()
                P.op("pool", lambda e, Bt=Bt, bq=bq, encw=encw: e.tensor_tensor(out=Bt[:], in0=bq[:], in1=encw[:], op=ALU.mult), [bbq, bencw], [bBt])
                Kt, bKt = b256.next()
                P.op("dve", lambda e, Kt=Kt, kmod=kmod, encw=encw: e.tensor_tensor(out=Kt[:], in0=kmod[:], in1=encw[:], op=ALU.mult), [bkmod, bencw], [bKt])
                Bb, bBb = b256.next()
                P.op("pool", lambda e, Bb=Bb, bq=bq, erem=erem: e.tensor_tensor(out=Bb[:], in0=bq[:], in1=erem[:], op=ALU.mult), [bbq, berem], [bBb])
                Kb, bKb = b256.next()
                P.op("pool", lambda e, Kb=Kb, kmod=kmod, erem=erem: e.tensor_tensor(out=Kb[:], in0=kmod[:], in1=erem[:], op=ALU.mult), [bkmod, berem], [bKb])
                is_first = ch not in done
                if is_first:
                    P.op("pool", lambda e, kmod=kmod, ch=ch: e.tensor_copy(out=Ksum[:, ch, :], in_=kmod[:]), [bkmod], [b_Ksum[ch]])
                yield
                pb, bpb = pnb()
                for qi, (src, bsrc) in enumerate(((At, bAt), (Rt, bRt), (Bt, bBt), (Kt, bKt))):
                    for fb in range(2):
                        P.op("pe", lambda e, pb=pb, src=src, fb=fb, qi=qi: e.transpose(out=pb[:, (fb * 4 + qi) * 128:(fb * 4 + qi + 1) * 128], in_=src[:, fb * 128:(fb + 1) * 128], identity=identb[:]), [bsrc, b_identb], [bpb])
                FMz = [FMz_r[0].next(), FMz_r[1].next()]
                P.op("act", lambda e, FMz=FMz, pb=pb: e.activation(out=FMz[0][0][0:64], in_=pb[0:64, :].rearrange("p (c t) -> p c t", c=8), func=AF.Copy), [bpb], [FMz[0][1]])
                P.op("dve", lambda e, FMz=FMz, pb=pb: e.tensor_copy(out=FMz[1][0][64:128], in_=pb[64:128, :].rearrange("p (c t) -> p c t", c=8)), [bpb], [FMz[1][1]])
                yield
                XT, bXT = XT_r.next()
                for h in range(4):
                    fb = h // 2
                    FMq, bFMq = FMz[h % 2]
                    ps, bps = pn()
                    P.op("pe", lambda e, ps=ps, FMq=FMq, fb=fb: e.matmul(ps[:, 0:256], lhsT=FMq[:, fb * 4 + 2, :], rhs=FMq[:, fb * 4:fb * 4 + 2, :], start=True, stop=True), [bFMq], [bps])
                    P.op("pe", lambda e, ps=ps, FMq=FMq, fb=fb: e.matmul(ps[:, 256:512], lhsT=FMq[:, fb * 4 + 3, :], rhs=FMq[:, fb * 4:fb * 4 + 2, :], start=True, stop=True), [bFMq], [bps])
                    P.op("dve", lambda e, ps=ps, XT=XT, h=h: e.tensor_tensor(out=XT[:, h, :], in0=ps[:, :], in1=mask4[:, d, :], op=ALU.mult), [bps, b_mask4], [bXT])
                psL, bpsL = pn()
                for h in range(4):
                    fb = h // 2
                    FMq, bFMq = FMz[h % 2]
                    P.op("pe", lambda e, psL=psL, FMq=FMq, fb=fb, h=h: e.matmul(psL[:, h * 128:(h + 1) * 128], lhsT=FMq[:, fb * 4 + 0, :], rhs=FMq[:, fb * 4 + 2, :], start=True, stop=True), [bFMq], [bpsL])
                Lp, bLp = Lp_r.next()
                P.op("dve", lambda e, Lp=Lp, psL=psL: e.tensor_tensor(out=Lp[:].rearrange("p h t -> p (h t)"), in0=psL[:, :], in1=maskL[:, d, :], op=ALU.mult), [bpsL, b_maskL], [bLp])
                yield
                psP, bpsP = pn()
                for h in range(4):
                    P.op("pe", lambda e, psP=psP, XT=XT, h=h, rk=rk: e.matmul(psP[:, h * 64:(h + 1) * 64], lhsT=XT[:, h, 256:384], rhs=rk[:, 512 + h * 64:512 + (h + 1) * 64], start=True, stop=True), [bXT, brk], [bpsP])
                Z, bZ = Z_r.next()
                P.op("pool", lambda e, Z=Z, At=At: e.tensor_copy(out=Z[:, :, 0:64], in_=v3(At[:])), [bAt], [bZ])
                P.op("act", lambda e, Z=Z, psP=psP: e.activation(out=Z[:, :, 64:128], in_=v3(psP[:, 0:256]), func=AF.Copy), [bpsP], [bZ])
                yield
                LpT_first = True
                LpT, bLpT = None, None
                for j in range(7):
                    psZ, bpsZ = pn()
                    for h in range(4):
                        lt = XT[:, h, 0:128] if LpT_first else LpT[:, h, :]
                        blt = bXT if LpT_first else bLpT
                        P.op("pe", lambda e, psZ=psZ, lt=lt, Z=Z, h=h: e.matmul(psZ[:, h * 128:(h + 1) * 128], lhsT=lt, rhs=Z[:, h, :], start=True, stop=True), [blt, bZ], [bpsZ])
                    P.op("dve", lambda e, Z=Z, psZ=psZ: e.tensor_tensor(out=Z[:].rearrange("p h t -> p (h t)"), in0=psZ[:, :], in1=Z[:].rearrange("p h t -> p (h t)"), op=ALU.add), [bpsZ, bZ], [bZ])
                    if j < 6:
                        ps1, bps1 = pn()
                        ps2, bps2 = pn()
                        for h in range(4):
                            lt = XT[:, h, 0:128] if LpT_first else LpT[:, h, :]
                            blt = bXT if LpT_first else bLpT
                            if j < 5:
                                P.op("pe", lambda e, ps1=ps1, lt=lt, Lp=Lp, h=h: e.matmul(ps1[:, h * 128:(h + 1) * 128], lhsT=lt, rhs=Lp[:, h, :], start=True, stop=True), [blt, bLp], [bps1])
                            P.op("pe", lambda e, ps2=ps2, lt=lt, Lp=Lp, h=h: e.matmul(ps2[:, h * 128:(h + 1) * 128], lhsT=Lp[:, h, :], rhs=lt, start=True, stop=True), [blt, bLp], [bps2])
                        nLp, bnLp = Lp_r.next()
                        nLpT, bnLpT = LpT_r.next()
                        if j < 5:
                            P.op("act", lambda e, nLp=nLp, ps1=ps1: e.activation(out=nLp[:].rearrange("p h t -> p (h t)"), in_=ps1[:, :], func=AF.Copy), [bps1], [bnLp])
                        P.op("dve", lambda e, nLpT=nLpT, ps2=ps2: e.tensor_copy(out=nLpT[:].rearrange("p h t -> p (h t)"), in_=ps2[:, :]), [bps2], [bnLpT])
                        Lp, bLp, LpT, bLpT = nLp, bnLp, nLpT, bnLpT
                        LpT_first = False
                    yield
                yield
                psFh, bpsFh = pn()
                psR, bpsR = pn()
                for h in range(4):
                    P.op("pe", lambda e, psFh=psFh, Z=Z, Bb=Bb, h=h: e.matmul(psFh[0:64, h * 64:(h + 1) * 64], lhsT=Z[:, h, 0:64], rhs=Bb[:, h * 64:(h + 1) * 64], start=True, stop=True), [bZ, bBb], [bpsFh])
                    P.op("pe", lambda e, psR=psR, Z=Z, XT=XT, h=h: e.matmul(psR[0:64, h * 128:(h + 1) * 128], lhsT=Z[:, h, 0:64], rhs=XT[:, h, 128:256], start=True, stop=False), [bZ, bXT], [bpsR])
                    P.op("pe", lambda e, psR=psR, Rt=Rt, h=h: e.matmul(psR[0:64, h * 128:(h + 1) * 128], lhsT=Rt[:, h * 64:(h + 1) * 64], rhs=identb[:], start=False, stop=True), [bRt, b_identb], [bpsR])
                PhiT, bPhiT = PhiT_r.next()
                for h in range(4):
                    P.op("dve", lambda e, PhiT=PhiT, psFh=psFh, wcs=wcs, h=h: e.scalar_tensor_tensor(out=PhiT[:, h, :], in0=identf[0:64, 0:64], scalar=wcs[0:64, h:h + 1], in1=psFh[0:64, h * 64:(h + 1) * 64], op0=ALU.mult, op1=ALU.add), [bpsFh, bwcs, b_identf], [bPhiT])
                RhT, bRhT = RhT_r.next()
                P.op("act", lambda e, RhT=RhT, psR=psR: e.activation(out=RhT[:].rearrange("p h t -> p (h t)"), in_=psR[0:64, :], func=AF.Copy), [bpsR], [bRhT])
                yield
                psYo, bpsYo = pn()
                psH, bpsH = pn()
                for h in range(4):
                    vh = rk[:, 512 + h * 64:512 + (h + 1) * 64]
                    P.op("pe", lambda e, psYo=psYo, XT=XT, Z=Z, h=h: e.matmul(psYo[:, h * 64:(h + 1) * 64], lhsT=XT[:, h, 128:256], rhs=Z[:, h, 64:128], start=True, stop=False), [bXT, bZ], [bpsYo])
                    P.op("pe", lambda e, psYo=psYo, XT=XT, vh=vh, h=h: e.matmul(psYo[:, h * 64:(h + 1) * 64], lhsT=XT[:, h, 384:512], rhs=vh, start=False, stop=False), [bXT, brk], [bpsYo])
                    P.op("pe", lambda e, psYo=psYo, RhT=RhT, H=H, h=h: e.matmul(psYo[:, h * 64:(h + 1) * 64], lhsT=RhT[:, h, :], rhs=H[:, h, :], start=False, stop=True), [bRhT, bH], [bpsYo])
                    P.op("pe", lambda e, psH=psH, PhiT=PhiT, H=H, h=h: e.matmul(psH[0:64, h * 64:(h + 1) * 64], lhsT=PhiT[:, h, :], rhs=H[:, h, :], start=True, stop=False), [bPhiT, bH], [bpsH])
                    P.op("pe", lambda e, psH=psH, Bb=Bb, Z=Z, h=h: e.matmul(psH[0:64, h * 64:(h + 1) * 64], lhsT=Bb[:, h * 64:(h + 1) * 64], rhs=Z[:, h, 64:128], start=False, stop=False), [bBb, bZ], [bpsH])
                    P.op("pe", lambda e, psH=psH, Kb=Kb, vh=vh, h=h: e.matmul(psH[0:64, h * 64:(h + 1) * 64], lhsT=Kb[:, h * 64:(h + 1) * 64], rhs=vh, start=False, stop=True), [bKb, brk], [bpsH])
                nH, bnH = H_r.next()
                P.op("act", lambda e, nH=nH, psH=psH: e.activation(out=nH[:].rearrange("p h t -> p (h t)"), in_=psH[0:64, 0:256], func=AF.Copy), [bpsH], [bnH])
                H, bH = nH, bnH
                if is_first:
                    P.op("dve", lambda e, psYo=psYo, ch=ch: e.tensor_copy(out=Yf[:, ch, :], in_=psYo[:, 0:256]), [bpsYo], [b_Yf[ch]])
                    done.add(ch)
                    yield
                    continue
                if l == NL - 1 and ch < 2:
                    yield
                    continue
                yield
                P.op("dve", lambda e, psYo=psYo, ch=ch: e.tensor_tensor(out=Yf[:, ch, :], in0=psYo[:, 0:256], in1=Yf[:, ch, :], op=ALU.add), [bpsYo, b_Yf[ch]], [b_Yf[ch]])
                P.op("pool", lambda e, kmod=kmod, ch=ch: e.tensor_tensor(out=Ksum[:, ch, :], in0=kmod[:], in1=Ksum[:, ch, :], op=ALU.add), [bkmod, b_Ksum[ch]], [b_Ksum[ch]])
                ro_queue.append(ch)
                yield

        def ro_lane():
            f256 = AR.rot(10, [128, 256], F32)
            rk_r = AR.rot(2, [128, 1024], BF16)
            b256 = AR.rot(2, [128, 256], BF16)
            ro_r = AR.rot(2, [128, 2, 128], BF16)
            lps = lps_shared

            def pnb():
                t_, b_ = lps.next()
                return t_[:, :].bitcast(BF16), b_
            ndone = 0
            total = len([c for c in range(NT) if not (l == NL - 1 and c < 2)]) if RW_ND == 2 and RW_NCH >= NT else 0
            while ndone < total:
                if not ro_queue:
                    yield
                    continue
                ch = ro_queue.pop(0)
                ndone += 1
                rk, brk = rk_r.next()
                P.dma("sp", lambda e, rk=rk, ch=ch: e.dma_start(out=rk[:], in_=rkvz[b, ch * 128:(ch + 1) * 128, :]), [B_rkvz[b][ch]], [brk])
                r_, k_, v_, z_ = (rk[:, i * 256:(i + 1) * 256] for i in range(4))
                y = Yf[:, ch, :]
                by = b_Yf[ch]
                s1, bs1 = sm_r.next()
                P.op("dve", lambda e, s1=s1, y=y: e.reduce_sum(out=s1[:, 0:4], in_=v3(y), axis=AX.X), [by], [bs1])
                P.op("dve", lambda e, s1=s1: e.tensor_scalar_mul(out=s1[:, 0:4], in0=s1[:, 0:4], scalar1=-1.0 / 64.0), [bs1], [bs1])
                yc, byc = f256.next()
                P.op("dve", lambda e, yc=yc, y=y, s1=s1: e.tensor_tensor(out=v3(yc[:]), in0=v3(y), in1=bc4(s1[:, 0:4]), op=ALU.add), [by, bs1], [byc])
                sq, bsq = f256.next()
                P.op("pool", lambda e, sq=sq, yc=yc: e.tensor_tensor(out=sq[:], in0=yc[:], in1=yc[:], op=ALU.mult), [byc], [bsq])
                P.op("dve", lambda e, s1=s1, sq=sq: e.reduce_sum(out=s1[:, 4:8], in_=v3(sq[:]), axis=AX.X), [bsq], [bs1])
                P.op("act", lambda e, s1=s1: e.activation(out=s1[:, 4:8], in_=s1[:, 4:8], func=AF.Ln, bias=epsc[:, 1:2], scale=1.0 / 64.0), [bs1, b_eps], [bs1])
                P.op("act", lambda e, s1=s1: e.activation(out=s1[:, 4:8], in_=s1[:, 4:8], func=AF.Exp, scale=-0.5), [bs1], [bs1])
                yield
                yn, byn = f256.next()
                P.op("dve", lambda e, yn=yn, yc=yc, s1=s1: e.tensor_tensor(out=v3(yn[:]), in0=v3(yc[:]), in1=bc4(s1[:, 4:8]), op=ALU.mult), [byc, bs1], [byn])
                P.op("pool", lambda e, yn=yn: e.tensor_tensor(out=yn[:], in0=yn[:], in1=lngb, op=ALU.mult), [byn, b_v256], [byn])
                P.op("pool", lambda e, yn=yn: e.tensor_tensor(out=yn[:], in0=yn[:], in1=lnbb, op=ALU.add), [byn, b_v256], [byn])
                ks, bks = f256.next()
                P.op("pool", lambda e, ks=ks, r_=r_, ch=ch: e.tensor_tensor(out=ks[:], in0=Ksum[:, ch, :], in1=r_, op=ALU.mult), [b_Ksum[ch], brk], [bks])
                P.op("pool", lambda e, ks=ks: e.tensor_tensor(out=ks[:], in0=ks[:], in1=rkb, op=ALU.mult), [bks, b_v256], [bks])
                s2, bs2 = sm_r.next()
                P.op("dve", lambda e, s2=s2, ks=ks: e.reduce_sum(out=s2[:, 0:4], in_=v3(ks[:]), axis=AX.X), [bks], [bs2])
                bv, bbv = f256.next()
                P.op("dve", lambda e, bv=bv, v_=v_, s2=s2: e.tensor_tensor(out=v3(bv[:]), in0=v3(v_), in1=bc4(s2[:, 0:4]), op=ALU.mult), [brk, bs2], [bbv])
                P.op("pool", lambda e, yn=yn, bv=bv: e.tensor_tensor(out=yn[:], in0=yn[:], in1=bv[:], op=ALU.add), [byn, bbv], [byn])
                yield
                sz, bsz = f256.next()
                P.op("act", lambda e, sz=sz, z_=z_: e.activation(out=sz[:], in_=z_, func=AF.Silu), [brk], [bsz])
                yb, byb = b256.next()
                P.op("dve", lambda e, yb=yb, yn=yn, sz=sz: e.tensor_tensor(out=yb[:], in0=yn[:], in1=sz[:], op=ALU.mult), [byn, bsz], [byb])
                pb2, bpb2 = pnb()
                for fb in range(2):
                    P.op("pe", lambda e, pb2=pb2, yb=yb, fb=fb: e.transpose(out=pb2[:, fb * 128:(fb + 1) * 128], in_=yb[:, fb * 128:(fb + 1) * 128], identity=identb[:]), [byb, b_identb], [bpb2])
                ro, bro = ro_r.next()
                P.op("act", lambda e, ro=ro, pb2=pb2: e.activation(out=ro[:].rearrange("p a t -> p (a t)"), in_=pb2[:, 0:256], func=AF.Copy), [bpb2], [bro])
                P.dma("pool", lambda e, ro=ro, ch=ch: e.dma_start(out=mixT[b, 0:2, :, ch * 128:(ch + 1) * 128].rearrange("k p t -> p k t"), in_=ro[:]), [bro], [B_mixR[b][ch]])
                yield

        gens = [lane(d) for d in range(RW_ND)] + [ro_lane()]
        while gens:
            for g in list(gens):
                try:
                    next(g)
                except StopIteration:
                    gens.remove(g)

    def phase_attn(l, b):
        AR.reset()
        kTt = AR.alloc([128, 4, T], BF16); b_kTh = bufs(4)
        Vt = AR.alloc([128, NT, 520], BF16); b_Vk = bufs(NT)
        qz = [AR.alloc([128, 4, 512], BF16) for _ in range(2)]
        bqg = Buf()
        for j in range(2):
            P.op("pool", lambda e, j=j: e.memset(qz[j], 0.0), [], [bqg])
        E_r = AR.rot(3, [128, 512], BF16)
        szq_r = AR.rot(1, [128, 4, 512], BF16)
        oall_r = AR.rot(1, [128, 4, 512], BF16)
        ast_r = AR.rot(2, [128, 4, 128], BF16)
        junk_r = AR.rot(1, [128, 128], BF16)
        P.op("pool", lambda e: e.memset(Vt.rearrange("p k (h e) -> p k h e", e=130)[:, :, :, 128:130], 1.0), [], list(b_Vk))
        scoreR = Rot(psAll[4:8], psBufs[4:8])
        oj_r = [AR.rot(2, [128, 4, 132], F32) for _ in range(2)]
        w_r = AR.rot(4, [128, 4, 128], F32)
        for hh in range(4):
            P.dma("sp", lambda e, hh=hh: e.dma_start(out=kTt[:, hh, :], in_=qkT[b, 1, hh, :, :]), list(B_k[b]), [b_kTh[hh]])
        for kt in range(NT):
            P.dma("sp", lambda e, kt=kt: e.dma_start(out=Vt[:, kt, :].rearrange("p (h e) -> p h e", e=130)[:, :, 0:128], in_=avd[b, kt * 128:(kt + 1) * 128, :].rearrange("p (h e) -> p h e", e=128)), [B_av[b][kt]], [b_Vk[kt]])
        qgroups = [([2, 3, 4, 5], list(range(NT))), ([6, 7, 8, 9], list(range(NT))), ([10, 11, 12, 13], list(range(NT))), ([14, 15, 16, 17], list(range(NT)))]
        if l < NL - 1:
            qgroups = [([0, 1], [0, 1])] + qgroups
        for (qt, kts) in qgroups[:AT_NG]:
            nq = len(qt)
            ntok = nq * 128
            tok0 = qt[0] * 128
            for j in range(2):
                P.dma("sp", lambda e, j=j, tok0=tok0, ntok=ntok: e.dma_start(out=qz[j][j * 64:(j + 1) * 64, :, 0:ntok], in_=qkT[b, 0, :, j * 64:(j + 1) * 64, tok0:tok0 + ntok].rearrange("h p t -> p h t")), [B_q[b][t] for t in qt], [bqg])
            szq, bszq = szq_r.next()
            for qi, t in enumerate(qt):
                P.dma("sp", lambda e, szq=szq, qi=qi, t=t: e.dma_start(out=szq[:, qi, :], in_=aszd[b, t * 128:(t + 1) * 128, :]), [B_asz[b][t]], [bszq])
            oall, boall = oall_r.next()
            from collections import deque
            items = [(h, j, ki, kt) for h in range(4) for j in range(2) for ki, kt in enumerate(kts)]
            acc = [(psAll[i_], psBufs[i_]) for i_ in range(nq)]
            pending = deque()
            ojs_h = {}

            def do_pv(item, E, bE):
                h, j, ki, kt = item
                first, last = (ki == 0), (ki == len(kts) - 1)
                for qi in range(nq):
                    ab, bab = acc[qi]
                    P.op("pe", lambda e, ab=ab, E=E, qi=qi: e.matmul(ab[:, 0:129], lhsT=E[:, qi * 128:(qi + 1) * 128], rhs=Vt[:, kt, h * 130:h * 130 + 129], start=first, stop=last), [bE, b_Vk[kt]], [bab])
                if not last:
                    return
                ojt, bojt = oj_r[j].next()
                ojs_h[j] = (ojt, bojt)
                for qi in range(nq):
                    if (qi + j) % 2 == 0:
                        P.op("dve", lambda e, qi=qi: e.tensor_copy(out=ojt[:, qi, 0:129], in_=acc[qi][0][:, 0:129]), [acc[qi][1]], [bojt])
                    else:
                        P.op("act", lambda e, qi=qi: e.activation(out=ojt[:, qi, 0:129], in_=acc[qi][0][:, 0:129], func=AF.Copy), [acc[qi][1]], [bojt])
                if j == 1:
                    combine(h)

            def combine(h):
                ojs = [ojs_h[0], ojs_h[1]]
                (oj0, boj0), (oj1, boj1) = ojs
                sm, bsm = sm_r.next()
                P.op("dve", lambda e, sm=sm, oj0=oj0: e.reciprocal(out=sm[:, 0:nq].unsqueeze(2), in_=oj0[:, 0:nq, 128:129]), [boj0], [bsm])
                P.op("dve", lambda e, sm=sm, oj1=oj1: e.reciprocal(out=sm[:, 4:4 + nq].unsqueeze(2), in_=oj1[:, 0:nq, 128:129]), [boj1], [bsm])
                P.op("dve", lambda e, sm=sm: e.tensor_scalar(out=sm[:, 4:4 + nq], in0=sm[:, 4:4 + nq], scalar1=neglam[:, 0:1], scalar2=None, op0=ALU.mult), [bsm, b_neglam], [bsm])
                t1, bt1 = w_r.next()
                t0, bt0 = w_r.next()
                P.op("dve", lambda e, t1=t1, oj1=oj1, sm=sm: e.tensor_tensor(out=t1[:, 0:nq, :], in0=oj1[:, 0:nq, 0:128], in1=sm[:, 4:4 + nq].unsqueeze(2).to_broadcast([128, nq, 128]), op=ALU.mult), [boj1, bsm], [bt1])
                P.op("dve", lambda e, t0=t0, oj0=oj0, sm=sm: e.tensor_tensor(out=t0[:, 0:nq, :], in0=oj0[:, 0:nq, 0:128], in1=sm[:, 0:nq].unsqueeze(2).to_broadcast([128, nq, 128]), op=ALU.mult), [boj0, bsm], [bt0])
                P.op("pool", lambda e, t0=t0, t1=t1: e.tensor_tensor(out=t0[:, 0:nq, :], in0=t0[:, 0:nq, :], in1=t1[:, 0:nq, :], op=ALU.add), [bt0, bt1], [bt0])
                P.op("pool", lambda e, t0=t0, t1=t1: e.tensor_tensor(out=t1[:, 0:nq, :], in0=t0[:, 0:nq, :], in1=t0[:, 0:nq, :], op=ALU.mult), [bt0], [bt1])
                sm2, bsm2 = sm_r.next()
                P.op("dve", lambda e, sm2=sm2, t1=t1: e.reduce_sum(out=sm2[:, 0:nq], in_=t1[:, 0:nq, :], axis=AX.X), [bt1], [bsm2])
                P.op("act", lambda e, sm2=sm2: e.activation(out=sm2[:, 0:nq], in_=sm2[:, 0:nq], func=AF.Ln, bias=epsc[:, 0:1], scale=1.0 / 128.0), [bsm2, b_eps], [bsm2])
                P.op("act", lambda e, sm2=sm2: e.activation(out=sm2[:, 0:nq], in_=sm2[:, 0:nq], func=AF.Exp, scale=-0.5), [bsm2], [bsm2])
                P.op("dve", lambda e, t0=t0, sm2=sm2: e.tensor_tensor(out=t0[:, 0:nq, :], in0=t0[:, 0:nq, :], in1=sm2[:, 0:nq].unsqueeze(2).to_broadcast([128, nq, 128]), op=ALU.mult), [bt0, bsm2], [bt0])
                P.op("dve", lambda e, t0=t0: e.tensor_tensor(out=t0[:, 0:nq, :], in0=t0[:, 0:nq, :], in1=gsub[:].unsqueeze(1).to_broadcast([128, nq, 128]), op=ALU.mult), [bt0, b_gsub], [bt0])
                P.op("pool", lambda e, t0=t0, oall=oall, szq=szq: e.tensor_tensor(out=oall[:, 0:nq, h * 128:(h + 1) * 128], in0=t0[:, 0:nq, :], in1=szq[:, 0:nq, h * 128:(h + 1) * 128], op=ALU.mult), [bt0, bszq], [boall])

            for item in items:
                h, j, ki, kt = item
                ps, bps = scoreR.next()
                P.op("pe", lambda e, ps=ps: e.matmul(ps[:, 0:ntok], lhsT=kTt[:, h, kt * 128:(kt + 1) * 128], rhs=qz[j][:, h, 0:ntok], start=True, stop=True), [b_kTh[h], bqg], [bps])
                E, bE = E_r.next()
                P.op("act", lambda e, E=E, ps=ps: e.activation(out=E[:, 0:ntok], in_=ps[:, 0:ntok], func=AF.Exp, scale=0.125), [bps], [bE])
                pending.append((item, E, bE))
                if len(pending) >= 3:
                    do_pv(*pending.popleft())
            while pending:
                do_pv(*pending.popleft())
            for qi, t in enumerate(qt if AT_CUT >= 4 else []):
                pb, bpb = psB.next()
                for h in range(4):
                    P.op("pe", lambda e, pb=pb, oall=oall, qi=qi, h=h: e.transpose(out=pb[:, h * 128:(h + 1) * 128], in_=oall[:, qi, h * 128:(h + 1) * 128], identity=identb[:]), [boall, b_identb], [bpb])
                ast, bast = ast_r.next()
                P.op("act", lambda e, ast=ast, pb=pb: e.activation(out=ast[:].rearrange("p h t -> p (h t)"), in_=pb[:, 0:512], func=AF.Copy), [bpb], [bast])
                P.dma("pool", lambda e, ast=ast, t=t: e.dma_start(out=mixT[b, 4:8, :, t * 128:(t + 1) * 128].rearrange("k p t -> p k t"), in_=ast[:]), [bast], [B_mixA[b][t]])

    def phase_out(l, b):
        AR.reset()
        xt_r = AR.rot(2, [128, D], F32)
        xo_r = AR.rot(2, [128, D], F32)
        t512 = AR.rot(4, [128, 512], F32)
        mx_r = AR.rot(2, [128, 8, 128], BF16)
        junk_r = AR.rot(2, [128, 512], BF16)
        bcC = [None, None, AR.alloc([128, D], F32)]; b_bcC = bufs(3)
        bcB = [None, None, AR.alloc([128, D], F32)]; b_bcB = bufs(3)
        bcast_rows(4, bcC, b_bcC, which=(2,))
        bcast_rows(b, bcB, b_bcB, which=(2,))
        woutb = AR.alloc([128, 8, D], BF16); b_woutb = Buf()
        P.dma("sp", lambda e: e.dma_start(out=woutb, in_=woutbf[l, :, :, :]), [B_woutbf[l]], [b_woutb])
        xsrc, Bx = (xall, B_x[0]) if l == 0 else (x1, B_x[1])
        tiles = list(range(NT)) if l < NL - 1 else list(range(2, NT))
        for t in tiles:
            G = (bcC, b_bcC) if t < 2 else (bcB, b_bcB)
            mx, bmx = mx_r.next()
            P.dma("sp", lambda e, mx=mx, t=t: e.dma_start(out=mx[:], in_=mixT[b, :, :, t * 128:(t + 1) * 128].rearrange("k p t -> p k t")), [B_mixR[b][t], B_mixC[b][t], B_mixA[b][t]], [bmx])
            xt, bxt = xt_r.next()
            P.dma("sp", lambda e, xt=xt, t=t: e.dma_start(out=xt[:], in_=xsrc[b, t * 128:(t + 1) * 128, :]), [Bx[b][t]], [bxt])
            pss = []
            sm, bsm = sm_r.next()
            for hf in range(2):
                ps, bps = psF.next()
                pss.append((ps, bps))
                for k in range(8):
                    P.op("pe", lambda e, ps=ps, mx=mx, k=k, hf=hf: e.matmul(ps[:, :], lhsT=mx[:, k, :], rhs=woutb[:, k, hf * 512:(hf + 1) * 512], start=(k == 0), stop=(k == 7)), [bmx, b_woutb], [bps])
                jk, bjk = junk_r.next()
                P.op("act", lambda e, jk=jk, ps=ps, sm=sm, hf=hf: e.activation(out=jk[:, 0:512], in_=ps[:, :], func=AF.Square, scale=1.0 / 32.0, accum_out=sm[:, hf:hf + 1]), [bps], [bjk, bsm])
            P.op("dve", lambda e, sm=sm: e.tensor_tensor(out=sm[:, 2:3], in0=sm[:, 0:1], in1=sm[:, 1:2], op=ALU.add), [bsm], [bsm])
            rstd_from_ms(sm[:, 2:3], bsm, 0)
            xo, bxo = xo_r.next()
            for hf in range(2):
                ps, bps = pss[hf]
                tq, btq = t512.next()
                P.op("dve", lambda e, tq=tq, ps=ps, sm=sm, hf=hf, G=G: e.scalar_tensor_tensor(out=tq[:], in0=ps[:, :], scalar=sm[:, 2:3], in1=G[0][2][:, hf * 512:(hf + 1) * 512], op0=ALU.mult, op1=ALU.mult), [bps, bsm, G[1][2]], [btq])
                P.op("pool", lambda e, xo=xo, tq=tq, xt=xt, hf=hf: e.tensor_tensor(out=xo[:, hf * 512:(hf + 1) * 512], in0=tq[:], in1=xt[:, hf * 512:(hf + 1) * 512], op=ALU.add), [btq, bxt], [bxo])
            if l < NL - 1:
                P.dma("pool", lambda e, xo=xo, t=t: e.dma_start(out=x1[b, t * 128:(t + 1) * 128, :], in_=xo[:]), [bxo], [B_x[1][b][t]])
                if dbg and b == 0:
                    P.dma("pool", lambda e, xo=xo, t=t: e.dma_start(out=dbgd["d_x1"][t * 128:(t + 1) * 128, :], in_=xo[:]), [bxo], [B_x[2][b][t]])
            else:
                P.dma("pool", lambda e, xo=xo, t=t: e.dma_start(out=outd[b, (t - 2) * 128:(t - 1) * 128, :], in_=xo[:]), [bxo], [B_x[2][b][t]])

    for l in range(NL):
        if upto >= 1:
            layer_setup(l)
        for b in range(NB):
            if upto >= 2:
                cv = phase_a(l, b)
            if upto >= 3:
                phase_conv(l, b, *cv)
            if upto >= 4:
                phase_rwkv(l, b)
            if upto >= 5:
                phase_attn(l, b)
            if upto >= 6:
                phase_out(l, b)
            if dbg and l == 0 and b == 0:
                bd = Buf()
                P.dma("sp", lambda e: e.dma_start(out=dbgd["d_mix"][:, :, :], in_=mixT[0, :, :, :]), B_mixR[0] + B_mixC[0] + B_mixA[0], [bd])
                P.dma("sp", lambda e: e.dma_start(out=dbgd["d_rkvz"][:, :], in_=rkvz[0, :, :]), B_rkvz[0], [bd])
                P.dma("sp", lambda e: e.dma_start(out=dbgd["d_q"][:, :, :, :], in_=qkT[0, :, :, :, :]), B_q[0] + B_k[0], [bd])
    stats = P.finish()
    return nc, stats


_CACHE = {}


def _prep(inputs, core, NB=NBF):
    cp = _CACHE.setdefault("colperm", _colperm())
    cst = _CACHE.setdefault("consts", _consts())
    f = lambda a: np.ascontiguousarray(np.asarray(a, dtype=np.float32))
    bs = slice(core * NB, (core + 1) * NB)
    x = np.asarray(inputs["x"])[bs]
    ctx = np.asarray(inputs["ctx"])[bs]
    m = {}
    m["xall"] = f(np.concatenate([ctx, x], axis=1))
    c5 = np.concatenate([np.asarray(inputs["c"])[bs], np.asarray(inputs["c_ctx"])[None, :]], axis=0)
    if c5.shape[0] < 5:
        c5 = np.concatenate([c5, np.zeros((5 - c5.shape[0], D), np.float32)], 0)
        c5[4] = np.asarray(inputs["c_ctx"])
    m["cc"] = f(c5)
    sh = _CACHE.get("shared")
    if sh is None:
        sh = {}
        sh["wext"] = f(np.asarray(inputs["w_in"])[:, :, cp])
        sh["wout"] = f(inputs["w_out"])
        sh["modw"] = f(inputs["mod_w"])
        sh["modb"] = f(np.asarray(inputs["mod_b"])[:, None, :])
        sh["preg"] = f(np.asarray(inputs["norm_pre_g"])[:, None, :])
        sh["postg"] = f(np.asarray(inputs["norm_post_g"])[:, None, :])
        sh["w0"] = f(np.asarray(inputs["rwkv_w0"]).reshape(L_FULL, 1, 512))
        sh["a0"] = f(np.asarray(inputs["rwkv_a0"]).reshape(L_FULL, 1, 512))
        sh["wup"] = f(np.asarray(inputs["rwkv_w_up"]).reshape(L_FULL, 128, 256))
        sh["aup"] = f(np.asarray(inputs["rwkv_a_up"]).reshape(L_FULL, 128, 256))
        sh["vec256"] = f(np.stack([np.asarray(inputs["rwkv_k_k"]), np.asarray(inputs["rwkv_k_a"]),
                                   np.asarray(inputs["rwkv_r_k"]).reshape(L_FULL, 256), np.asarray(inputs["rwkv_ln_g"]),
                                   np.asarray(inputs["rwkv_ln_b"]), np.asarray(inputs["diff_lambda"]).reshape(L_FULL, 256)], axis=1))
        sh["convw"] = f(inputs["conv_w"])
        sh["subg"] = f(np.asarray(inputs["diff_subln_g"])[:, None, :])
        for k, v in cst.items():
            sh[k] = f(v)
        _CACHE["shared"] = sh
    m.update(sh)
    return m


def kernel(**inputs):
    _CACHE.pop("shared", None)
    nc, stats = build()
    in_maps = [_prep(inputs, core) for core in range(8)]
    res = run_bass_kernel_spmd(nc, in_maps, core_ids=list(range(8)))
    out = np.concatenate([np.asarray(r["out"]) for r in res.results], axis=0)
    return out.astype(np.float32)
```
